# Optimizing a Trainium2 kernel written in Bass

```python
import jax, jax.numpy as jnp
from jax import lax
import numpy as np

D_MODEL = 2048
BATCH = 8
SEQ = 2048
DEPTH = 4

GRID_W = 64
CTX_LEN = 256
HEAD_DIM = 128
A_HEADS = 8
A_KV_HEADS = 2
WINDOW = 128
A_BLOCK = 128
B_HEADS = 8
B_KV_HEADS = 2
Q_BLOCK = 128
C_HEADS = 8
C_DK = 128
C_DV = 128
C_CHUNK = 64
A_WIDTH = A_HEADS * HEAD_DIM
B_WIDTH = B_HEADS * HEAD_DIM
C_WIDTH = C_HEADS * C_DV
N_BRANCH = 3
ROPE_THETA = 10000.0
NORM_EPS = 1e-6
IN_SPLITS = (A_WIDTH, A_KV_HEADS * HEAD_DIM, A_KV_HEADS * HEAD_DIM, A_WIDTH,
             B_WIDTH, B_KV_HEADS * HEAD_DIM, B_KV_HEADS * HEAD_DIM, B_WIDTH,
             C_HEADS * C_DK, C_HEADS * C_DK, C_HEADS * C_DK, C_WIDTH, C_WIDTH,
             N_BRANCH * D_MODEL)
W_IN_TOTAL = sum(IN_SPLITS)

kernel_name = "hybrid_gated_branch_dit_block"

F32 = jnp.float32


def rms_norm(x, g):
    xf = x.astype(F32)
    y = xf * lax.rsqrt(jnp.mean(xf * xf, axis=-1, keepdims=True) + NORM_EPS)
    return (y * g.astype(F32)).astype(x.dtype)


def axial_rope_tables(n):
    rows = n // GRID_W
    row = jnp.repeat(jnp.arange(rows, dtype=F32), GRID_W)
    col = jnp.tile(jnp.arange(GRID_W, dtype=F32), rows)
    n_freq = HEAD_DIM // 4
    inv_freq = ROPE_THETA ** (-jnp.arange(n_freq, dtype=F32) / n_freq)
    ang = jnp.concatenate([row[:, None] * inv_freq, col[:, None] * inv_freq], axis=-1)
    ang = jnp.concatenate([ang, ang], axis=-1)
    return jnp.cos(ang), jnp.sin(ang)


def apply_rope(x, cos, sin):
    xf = x.astype(F32)
    x1, x2 = jnp.split(xf, 2, axis=-1)
    rot = jnp.concatenate([-x2, x1], axis=-1)
    return (xf * cos[:, None] + rot * sin[:, None]).astype(x.dtype)


def sink_softmax(s, sink):
    col = jnp.broadcast_to(sink[None, :, :, None, None], s.shape[:-1] + (1,))
    return jax.nn.softmax(jnp.concatenate([s, col], axis=-1), axis=-1)[..., :-1]


def window_sink_attention(q_c, k_c, v_c, q_l, k_l, v_l, sink, with_ctx):
    b, n = q_l.shape[:2]
    m = q_c.shape[1]
    g = A_HEADS // A_KV_HEADS
    scale = HEAD_DIM ** -0.5
    sink = sink.astype(F32).reshape(A_KV_HEADS, g)
    nb = n // A_BLOCK
    qb = jnp.moveaxis(q_l.reshape(b, nb, A_BLOCK, A_KV_HEADS, g, HEAD_DIM), 1, 0)

    def band(t):
        tb = jnp.pad(t, ((0, 0), (A_BLOCK, A_BLOCK), (0, 0), (0, 0)))
        tb = tb.reshape(b, nb + 2, A_BLOCK, A_KV_HEADS, HEAD_DIM)
        return jnp.moveaxis(jnp.concatenate([tb[:, :-2], tb[:, 1:-1], tb[:, 2:]], axis=2), 1, 0)

    kw, vw = band(k_l), band(v_l)
    blk = jnp.arange(nb)[:, None, None]
    qpos = blk * A_BLOCK + jnp.arange(A_BLOCK)[None, :, None]
    kpos = (blk - 1) * A_BLOCK + jnp.arange(3 * A_BLOCK)[None, None, :]
    valid = (jnp.abs(qpos - kpos) <= WINDOW) & (kpos >= 0) & (kpos < n)

    def block(args):
        qblk, kblk, vblk, mask = args
        s_win = jnp.einsum('bqhgd,bkhd->bhgqk', qblk, kblk).astype(F32) * scale
        s_win = jnp.where(mask, s_win, -jnp.inf)
        s_ctx = jnp.einsum('bqhgd,bkhd->bhgqk', qblk, k_c).astype(F32) * scale
        p = sink_softmax(jnp.concatenate([s_win, s_ctx], axis=-1), sink).astype(vblk.dtype)
        return (jnp.einsum('bhgqk,bkhd->bqhgd', p[..., :3 * A_BLOCK], vblk)
                + jnp.einsum('bhgqk,bkhd->bqhgd', p[..., 3 * A_BLOCK:], v_c))

    o_l = jnp.moveaxis(lax.map(block, (qb, kw, vw, valid)), 0, 1).reshape(b, n, A_WIDTH)
    o_c = None
    if with_ctx:
        qc = q_c.reshape(b, m, A_KV_HEADS, g, HEAD_DIM)
        s = jnp.einsum('bqhgd,bkhd->bhgqk', qc, k_c).astype(F32) * scale
        p = sink_softmax(s, sink).astype(v_c.dtype)
        o_c = jnp.einsum('bhgqk,bkhd->bqhgd', p, v_c).reshape(b, m, A_WIDTH)
    return o_c, o_l


def dense_qknorm_attention(q_c, k_c, v_c, q_l, k_l, v_l, with_ctx):
    b, n = q_l.shape[:2]
    m = q_c.shape[1]
    g = B_HEADS // B_KV_HEADS
    scale = HEAD_DIM ** -0.5
    k_all = jnp.concatenate([k_c, k_l], axis=1)
    v_all = jnp.concatenate([v_c, v_l], axis=1)
    nb = n // Q_BLOCK
    qb = jnp.moveaxis(q_l.reshape(b, nb, Q_BLOCK, B_KV_HEADS, g, HEAD_DIM), 1, 0)

    def block(qblk):
        s = jnp.einsum('bqhgd,bkhd->bhgqk', qblk, k_all).astype(F32) * scale
        p = jax.nn.softmax(s, axis=-1).astype(v_all.dtype)
        return jnp.einsum('bhgqk,bkhd->bqhgd', p, v_all)

    o_l = jnp.moveaxis(lax.map(block, qb), 0, 1).reshape(b, n, B_WIDTH)
    o_c = None
    if with_ctx:
        qc = q_c.reshape(b, m, B_KV_HEADS, g, HEAD_DIM)
        s = jnp.einsum('bqhgd,bkhd->bhgqk', qc, k_c).astype(F32) * scale
        p = jax.nn.softmax(s, axis=-1).astype(v_c.dtype)
        o_c = jnp.einsum('bhgqk,bkhd->bqhgd', p, v_c).reshape(b, m, B_WIDTH)
    return o_c, o_l


def hgrn2_gates(z, lb):
    log_f = jnp.logaddexp(jnp.log(lb), jnp.log1p(-lb) + jax.nn.log_sigmoid(z))
    k = (1.0 - lb) * jax.nn.sigmoid(-z)
    return log_f, k


def chunk_scan(q, k, v, log_f, s0):
    b, L, h, dk = q.shape
    nc = L // C_CHUNK

    def chunks(t):
        return t.reshape(b, nc, C_CHUNK, h, t.shape[-1]).transpose(1, 0, 3, 2, 4)

    tri = jnp.tril(jnp.ones((C_CHUNK, C_CHUNK), dtype=bool))[:, :, None]

    def step(S, inp):
        qc, kc, vc, gc = inp
        cum = jnp.cumsum(gc, axis=2)
        rel = cum[:, :, :, None, :] - cum[:, :, None, :, :]
        decay = jnp.exp(jnp.where(tri, rel, -jnp.inf))
        attn = jnp.einsum('bhtk,bhsk,bhtsk->bhts', qc, kc, decay)
        o = attn @ vc + jnp.einsum('bhtk,bhkv->bhtv', qc * jnp.exp(cum), S)
        tot = cum[:, :, -1:, :]
        S = (jnp.exp(tot[:, :, 0, :, None]) * S
             + jnp.einsum('bhsk,bhsv->bhkv', kc * jnp.exp(tot - cum), vc))
        return S, o

    S, o = lax.scan(step, s0, (chunks(q), chunks(k), chunks(v), chunks(log_f)))
    o = o.transpose(1, 0, 3, 2, 4).reshape(b, L, h, v.shape[-1])
    return o, S


def hgrn2_bidirectional(q, i, z_f, z_b, lb, gain, m, with_ctx):
    b, t, h, dk = q.shape
    dv = i.shape[-1]
    q = q.astype(F32) * dk ** -0.5
    i = i.astype(F32)
    s0 = jnp.zeros((b, h, dk, dv), F32)
    outs = []
    for z, lb_d, rev in ((z_f, lb[0], False), (z_b, lb[1], True)):
        log_f, k = hgrn2_gates(z.astype(F32), lb_d.reshape(h, dk))
        flip = (lambda a: jnp.flip(a, axis=1)) if rev else (lambda a: a)
        o_c, s_c = chunk_scan(flip(q[:, :m]), flip(k[:, :m]), flip(i[:, :m]), flip(log_f[:, :m]), s0)
        o_l, _ = chunk_scan(flip(q[:, m:]), flip(k[:, m:]), flip(i[:, m:]), flip(log_f[:, m:]), s_c)
        outs.append(jnp.concatenate([flip(o_c), flip(o_l)], axis=1) if with_ctx else flip(o_l))
    o = rms_norm(outs[0] + outs[1], gain)
    return o.reshape(b, o.shape[1], h * dv)


def join(o_c, o_l, with_ctx):
    return jnp.concatenate([o_c, o_l], axis=1) if with_ctx else o_l


def hybrid_layer(x_lat, x_ctx, c, c_ctx, ada_w, ada_b, norm_g, w_in, a_sink, b_qn, b_kn,
                 lb, c_gn, w_ba, w_bb, w_bc, w_out, cos, sin, with_ctx):
    b, n, _ = x_lat.shape
    m = x_ctx.shape[1]
    sh_l, sc_l, gt_l = jnp.split(jax.nn.silu(c) @ ada_w + ada_b, 3, axis=-1)
    sh_c, sc_c, gt_c = jnp.split(jax.nn.silu(c_ctx) @ ada_w + ada_b, 3, axis=-1)
    h_lat = rms_norm(x_lat, norm_g) * (1.0 + sc_l[:, None]) + sh_l[:, None]
    h_ctx = rms_norm(x_ctx, norm_g) * (1.0 + sc_c) + sh_c
    h = jnp.concatenate([h_ctx, h_lat], axis=1)
    w_parts = jnp.split(w_in, np.cumsum(IN_SPLITS)[:-1].tolist(), axis=1)
    (a_q, a_k, a_v, a_z, b_q, b_k, b_v, b_z,
     c_q, c_zf, c_zb, c_i, c_z, br_gate) = [h @ w for w in w_parts]

    def heads(t, nh, d=HEAD_DIM):
        return t.reshape(b, m + n, nh, d)

    aq, ak, av = heads(a_q, A_HEADS), heads(a_k, A_KV_HEADS), heads(a_v, A_KV_HEADS)
    oa_c, oa_l = window_sink_attention(aq[:, :m], ak[:, :m], av[:, :m],
                                       apply_rope(aq[:, m:], cos, sin), apply_rope(ak[:, m:], cos, sin),
                                       av[:, m:], a_sink, with_ctx)
    bq = rms_norm(heads(b_q, B_HEADS), b_qn)
    bk = rms_norm(heads(b_k, B_KV_HEADS), b_kn)
    bv = heads(b_v, B_KV_HEADS)
    ob_c, ob_l = dense_qknorm_attention(bq[:, :m], bk[:, :m], bv[:, :m],
                                        apply_rope(bq[:, m:], cos, sin), apply_rope(bk[:, m:], cos, sin),
                                        bv[:, m:], with_ctx)
    oc = hgrn2_bidirectional(heads(c_q, C_HEADS, C_DK), heads(c_i, C_HEADS, C_DV),
                             heads(c_zf, C_HEADS, C_DK), heads(c_zb, C_HEADS, C_DK),
                             lb, c_gn, m, with_ctx).astype(x_lat.dtype)

    q0 = 0 if with_ctx else m
    y_a = (join(oa_c, oa_l, with_ctx) * jax.nn.silu(a_z[:, q0:])) @ w_ba
    y_b = (join(ob_c, ob_l, with_ctx) * jax.nn.silu(b_z[:, q0:])) @ w_bb
    y_c = (oc * jax.nn.silu(c_z[:, q0:])) @ w_bc
    g_a, g_b, g_c = jnp.split(jax.nn.sigmoid(br_gate[:, q0:]), 3, axis=-1)
    out = (g_a * y_a + g_b * y_b + g_c * y_c) @ w_out
    new_lat = x_lat + gt_l[:, None] * out[:, -n:]
    new_ctx = x_ctx + gt_c * out[:, :m] if with_ctx else x_ctx
    return new_lat, new_ctx


def setup_inputs(seed: int = 0) -> dict:
    key = jax.random.key(seed)
    ks = jax.random.split(key, 20)
    d = D_MODEL

    def nrm(k, shape, scale):
        return jax.random.normal(k, shape, F32) * scale

    return {
        "x": nrm(ks[0], (BATCH, SEQ, d), 1.0),
        "c": nrm(ks[1], (BATCH, d), 1.0),
        "ctx": nrm(ks[2], (BATCH, CTX_LEN, d), 1.0),
        "c_ctx": nrm(ks[3], (d,), 1.0),
        "ada_w": nrm(ks[4], (DEPTH, d, 3 * d), 0.5 * d ** -0.5),
        "ada_b": nrm(ks[5], (DEPTH, 3 * d), 0.02),
        "norm_g": 1.0 + nrm(ks[6], (DEPTH, d), 0.02),
        "w_in": nrm(ks[7], (DEPTH, d, W_IN_TOTAL), d ** -0.5),
        "a_sink": nrm(ks[8], (DEPTH, A_HEADS), 0.5),
        "b_q_norm": 1.0 + nrm(ks[9], (DEPTH, HEAD_DIM), 0.02),
        "b_k_norm": 1.0 + nrm(ks[10], (DEPTH, HEAD_DIM), 0.02),
        "c_lower_bound": nrm(ks[11], (DEPTH, 2, C_HEADS * C_DK), 0.1),
        "c_out_norm": 1.0 + nrm(ks[12], (DEPTH, C_DV), 0.02),
        "w_branch_a": nrm(ks[13], (DEPTH, A_WIDTH, d), A_WIDTH ** -0.5),
        "w_branch_b": nrm(ks[14], (DEPTH, B_WIDTH, d), B_WIDTH ** -0.5),
        "w_branch_c": nrm(ks[15], (DEPTH, C_WIDTH, d), C_WIDTH ** -0.5),
        "w_out": nrm(ks[16], (DEPTH, d, d), d ** -0.5),
        "final_norm_g": 1.0 + nrm(ks[17], (d,), 0.02),
    }


def reference(x, c, ctx, c_ctx, ada_w, ada_b, norm_g, w_in, a_sink, b_q_norm, b_k_norm,
              c_lower_bound, c_out_norm, w_branch_a, w_branch_b, w_branch_c, w_out, final_norm_g):
    n = x.shape[1]
    cos, sin = axial_rope_tables(n)
    lb_all = jnp.cumsum(jax.nn.softmax(c_lower_bound.astype(F32), axis=0), axis=0)
    lb_all = lb_all - lb_all[0:1]
    h_lat, h_ctx = x, ctx
    for l in range(DEPTH):
        h_lat, h_ctx = hybrid_layer(h_lat, h_ctx, c, c_ctx, ada_w[l], ada_b[l], norm_g[l], w_in[l],
                                    a_sink[l], b_q_norm[l], b_k_norm[l], lb_all[l], c_out_norm[l],
                                    w_branch_a[l], w_branch_b[l], w_branch_c[l], w_out[l],
                                    cos, sin, l < DEPTH - 1)
    return rms_norm(h_lat, final_norm_g)
```

```python
import numpy as np
import concourse.bass as bass
import concourse.mybir as mybir
from concourse.bass_utils import run_bass_kernel_spmd

F32 = mybir.dt.float32
BF16 = mybir.dt.bfloat16
AF = mybir.ActivationFunctionType
ALU = mybir.AluOpType

L = 4
D = 2048
T = 2304
MC = 256
NL = 2048
BLK = [(0, 256), (256, 512), (768, 512), (1280, 512), (1792, 512)]
EPS = 1e-6
ENGS = ['pe', 'act', 'dve', 'pool', 'sp']
NCORES = 2


class Res:
    __slots__ = ('w', 'r')

    def __init__(self):
        self.w = None
        self.r = {}


class Sched:
    NDMA = 8

    def __init__(self):
        self.ops = {e: [] for e in ENGS}
        self.clock = {e: {} for e in ENGS}
        self.dma_n = 0
        self.dma_cnt = [0] * self.NDMA

    @staticmethod
    def _kv(tok):
        if tok[0] == 'dma':
            return ('dma', tok[1]), tok[2]
        return tok[0], tok[1]

    def op(self, eng, fn, reads=(), writes=(), dma=False):
        idx = len(self.ops[eng])
        deps = []
        for t in reads:
            if t.w is not None:
                deps.append((t.w, True))
        for t in writes:
            if t.w is not None:
                deps.append((t.w, False))
            for k, v in t.r.items():
                deps.append(((('dma', k[1], v) if k[0] == 'dma' else (k, v)), False))
        clk = self.clock[eng]
        if dma:
            slot = self.dma_n % self.NDMA
            self.dma_n += 1
            if self.dma_cnt[slot] > 0:
                deps.append((('dma', slot, self.dma_cnt[slot]), False))
            self.dma_cnt[slot] += 1
            mytok = ('dma', slot, self.dma_cnt[slot])
        else:
            mytok = (eng, idx)
        best = {}
        for tok, raw in deps:
            key, val = self._kv(tok)
            if key == eng and (eng in ('pe', 'sp') or not raw):
                continue
            if clk.get(key, -1) >= val:
                continue
            clk[key] = val
            best[key] = tok
            if tok[0] != 'dma':
                o = self.ops[tok[0]][tok[1]]
                o['sig'] = True
                for k2, v2 in o['clk'].items():
                    if clk.get(k2, -1) < v2:
                        clk[k2] = v2
        snap = dict(clk)
        if not dma:
            snap[eng] = idx
        rec = dict(fn=fn, waits=list(best.values()), sig=False, clk=snap, tok=mytok, dma=dma)
        self.ops[eng].append(rec)
        mk, mv = self._kv(mytok)
        for t in reads:
            if t.r.get(mk, -1) < mv:
                t.r[mk] = mv
        for t in writes:
            t.w = mytok
            t.r = {}
        return mytok

    def barrier(self):
        rs = []
        for e in ENGS:
            if e != 'sp' and self.ops[e]:
                r = Res()
                r.w = (e, len(self.ops[e]) - 1)
                rs.append(r)
        for s in range(self.NDMA):
            if self.dma_cnt[s] > 0:
                r = Res()
                r.w = ('dma', s, self.dma_cnt[s])
                rs.append(r)
        for e in ENGS:
            self.op(e, lambda g: g.nop(), reads=rs)

    def emit(self, nc):
        sems = {e: nc.alloc_semaphore('s_' + e) for e in ENGS if e != 'sp'}
        dsems = [nc.alloc_semaphore('d_%d' % i) for i in range(self.NDMA)]
        cnt = {}
        for e in ENGS:
            c = 0
            arr = []
            for o in self.ops[e]:
                if o['sig'] and not o['dma']:
                    c += 1
                arr.append(c)
            cnt[e] = arr

        def emit_engine(ename, eng):
            for o in self.ops[ename]:
                for tok in o['waits']:
                    if tok[0] == 'dma':
                        eng.wait_ge(dsems[tok[1]], 16 * tok[2])
                    else:
                        eng.wait_ge(sems[tok[0]], cnt[tok[0]][tok[1]])
                ins = o['fn'](eng)
                if o['dma']:
                    ins.then_inc(dsems[o['tok'][1]], 16)
                elif o['sig']:
                    ins.then_inc(sems[ename], 1)

        with nc.Block() as block:
            @block.tensor
            def _(e):
                emit_engine('pe', e)

            @block.scalar
            def _(e):
                emit_engine('act', e)

            @block.vector
            def _(e):
                emit_engine('dve', e)

            @block.gpsimd
            def _(e):
                emit_engine('pool', e)

            @block.sync
            def _(e):
                emit_engine('sp', e)


def build(NB, LAYERS=L, dbg=False, stop_after=None):
    nc = bass.Bass('TRN2', target_bir_lowering=False, dynamic_dma_scratch_size=2048)
    S = Sched()

    def din(name, shape, dt=F32):
        return nc.dram_tensor(name, list(shape), dt, kind='ExternalInput').ap()

    xT = din('xT', [NB, 16, 128, T])
    cT = din('cT', [128, 16, NB + 1])
    ada_w = din('ada_w', [LAYERS, 16, 128, 16, 384])
    ada_b = din('ada_b', [128, L, 48])
    norm_g = din('norm_g', [128, L, 16])
    fnorm_g = din('fnorm_g', [128, 16])
    w_in = din('w_in', [LAYERS, 128, 128, 16, 128])
    w_br = din('w_br', [LAYERS, 3, 16, 128, 8, 128])
    w_out = din('w_out', [LAYERS, 16, 128, 16, 128])
    sink_in = din('sink', [128, L, 8])
    qkn_in = din('qkn', [128, L, 3])
    clb_in = din('clb', [128, 2, 8, L])
    rope_in = din('rope', [128, 2, NL])
    masks_in = din('masks', [128, 9, 128])
    m64_in = din('m64', [128, 512])
    outT = nc.dram_tensor('outT', [NB, 16, 128, NL], F32, kind='ExternalOutput').ap()
    XS = nc.dram_tensor('XS', [NB, 16, 128, T], F32).ap()
    G = nc.dram_tensor('G', [24, 128, T], BF16).ap()
    if dbg:
        dbg_h = nc.dram_tensor('dbg_h', [16, 128, T], BF16, kind='ExternalOutput').ap()
        dbg_g = nc.dram_tensor('dbg_g', [24, 128, T], BF16, kind='ExternalOutput').ap()
        dbg_x = nc.dram_tensor('dbg_x', [16, 128, T], F32, kind='ExternalOutput').ap()
        dbg_mod = nc.dram_tensor('dbg_mod', [128, L * (NB + 1) * 48], F32, kind='ExternalOutput').ap()

    def sb(name, shape, dt=F32):
        return nc.alloc_sbuf_tensor('sb_' + name, list(shape), dt)

    hT = sb('hT', [128, 16, T], BF16)
    hR = [Res() for _ in range(16)]
    arena = sb('arena', [128, 26624], BF16)
    wst = [sb('wst%d' % i, [128, 16, 128]) for i in range(2)]
    wstR = [Res() for _ in range(2)]
    wbf = [sb('wbf%d' % i, [128, 16, 128], BF16) for i in range(3)]
    wbfR = [Res() for _ in range(3)]
    rope = sb('rope', [128, 2, NL])
    ropeR = Res()
    NTMP = 8
    tmp = [sb('tmp%d' % i, [128, 512]) for i in range(NTMP)]
    tmpR = [Res() for _ in range(NTMP)]
    mod = sb('mod', [128, L, NB + 1, 48])
    modR = Res()
    gp = sb('gp', [128, 2, 16])
    gpR = Res()
    ng = sb('ng', [128, L, 16])
    fg = sb('fg', [128, 16])
    adab = sb('adab', [128, L, 48])
    cs = sb('cs', [128, 16, NB + 1])
    csig = sb('csig', [128, 16, NB + 1])
    sinkE = sb('sinkE', [128, L, 8])
    qkn = sb('qkn_s', [128, L, 3])
    clb = sb('clb_s', [128, 2, 8, L])
    lbt = sb('lbt', [128, 2, 8, L])
    oml = sb('oml', [128, 2, 8, L])
    noml = sb('noml', [128, 2, 8, L])
    lbtmp = sb('lbtmp', [128, 2, 8])
    masks = sb('masks_s', [128, 9, 128])
    m64 = sb('m64_s', [128, 512])
    identb = sb('identb', [128, 128], BF16)
    onesb = sb('onesb', [128, 128], BF16)
    scal = sb('scal', [128, 2, 3, 36])
    scalR = Res()
    stmp = sb('stmp', [128, 16])
    stmpR = Res()
    Sst = sb('Sst', [128, 128])
    SstR = Res()
    Sbf = [sb('Sbf%d' % i, [128, 128], BF16) for i in range(2)]
    SbfR = [Res(), Res()]
    aTs = [sb('aTs%d' % i, [128, 128], BF16) for i in range(2)]
    aTsR = [Res(), Res()]
    pT = [sb('pT%d' % i, [128, 512], BF16) for i in range(3)]
    pTR = [Res() for _ in range(3)]
    gout = [sb('gout%d' % i, [128, 512], BF16) for i in range(2)]
    goutR = [Res(), Res()]
    sqb = [sb('sqb%d' % i, [128, 512], BF16) for i in range(2)]
    sqbR = [Res(), Res()]
    constR = Res()
    rsT = [sb('rsT%d' % i, [128, 512]) for i in range(2)]
    rsTR = [Res(), Res()]
    zsT = [sb('zsT%d' % i, [128, 512]) for i in range(2)]
    zsTR = [Res(), Res()]

    PS = [nc.alloc_psum_tensor('ps%d' % i, [128, 512], F32) for i in range(7)]
    PSR = [Res() for _ in range(7)]
    PSB = nc.alloc_psum_tensor('psb', [128, 512], BF16)
    PSBR = Res()

    def av(off, shape, dt=BF16):
        n = 1
        for s_ in shape:
            n *= s_
        if dt == F32:
            a = arena[:, off:off + 2 * n].bitcast(F32)
        else:
            a = arena[:, off:off + n]
        if len(shape) == 2:
            return a.rearrange('p (a b) -> p a b', b=shape[1])
        return a

    A_ = [av(i * T, [T]) for i in range(4)]
    AR = [Res() for _ in range(4)]
    V_ = [av(4 * T + i * T, [18, 128]) for i in range(3)]
    VR = [Res() for _ in range(3)]
    QF = av(7 * T, [T], F32)
    QFR = Res()
    OACC = av(9 * T, [T], F32)
    OACCR = Res()
    assert 11 * T <= 26624
    Gblk = av(0, [24, 512])
    GblkR = [Res() for _ in range(24)]
    Ublk = av(24 * 512, [16, 512])
    UblkR = [Res() for _ in range(16)]
    ADA = [av(i * 12288, [16, 384], F32) for i in range(2)]
    ADAR = [Res(), Res()]

    cnt = {'w': 0, 't': 0, 'ps': 0, 'pt': 0, 'g': 0, 'sq': 0, 'rs': 0, 'zs': 0}

    def newtmp():
        i = cnt['t'] % NTMP
        cnt['t'] += 1
        return tmp[i], tmpR[i]

    def newps():
        i = cnt['ps'] % 2
        cnt['ps'] += 1
        return PS[i], PSR[i]

    def dma(out, in_, reads, writes):
        S.op('sp', lambda e, o=out, i=in_: e.dma_start(out=o, in_=i), reads=reads, writes=writes, dma=True)

    def mm(out, lhsT, rhs, start, stop, reads, writes):
        S.op('pe', lambda e, o=out, a=lhsT, b=rhs, s0=start, s1=stop: e.matmul(o, a, b, start=s0, stop=s1),
             reads=reads, writes=writes)

    def act(out, in_, func, reads, writes, bias=None, scale=None):
        kw = {}
        if bias is not None:
            kw['bias'] = bias
        if scale is not None:
            kw['scale'] = scale
        S.op('act', lambda e, o=out, i=in_, f=func, k=kw: e.activation(o, i, f, **k), reads=reads, writes=writes)

    def tt(eng, out, in0, in1, op, reads, writes):
        S.op(eng, lambda e, o=out, a=in0, b=in1, p=op: e.tensor_tensor(o, a, b, p), reads=reads, writes=writes)

    def ts(eng, out, in0, s1, s2, op0, op1, reads, writes):
        if op1 is None:
            S.op(eng, lambda e, o=out, a=in0, x=s1, p0=op0: e.tensor_scalar(o, a, x, None, p0),
                 reads=reads, writes=writes)
        else:
            S.op(eng, lambda e, o=out, a=in0, x=s1, y=s2, p0=op0, p1=op1: e.tensor_scalar(o, a, x, y, p0, p1),
                 reads=reads, writes=writes)

    def stt(out, in0, scalar, in1, op0, op1, reads, writes):
        S.op('dve', lambda e, o=out, a=in0, s_=scalar, b=in1, p0=op0, p1=op1:
             e.scalar_tensor_tensor(o, a, s_, b, p0, p1), reads=reads, writes=writes)

    def cp(eng, out, in_, reads, writes):
        if eng == 'act':
            S.op(eng, lambda e, o=out, i=in_: e.activation(o, i, AF.Identity), reads=reads, writes=writes)
        else:
            S.op(eng, lambda e, o=out, i=in_: e.tensor_copy(o, i), reads=reads, writes=writes)

    def load_w(src, nkc):
        i = cnt['w']
        cnt['w'] += 1
        st, stR = wst[i % 2], wstR[i % 2]
        wb, wbR = wbf[i % 3], wbfR[i % 3]
        dma(st[:, 0:nkc, :], src, [], [stR])
        cp('pool', wb[:, 0:nkc, :], st[:, 0:nkc, :], [stR], [wbR])
        return wb, wbR

    def proj_fm(wb, wbR, nkc, rhs_fn, ps, psR, n):
        for kc in range(nkc):
            r_ap, r_res = rhs_fn(kc)
            mm(ps[:, 0:n], wb[:, kc, :], r_ap, kc == 0, kc == nkc - 1, [wbR, r_res], [psR])

    def h_rhs(t0, n):
        return lambda kc: (hT[:, kc, t0:t0 + n], hR[kc])

    def rstd_from_ps(ps, psR, n, inv_n):
        t1, t1R = newtmp()
        act(t1[:, 0:n], ps[:, 0:n], AF.Ln, [psR, constR], [t1R], bias=epsT[:, 0:1], scale=inv_n)
        i = cnt['rs'] % 2
        cnt['rs'] += 1
        t2, t2R = rsT[i], rsTR[i]
        act(t2[:, 0:n], t1[:, 0:n], AF.Exp, [t1R], [t2R], scale=-0.5)
        return t2, t2R

    def colsumsq(src_ap, srcR, ps, psR, n, first, last):
        i = cnt['sq'] % 2
        cnt['sq'] += 1
        act(sqb[i][:, 0:n], src_ap, AF.Square, [srcR], [sqbR[i]])
        mm(ps[:, 0:n], onesb[:, :], sqb[i][:, 0:n], first, last, [sqbR[i], constR], [psR])

    epsT = sb('epsT', [128, 1])
    S.op('dve', lambda e: e.memset(epsT[:], EPS), writes=[constR])
    S.op('dve', lambda e: e.memset(onesb[:], 1.0), writes=[constR])
    ldR = Res()
    for dst, src in [(ng[:], norm_g), (fg[:], fnorm_g), (adab[:], ada_b), (cs[:], cT), (sinkE[:], sink_in),
                     (qkn[:], qkn_in), (clb[:], clb_in), (rope[:], rope_in), (masks[:], masks_in), (m64[:], m64_in)]:
        dma(dst, src, [], [ldR])
    cp('dve', identb[:], masks[:, 4, :], [ldR], [constR])
    act(csig[:], cs[:], AF.Sigmoid, [ldR], [constR])
    tt('dve', cs[:], cs[:], csig[:], ALU.mult, [constR, ldR], [constR])
    act(sinkE[:], sinkE[:], AF.Exp, [ldR], [constR])
    act(clb[:], clb[:], AF.Exp, [ldR], [constR])
    S.op('dve', lambda e: e.tensor_reduce(lbtmp[:], clb[:], mybir.AxisListType.X, ALU.add), reads=[constR], writes=[constR])
    S.op('dve', lambda e: e.reciprocal(lbtmp[:], lbtmp[:]), reads=[constR], writes=[constR])
    S.op('dve', lambda e: e.memset(lbt[:], 0.0), writes=[constR])
    for l in range(1, L):
        tt('dve', clb[:, :, :, l], clb[:, :, :, l], lbtmp[:], ALU.mult, [constR], [constR])
        tt('dve', lbt[:, :, :, l], lbt[:, :, :, l - 1], clb[:, :, :, l], ALU.add, [constR], [constR])
    ts('dve', oml[:], lbt[:], -1.0, 1.0, ALU.mult, ALU.add, [constR], [constR])
    ts('dve', noml[:], oml[:], -1.0, None, ALU.mult, None, [constR], [constR])

    nslab = 0
    for l in range(LAYERS):
        for s_ in range(16):
            a, aR = ADA[nslab % 2], ADAR[nslab % 2]
            nslab += 1
            dma(a, ada_w[l, s_], [], [aR])
            for jj in range(3):
                j = 3 * s_ + jj
                ps, psR = newps()
                for kc in range(16):
                    mm(ps[:, 0:NB + 1], a[:, kc, jj * 128:(jj + 1) * 128], cs[:, kc, :], kc == 0, kc == 15,
                       [aR, constR], [psR])
                ts('dve', mod[:, l, :, j], ps[:, 0:NB + 1], adab[:, l, j:j + 1], None, ALU.add, None,
                   [psR, ldR], [modR])
    S.barrier()

    def norm_phase(b, l, src, final):
        if not final:
            for ci, col in enumerate((b, NB)):
                ts('dve', gp[:, ci, :], mod[:, l, col, 16:32], 1.0, None, ALU.add, None, [modR], [gpR])
                tt('dve', gp[:, ci, :], gp[:, ci, :], ng[:, l, :], ALU.mult, [gpR, ldR], [gpR])
        for bi, (t0, n) in enumerate(BLK):
            if final and bi == 0:
                continue
            ps, psR = newps()
            for kc in range(16):
                xt, xtR = newtmp()
                dma(xt[:, 0:n], src[b, kc, :, t0:t0 + n], [xsR[kc]], [xtR])
                colsumsq(xt[:, 0:n], xtR, ps, psR, n, kc == 0, kc == 15)
            rs, rsR = rstd_from_ps(ps, psR, n, 1.0 / D)
            ci = 1 if bi == 0 else 0
            col = NB if bi == 0 else b
            dbgn = dbg and b == 0 and l == 0 and bi == 1 and not final
            if dbgn:
                d1 = nc.dram_tensor('dbg_rs', [128, 512], F32, kind='ExternalOutput').ap()
                dma(d1, rs[:, :], [rsR], [outR])
                d3 = nc.dram_tensor('dbg_gp', [128, 32], F32, kind='ExternalOutput').ap()
                dma(d3, gp[:].rearrange('p a b -> p (a b)'), [gpR], [outR])
                d4 = nc.dram_tensor('dbg_ng', [128, 64], F32, kind='ExternalOutput').ap()
                dma(d4, ng[:].rearrange('p a b -> p (a b)'), [ldR], [outR])
            for kc in range(16):
                xt, xtR = newtmp()
                dma(xt[:, 0:n], src[b, kc, :, t0:t0 + n], [xsR[kc]], [xtR])
                if final:
                    stt(xt[:, 0:n], xt[:, 0:n], fg[:, kc:kc + 1], rs[:, 0:n], ALU.mult, ALU.mult,
                        [xtR, rsR, ldR], [xtR])
                    dma(outT[b, kc, :, t0 - MC:t0 - MC + n], xt[:, 0:n], [xtR], [outR])
                else:
                    if dbgn and kc == 0:
                        d5 = nc.dram_tensor('dbg_x0', [128, 512], F32, kind='ExternalOutput').ap()
                        dma(d5, xt[:, :], [xtR], [outR])
                    tt('dve', xt[:, 0:n], xt[:, 0:n], rs[:, 0:n], ALU.mult, [xtR, rsR], [xtR])
                    if dbgn and kc == 0:
                        d6 = nc.dram_tensor('dbg_x1', [128, 512], F32, kind='ExternalOutput').ap()
                        dma(d6, xt[:, :], [xtR], [outR])
                    act(hT[:, kc, t0:t0 + n], xt[:, 0:n], AF.Identity, [xtR, gpR, modR], [hR[kc]],
                        bias=mod[:, l, col, kc:kc + 1], scale=gp[:, ci, kc:kc + 1])

    def rope_apply(src, srcR, dst_ap, dstR, t0, n):
        p0 = t0 - MC
        t1, t1R = newtmp()
        tt('pool', t1[:, 0:n], src[:, 0:n], rope[:, 0, p0:p0 + n], ALU.mult, [srcR, ldR], [t1R])
        t2, t2R = newtmp()
        tt('dve', t2[0:64, 0:n], src[64:128, 0:n], rope[64:128, 1, p0:p0 + n], ALU.mult, [srcR, ldR], [t2R])
        tt('dve', t2[64:128, 0:n], src[0:64, 0:n], rope[0:64, 1, p0:p0 + n], ALU.mult, [srcR, ldR], [t2R])
        tt('pool', dst_ap, t1[:, 0:n], t2[:, 0:n], ALU.add, [t1R, t2R], [dstR])

    def qk_proj(l, j, dstA, dstR, norm_col):
        wb, wbR = load_w(w_in[l, j], 16)
        for bi, (t0, n) in enumerate(BLK):
            ps, psR = newps()
            proj_fm(wb, wbR, 16, h_rhs(t0, n), ps, psR, n)
            q, qR = newtmp()
            if norm_col is None:
                cp('act', q[:, 0:n], ps[:, 0:n], [psR], [qR])
            else:
                cp('act', q[:, 0:n], ps[:, 0:n], [psR], [qR])
                ps2, ps2R = PS[6], PSR[6]
                colsumsq(q[:, 0:n], qR, ps2, ps2R, n, True, True)
                rs, rsR = rstd_from_ps(ps2, ps2R, n, 1.0 / 128)
                stt(q[:, 0:n], q[:, 0:n], qkn[:, l, norm_col:norm_col + 1], rs[:, 0:n], ALU.mult, ALU.mult,
                    [qR, rsR, ldR], [qR])
            if bi == 0:
                cp('pool', dstA[:, t0:t0 + n], q[:, 0:n], [qR], [dstR])
            else:
                rope_apply(q, qR, dstA[:, t0:t0 + n], dstR, t0, n)

    def v_proj(l, j, dstV, dstR):
        wb, wbR = load_w(w_in[l, j], 16)
        for grp in range(5):
            tts = list(range(grp * 4, min(18, grp * 4 + 4)))
            ps, psR = newps()
            for ii, tti in enumerate(tts):
                for kc in range(16):
                    mm(ps[:, ii * 128:(ii + 1) * 128], hT[:, kc, tti * 128:(tti + 1) * 128], wb[:, kc, :],
                       kc == 0, kc == 15, [wbR, hR[kc]], [psR])
            nn = len(tts) * 128
            cp('act', dstV[:, tts[0]:tts[0] + len(tts), :], ps[:, 0:nn].rearrange('p (a b) -> p a b', b=128),
               [psR], [dstR])

    def gate_zs(l, jz, t0, n):
        wb, wbR = load_w(w_in[l, jz], 16)
        ps, psR = newps()
        proj_fm(wb, wbR, 16, h_rhs(t0, n), ps, psR, n)
        i = cnt['zs'] % 2
        cnt['zs'] += 1
        sg, sgR = zsT[i], zsTR[i]
        act(sg[:, 0:n], ps[:, 0:n], AF.Sigmoid, [psR], [sgR])
        tt('dve', sg[:, 0:n], ps[:, 0:n], sg[:, 0:n], ALU.mult, [psR, sgR], [sgR])
        return sg, sgR

    def store_g(src, srcR, n, zs, zsR, zoff, gchunk, t0):
        i = cnt['g'] % 2
        cnt['g'] += 1
        tt('pool', gout[i][:, 0:n], src[:, 0:n], zs[:, zoff:zoff + n], ALU.mult, [srcR, zsR], [goutR[i]])
        dma(G[gchunk, :, t0:t0 + n], gout[i][:, 0:n], [goutR[i]], [gR[gchunk]])

    def attend(KT, KR, VT, VTR, QT, QR, q0, qn, klist, sink_ap, zs, zsR, zoff, gchunk):
        oT, oTR = PS[4], PSR[4]
        dn, dnR = PS[5], PSR[5]
        nk = len(klist)
        for idx, (kt, mk) in enumerate(klist):
            sT, sTR = PS[2 + idx % 2], PSR[2 + idx % 2]
            mm(sT[:, 0:qn], KT[:, kt * 128:(kt + 1) * 128], QT[:, q0:q0 + qn], True, True, [KR, QR], [sTR])
            pi = cnt['pt'] % 3
            cnt['pt'] += 1
            act(pT[pi][:, 0:qn], sT[:, 0:qn], AF.Exp, [sTR], [pTR[pi]], scale=128 ** -0.5)
            if mk is not None:
                tt('pool', pT[pi][:, 0:qn], pT[pi][:, 0:qn], masks[:, mk, 0:qn], ALU.mult, [pTR[pi], ldR], [pTR[pi]])
            mm(oT[:, 0:qn], VT[:, kt, :], pT[pi][:, 0:qn], idx == 0, idx == nk - 1, [VTR, pTR[pi]], [oTR])
            mm(dn[:, 0:qn], onesb[:, :], pT[pi][:, 0:qn], idx == 0, idx == nk - 1, [constR, pTR[pi]], [dnR])
        r, rR = newtmp()
        if sink_ap is not None:
            ts('dve', r[:, 0:qn], dn[:, 0:qn], sink_ap, None, ALU.add, None, [dnR, constR], [rR])
            S.op('dve', lambda e, a=r[:, 0:qn]: e.reciprocal(a, a), reads=[rR], writes=[rR])
        else:
            S.op('dve', lambda e, a=r[:, 0:qn], d=dn[:, 0:qn]: e.reciprocal(a, d), reads=[dnR], writes=[rR])
        tt('dve', r[:, 0:qn], oT[:, 0:qn], r[:, 0:qn], ALU.mult, [oTR, rR], [rR])
        store_g(r, rR, qn, zs, zsR, zoff, gchunk, q0)

    def mixer_attn(l, which, with_ctx):
        base = 0 if which == 'a' else 20
        for g in range(2):
            KT, KR = A_[1], AR[1]
            VT, VTR = V_[0], VR[0]
            qk_proj(l, base + 8 + g, KT, KR, None if which == 'a' else 1)
            v_proj(l, base + 10 + g, VT, VTR)
            for i in range(4):
                h = 4 * g + i
                QT, QR = A_[0], AR[0]
                qk_proj(l, base + h, QT, QR, None if which == 'a' else 0)
                sink_ap = sinkE[:, l, h:h + 1] if which == 'a' else None
                gchunk = (0 if which == 'a' else 8) + h
                for bi, (t0, n) in enumerate(BLK):
                    if bi == 0 and not with_ctx:
                        continue
                    zs, zsR = gate_zs(l, base + 12 + h, t0, n)
                    if bi == 0:
                        attend(KT, KR, VT, VTR, QT, QR, 0, 256, [(0, None), (1, None)], sink_ap, zs, zsR, 0, gchunk)
                    elif which == 'b':
                        attend(KT, KR, VT, VTR, QT, QR, t0, n, [(k, None) for k in range(18)], sink_ap,
                               zs, zsR, 0, gchunk)
                    else:
                        for sbk in range(4):
                            nblk = (t0 - MC) // 128 + sbk
                            kl = [(0, None), (1, None)]
                            if nblk > 0:
                                kl.append((2 + nblk - 1, 2))
                            kl.append((2 + nblk, None))
                            if nblk < 15:
                                kl.append((2 + nblk + 1, 3))
                            attend(KT, KR, VT, VTR, QT, QR, t0 + sbk * 128, 128, kl, sink_ap, zs, zsR,
                                   sbk * 128, gchunk)

    def mixer_c(l, with_ctx):
        for h in range(8):
            wb, wbR = load_w(w_in[l, 40 + h], 16)
            for (t0, n) in BLK:
                ps, psR = newps()
                proj_fm(wb, wbR, 16, h_rhs(t0, n), ps, psR, n)
                act(QF[:, t0:t0 + n], ps[:, 0:n], AF.Identity, [psR], [QFR], scale=128 ** -0.5)
            v_proj(l, 64 + h, V_[0], VR[0])
            for d in range(2):
                qt_, qtR = A_[2 * d], AR[2 * d]
                kt_, ktR = A_[2 * d + 1], AR[2 * d + 1]
                wb, wbR = load_w(w_in[l, 48 + 8 * d + h], 16)
                lb_ap = lbt[:, d, h, l:l + 1]
                oml_ap = oml[:, d, h, l:l + 1]
                noml_ap = noml[:, d, h, l:l + 1]
                ref = 31 if d == 0 else 32
                for (t0, n) in BLK:
                    nch = n // 64
                    c0 = t0 // 64
                    ps, psR = newps()
                    proj_fm(wb, wbR, 16, h_rhs(t0, n), ps, psR, n)
                    r, rR = newtmp()
                    act(r[:, 0:n], ps[:, 0:n], AF.Sigmoid, [psR], [rR])
                    lf, lfR = newtmp()
                    act(lf[:, 0:n], r[:, 0:n], AF.Ln, [rR, constR], [lfR], bias=lb_ap, scale=oml_ap)
                    kk, kkR = newtmp()
                    ts('dve', kk[:, 0:n], r[:, 0:n], noml_ap, oml_ap, ALU.mult, ALU.add, [rR, constR], [kkR])
                    cm, cmR = newtmp()
                    S.op('dve', lambda e, o=cm[:, 0:n], a=m64[:, 0:n], b_=lf[:, 0:n]:
                         e.tensor_tensor_scan(o, a, b_, 0.0, ALU.mult, ALU.add), reads=[ldR, lfR], writes=[cmR])
                    cm3 = cm[:, 0:n].rearrange('p (c k) -> p c k', k=64)
                    al = scal[:, d, 0, c0:c0 + nch]
                    be = scal[:, d, 1, c0:c0 + nch]
                    ga = scal[:, d, 2, c0:c0 + nch]
                    if d == 0:
                        C3 = cm3
                        CR = cmR
                        act(al, cm3[:, :, ref], AF.Exp, [cmR], [scalR])
                        act(be, cm3[:, :, 63], AF.Exp, [cmR], [scalR])
                        tt('dve', stmp[:, 0:nch], cm3[:, :, 63], cm3[:, :, ref], ALU.subtract, [cmR], [stmpR])
                        act(ga, stmp[:, 0:nch], AF.Exp, [stmpR], [scalR])
                    else:
                        ec, ecR = newtmp()
                        tt('pool', ec[:, 0:n], cm[:, 0:n], lf[:, 0:n], ALU.subtract, [cmR, lfR], [ecR])
                        C3 = ec[:, 0:n].rearrange('p (c k) -> p c k', k=64)
                        CR = ecR
                        act(be, cm3[:, :, 63], AF.Exp, [cmR], [scalR])
                        act(ga, C3[:, :, ref], AF.Exp, [ecR], [scalR])
                        tt('dve', stmp[:, 0:nch], cm3[:, :, 63], C3[:, :, ref], ALU.subtract, [cmR, ecR], [stmpR])
                        act(al, stmp[:, 0:nch], AF.Exp, [stmpR], [scalR])
                    aa, aaR = newtmp()
                    aa3 = aa[:, 0:n].rearrange('p (c k) -> p c k', k=64)
                    tt('dve', aa3, C3, C3[:, :, ref:ref + 1].broadcast_to([128, nch, 64]), ALU.subtract, [CR], [aaR])
                    wq, wqR = newtmp()
                    act(wq[:, 0:n], aa[:, 0:n], AF.Exp, [aaR], [wqR], scale=(1.0 if d == 0 else -1.0))
                    tt('pool', qt_[:, t0:t0 + n], QF[:, t0:t0 + n], wq[:, 0:n], ALU.mult, [QFR, wqR], [qtR])
                    wk, wkR = newtmp()
                    act(wk[:, 0:n], aa[:, 0:n], AF.Exp, [aaR], [wkR], scale=(-1.0 if d == 0 else 1.0))
                    tt('pool', kt_[:, t0:t0 + n], kk[:, 0:n], wk[:, 0:n], ALU.mult, [kkR, wkR], [ktR])
                ktok, ktokR = V_[1 + d], VR[1 + d]
                for grp in range(5):
                    tts = list(range(grp * 4, min(18, grp * 4 + 4)))
                    for ii, tti in enumerate(tts):
                        S.op('pe', lambda e, o=PSB[:, ii * 128:(ii + 1) * 128], a=kt_[:, tti * 128:(tti + 1) * 128]:
                             e.transpose(o, a, identb[:, :]), reads=[ktR, constR], writes=[PSBR])
                    nn = len(tts) * 128
                    cp('act', ktok[:, tts[0]:tts[0] + len(tts), :], PSB[:, 0:nn].rearrange('p (a b) -> p a b', b=128),
                       [PSBR], [ktokR])
            for d in range(2):
                qt_, qtR = A_[2 * d], AR[2 * d]
                kt_, ktR = A_[2 * d + 1], AR[2 * d + 1]
                ktok, ktokR = V_[1 + d], VR[1 + d]
                vtok, vtokR = V_[0], VR[0]
                order = list(range(18)) if d == 0 else [1, 0] + list(range(17, 1, -1))
                S.op('dve', lambda e: e.memset(Sst[:], 0.0), writes=[SstR])
                for pi_, pp in enumerate(order):
                    aps, apsR = PS[2 + pi_ % 2], PSR[2 + pi_ % 2]
                    tk = slice(pp * 128, (pp + 1) * 128)
                    mm(aps[:, 0:128], kt_[:, tk], qt_[:, tk], True, True, [ktR, qtR], [apsR])
                    ai = pi_ % 2
                    mt, mtR = newtmp()
                    tt('dve', mt[:, 0:128], aps[:, 0:128], masks[:, 5 + 2 * d, :], ALU.min, [apsR, ldR], [mtR])
                    tt('dve', aTs[ai][:, :], mt[:, 0:128], masks[:, 6 + 2 * d, :], ALU.max, [mtR, ldR], [aTsR[ai]])
                    ops_, opsR = PS[4], PSR[4]
                    mm(ops_[:, 0:128], vtok[:, pp, :], aTs[ai][:, :], True, False, [vtokR, aTsR[ai]], [opsR])
                    halves = (0, 1) if d == 0 else (1, 0)
                    for hi, hf in enumerate(halves):
                        c = 2 * pp + hf
                        tc_ = slice(pp * 128 + hf * 64, pp * 128 + hf * 64 + 64)
                        si = hi
                        act(Sbf[si][:, :], Sst[:, :], AF.Identity, [SstR, scalR], [SbfR[si]], scale=scal[:, d, 0, c:c + 1])
                        mm(ops_[:, hf * 64:hf * 64 + 64], Sbf[si][:, :], qt_[:, tc_], False, hi == 1,
                           [SbfR[si], qtR], [opsR])
                        ups, upsR = PS[5 + hi % 2], PSR[5 + hi % 2]
                        mm(ups[:, 0:128], ktok[hf * 64:hf * 64 + 64, pp, :], vtok[hf * 64:hf * 64 + 64, pp, :],
                           True, True, [ktokR, vtokR], [upsR])
                        ts('dve', Sst[:, :], Sst[:, :], scal[:, d, 1, c:c + 1], None, ALU.mult, None,
                           [SstR, scalR], [SstR])
                        stt(Sst[:, :], ups[:, 0:128], scal[:, d, 2, c:c + 1], Sst[:, :], ALU.mult, ALU.add,
                            [upsR, SstR, scalR], [SstR])
                    if d == 0:
                        cp('act', OACC[:, tk], ops_[:, 0:128], [opsR], [OACCR])
                    else:
                        tt('dve', OACC[:, tk], ops_[:, 0:128], OACC[:, tk], ALU.add, [opsR, OACCR], [OACCR])
            for bi, (t0, n) in enumerate(BLK):
                if bi == 0 and not with_ctx:
                    continue
                zs, zsR = gate_zs(l, 72 + h, t0, n)
                ps2, ps2R = PS[6], PSR[6]
                colsumsq(OACC[:, t0:t0 + n], OACCR, ps2, ps2R, n, True, True)
                rs, rsR = rstd_from_ps(ps2, ps2R, n, 1.0 / 128)
                o, oR = newtmp()
                stt(o[:, 0:n], OACC[:, t0:t0 + n], qkn[:, l, 2:3], rs[:, 0:n], ALU.mult, ALU.mult,
                    [OACCR, rsR, ldR], [oR])
                store_g(o, oR, n, zs, zsR, 0, 16 + h, t0)

    def phase3(b, l, src, last):
        for bi, (t0, n) in enumerate(BLK):
            if last and bi == 0:
                continue
            for c in range(24):
                dma(Gblk[:, c, 0:n], G[c, :, t0:t0 + n], [gR[c]], [GblkR[c]])
            col = NB if bi == 0 else b
            for j in range(16):
                ua, uaR = newtmp()
                for x in range(3):
                    wb, wbR = load_w(w_br[l, x, j], 8)
                    yps, ypsR = newps()
                    proj_fm(wb, wbR, 8, lambda kc, x=x: (Gblk[:, 8 * x + kc, 0:n], GblkR[8 * x + kc]), yps, ypsR, n)
                    wb2, wb2R = load_w(w_in[l, 80 + 16 * x + j], 16)
                    bps, bpsR = newps()
                    proj_fm(wb2, wb2R, 16, h_rhs(t0, n), bps, bpsR, n)
                    sg, sgR = newtmp()
                    act(sg[:, 0:n], bps[:, 0:n], AF.Sigmoid, [bpsR], [sgR])
                    if x == 0:
                        tt('dve', ua[:, 0:n], yps[:, 0:n], sg[:, 0:n], ALU.mult, [ypsR, sgR], [uaR])
                    else:
                        tt('dve', sg[:, 0:n], yps[:, 0:n], sg[:, 0:n], ALU.mult, [ypsR, sgR], [sgR])
                        if x == 1:
                            tt('pool', ua[:, 0:n], ua[:, 0:n], sg[:, 0:n], ALU.add, [uaR, sgR], [uaR])
                        else:
                            tt('pool', Ublk[:, j, 0:n], ua[:, 0:n], sg[:, 0:n], ALU.add, [uaR, sgR], [UblkR[j]])
            for j in range(16):
                wb, wbR = load_w(w_out[l, j], 16)
                ops_, opsR = newps()
                proj_fm(wb, wbR, 16, lambda kc: (Ublk[:, kc, 0:n], UblkR[kc]), ops_, opsR, n)
                xt, xtR = newtmp()
                dma(xt[:, 0:n], src[b, j, :, t0:t0 + n], [xsR[j]], [xtR])
                stt(xt[:, 0:n], ops_[:, 0:n], mod[:, l, col, 32 + j:33 + j], xt[:, 0:n], ALU.mult, ALU.add,
                    [opsR, xtR, modR], [xtR])
                dma(XS[b, j, :, t0:t0 + n], xt[:, 0:n], [xtR], [xsR[j]])

    xsR = [Res() for _ in range(16)]
    gR = [Res() for _ in range(24)]
    outR = Res()
    for b in range(NB):
        for l in range(LAYERS):
            src = xT if l == 0 else XS
            last = (l == L - 1)
            norm_phase(b, l, src, False)
            if dbg and b == 0 and l == 0:
                for kc in range(16):
                    dma(dbg_h[kc], hT[:, kc, :], [hR[kc]], [outR])
            if stop_after == 'norm':
                break
            mixer_attn(l, 'a', not last)
            mixer_attn(l, 'b', not last)
            mixer_c(l, not last)
            S.barrier()
            if dbg and b == 0 and l == 0:
                dma(dbg_g, G, gR, [outR])
            phase3(b, l, src, last)
            if last:
                pass
            S.barrier()
            if dbg and b == 0 and l == 0:
                dma(dbg_x, XS[0], xsR, [outR])
        if LAYERS == L and stop_after is None:
            norm_phase(b, L - 1, XS, True)
            S.barrier()
    if dbg:
        dma(dbg_mod, mod[:].rearrange('p a b c -> p (a b c)'), [modR], [outR])
    S.barrier()
    S.emit(nc)
    return nc


def _retile(w, kc):
    K, N = w.shape
    return np.ascontiguousarray(w.reshape(kc, 128, N // 128, 128).transpose(2, 1, 0, 3))


def _vec128(v):
    n = v.shape[-1] // 128
    a = v.reshape(v.shape[:-1] + (n, 128))
    return np.ascontiguousarray(np.moveaxis(a, -1, 0))


def host_prep(inp, NB, cores, LW=L):
    f = np.float32
    shared = {}
    ada_w = np.asarray(inp['ada_w'][:LW], f)
    shared['ada_w'] = np.ascontiguousarray(
        ada_w.reshape(LW, 16, 128, 16, 384).transpose(0, 3, 2, 1, 4))
    shared['ada_b'] = np.ascontiguousarray(np.asarray(inp['ada_b'], f).reshape(L, 48, 128).transpose(2, 0, 1))
    shared['norm_g'] = np.ascontiguousarray(np.asarray(inp['norm_g'], f).reshape(L, 16, 128).transpose(2, 0, 1))
    shared['fnorm_g'] = np.ascontiguousarray(np.asarray(inp['final_norm_g'], f).reshape(16, 128).T)
    shared['w_in'] = np.stack([_retile(np.asarray(inp['w_in'][l], f), 16) for l in range(LW)])
    shared['w_br'] = np.stack([np.stack([_retile(np.asarray(inp[k][l], f), 8) for k in
                                         ('w_branch_a', 'w_branch_b', 'w_branch_c')]) for l in range(LW)])
    shared['w_out'] = np.stack([_retile(np.asarray(inp['w_out'][l], f), 16) for l in range(LW)])
    shared['sink'] = np.ascontiguousarray(np.broadcast_to(np.asarray(inp['a_sink'], f)[None], (128, L, 8)))
    qkn = np.stack([np.asarray(inp['b_q_norm'], f), np.asarray(inp['b_k_norm'], f),
                    np.asarray(inp['c_out_norm'], f)], axis=-1)
    shared['qkn'] = np.ascontiguousarray(qkn.transpose(1, 0, 2))
    clb = np.asarray(inp['c_lower_bound'], f).reshape(L, 2, 8, 128)
    shared['clb'] = np.ascontiguousarray(clb.transpose(3, 1, 2, 0))
    row = np.repeat(np.arange(NL // 64, dtype=f), 64)
    colv = np.tile(np.arange(64, dtype=f), NL // 64)
    inv_freq = (np.float32(10000.0) ** (-np.arange(32, dtype=f) / np.float32(32))).astype(f)
    ang = np.concatenate([row[:, None] * inv_freq, colv[:, None] * inv_freq], axis=-1).astype(f)
    ang = np.concatenate([ang, ang], axis=-1)
    cosT = np.cos(ang).astype(f).T
    sinT = np.sin(ang).astype(f).T.copy()
    sinT[64:] *= -1.0
    shared['rope'] = np.ascontiguousarray(np.stack([cosT, sinT], axis=1))
    kk = np.arange(128)[:, None]
    qq = np.arange(128)[None, :]
    same = (kk // 64) == (qq // 64)
    masks = np.stack([(same & (kk <= qq)), (same & (kk >= qq)), (kk >= qq), (kk <= qq), (kk == qq)], axis=1).astype(f)
    BIG = np.float32(1e30)
    ext = np.stack([BIG * masks[:, 0], -BIG * masks[:, 0], BIG * masks[:, 1], -BIG * masks[:, 1]], axis=1).astype(f)
    shared['masks'] = np.ascontiguousarray(np.concatenate([masks, ext], axis=1))
    m64 = np.ones((128, 512), f)
    m64[:, ::64] = 0.0
    shared['m64'] = m64
    maps = []
    for ci in range(cores):
        bs = list(range(ci * NB, (ci + 1) * NB))
        xs = []
        for b_ in bs:
            full = np.concatenate([np.asarray(inp['ctx'][b_], f), np.asarray(inp['x'][b_], f)], axis=0)
            xs.append(np.ascontiguousarray(full.T.reshape(16, 128, T)))
        m = dict(shared)
        m['xT'] = np.stack(xs)
        cc = np.stack([np.asarray(inp['c'][b_], f) for b_ in bs] + [np.asarray(inp['c_ctx'], f)], axis=-1)
        m['cT'] = np.ascontiguousarray(cc.reshape(16, 128, NB + 1).transpose(1, 0, 2))
        maps.append(m)
    return maps


def kernel(**inputs):
    B = inputs['x'].shape[0]
    cores = NCORES
    NB = B // cores
    nc = build(NB)
    maps = host_prep(inputs, NB, cores)
    res = run_bass_kernel_spmd(nc, maps, core_ids=list(range(cores)))
    out = np.empty((B, NL, D), np.float32)
    for ci in range(cores):
        o = res.results[ci]['outT']
        for i in range(NB):
            out[ci * NB + i] = o[i].reshape(D, NL).T
    return out
```

```python
import numpy as np
import concourse.bass as bass
import concourse.mybir as mybir
from concourse.bass_utils import run_bass_kernel_spmd

F32 = mybir.dt.float32
BF16 = mybir.dt.bfloat16
AF = mybir.ActivationFunctionType
ALU = mybir.AluOpType

L = 4
D = 2048
T = 2304
MC = 256
NL = 2048
BLK = [(0, 256), (256, 512), (768, 512), (1280, 512), (1792, 512)]
EPS = 1e-6
ENGS = ['pe', 'act', 'dve', 'pool', 'sp']
NCORES = 4


class Res:
    __slots__ = ('w', 'r')

    def __init__(self):
        self.w = None
        self.r = {}


class Sched:
    NDMA = 8

    def __init__(self):
        self.ops = {e: [] for e in ENGS}
        self.clock = {e: {} for e in ENGS}
        self.dma_n = 0
        self.dma_cnt = [0] * self.NDMA

    @staticmethod
    def _kv(tok):
        if tok[0] == 'dma':
            return ('dma', tok[1]), tok[2]
        return tok[0], tok[1]

    def op(self, eng, fn, reads=(), writes=(), dma=False):
        idx = len(self.ops[eng])
        deps = []
        for t in reads:
            if t.w is not None:
                deps.append((t.w, True))
        for t in writes:
            if t.w is not None:
                deps.append((t.w, False))
            for k, v in t.r.items():
                deps.append(((('dma', k[1], v) if k[0] == 'dma' else (k, v)), False))
        clk = self.clock[eng]
        if dma:
            slot = self.dma_n % self.NDMA
            self.dma_n += 1
            if self.dma_cnt[slot] > 0:
                deps.append((('dma', slot, self.dma_cnt[slot]), False))
            self.dma_cnt[slot] += 1
            mytok = ('dma', slot, self.dma_cnt[slot])
        else:
            mytok = (eng, idx)
        best = {}
        for tok, raw in deps:
            key, val = self._kv(tok)
            if key == eng and (eng in ('pe', 'sp') or not raw):
                continue
            if clk.get(key, -1) >= val:
                continue
            clk[key] = val
            best[key] = tok
            if tok[0] != 'dma':
                o = self.ops[tok[0]][tok[1]]
                o['sig'] = True
                for k2, v2 in o['clk'].items():
                    if clk.get(k2, -1) < v2:
                        clk[k2] = v2
        snap = dict(clk)
        if not dma:
            snap[eng] = idx
        rec = dict(fn=fn, waits=list(best.values()), sig=False, clk=snap, tok=mytok, dma=dma)
        self.ops[eng].append(rec)
        mk, mv = self._kv(mytok)
        for t in reads:
            if t.r.get(mk, -1) < mv:
                t.r[mk] = mv
        for t in writes:
            t.w = mytok
            t.r = {}
        return mytok

    def barrier(self):
        rs = []
        for e in ENGS:
            if e != 'sp' and self.ops[e]:
                r = Res()
                r.w = (e, len(self.ops[e]) - 1)
                rs.append(r)
        for s in range(self.NDMA):
            if self.dma_cnt[s] > 0:
                r = Res()
                r.w = ('dma', s, self.dma_cnt[s])
                rs.append(r)
        for e in ENGS:
            self.op(e, lambda g: g.nop(), reads=rs)

    def emit(self, nc):
        sems = {e: nc.alloc_semaphore('s_' + e) for e in ENGS if e != 'sp'}
        dsems = [nc.alloc_semaphore('d_%d' % i) for i in range(self.NDMA)]
        cnt = {}
        for e in ENGS:
            c = 0
            arr = []
            for o in self.ops[e]:
                if o['sig'] and not o['dma']:
                    c += 1
                arr.append(c)
            cnt[e] = arr

        def emit_engine(ename, eng):
            for o in self.ops[ename]:
                for tok in o['waits']:
                    if tok[0] == 'dma':
                        eng.wait_ge(dsems[tok[1]], 16 * tok[2])
                    else:
                        eng.wait_ge(sems[tok[0]], cnt[tok[0]][tok[1]])
                ins = o['fn'](eng)
                if o['dma']:
                    ins.then_inc(dsems[o['tok'][1]], 16)
                elif o['sig']:
                    ins.then_inc(sems[ename], 1)

        with nc.Block() as block:
            @block.tensor
            def _(e):
                emit_engine('pe', e)

            @block.scalar
            def _(e):
                emit_engine('act', e)

            @block.vector
            def _(e):
                emit_engine('dve', e)

            @block.gpsimd
            def _(e):
                emit_engine('pool', e)

            @block.sync
            def _(e):
                emit_engine('sp', e)


def build(NB, LAYERS=L, dbg=False, stop_after=None):
    nc = bass.Bass('TRN2', target_bir_lowering=False, dynamic_dma_scratch_size=2048)
    S = Sched()

    def din(name, shape, dt=F32):
        return nc.dram_tensor(name, list(shape), dt, kind='ExternalInput').ap()

    xT = din('xT', [NB, 16, 128, T])
    cT = din('cT', [128, 16, NB + 1])
    ada_w = din('ada_w', [LAYERS, 16, 128, 16, 384])
    ada_b = din('ada_b', [128, L, 48])
    norm_g = din('norm_g', [128, L, 16])
    fnorm_g = din('fnorm_g', [128, 16])
    w_in = din('w_in', [LAYERS, 128, 128, 16, 128])
    w_br = din('w_br', [LAYERS, 3, 16, 128, 8, 128])
    w_out = din('w_out', [LAYERS, 16, 128, 16, 128])
    sink_in = din('sink', [128, L, 8])
    qkn_in = din('qkn', [128, L, 3])
    clb_in = din('clb', [128, 2, 8, L])
    rope_in = din('rope', [128, 2, NL])
    masks_in = din('masks', [128, 9, 128])
    m64_in = din('m64', [128, 512])
    outT = nc.dram_tensor('outT', [NB, 16, 128, NL], F32, kind='ExternalOutput').ap()
    XS = nc.dram_tensor('XS', [NB, 16, 128, T], F32).ap()
    G = nc.dram_tensor('G', [24, 128, T], BF16).ap()
    if dbg:
        dbg_h = nc.dram_tensor('dbg_h', [16, 128, T], BF16, kind='ExternalOutput').ap()
        dbg_g = nc.dram_tensor('dbg_g', [24, 128, T], BF16, kind='ExternalOutput').ap()
        dbg_x = nc.dram_tensor('dbg_x', [16, 128, T], F32, kind='ExternalOutput').ap()
        dbg_mod = nc.dram_tensor('dbg_mod', [128, L * (NB + 1) * 48], F32, kind='ExternalOutput').ap()

    def sb(name, shape, dt=F32):
        return nc.alloc_sbuf_tensor('sb_' + name, list(shape), dt)

    hT = sb('hT', [128, 16, T], BF16)
    hR = [Res() for _ in range(16)]
    arena = sb('arena', [128, 26624], BF16)
    wst = [sb('wst%d' % i, [128, 16, 128]) for i in range(2)]
    wstR = [Res() for _ in range(2)]
    wbf = [sb('wbf%d' % i, [128, 16, 128], BF16) for i in range(3)]
    wbfR = [Res() for _ in range(3)]
    rope = sb('rope', [128, 2, NL])
    ropeR = Res()
    NTMP = 8
    tmp = [sb('tmp%d' % i, [128, 512]) for i in range(NTMP)]
    tmpR = [Res() for _ in range(NTMP)]
    mod = sb('mod', [128, L, NB + 1, 48])
    modR = Res()
    gp = sb('gp', [128, 2, 16])
    gpR = Res()
    ng = sb('ng', [128, L, 16])
    fg = sb('fg', [128, 16])
    adab = sb('adab', [128, L, 48])
    cs = sb('cs', [128, 16, NB + 1])
    csig = sb('csig', [128, 16, NB + 1])
    sinkE = sb('sinkE', [128, L, 8])
    qkn = sb('qkn_s', [128, L, 3])
    clb = sb('clb_s', [128, 2, 8, L])
    lbt = sb('lbt', [128, 2, 8, L])
    oml = sb('oml', [128, 2, 8, L])
    noml = sb('noml', [128, 2, 8, L])
    lbtmp = sb('lbtmp', [128, 2, 8])
    masks = sb('masks_s', [128, 9, 128])
    m64 = sb('m64_s', [128, 512])
    identb = sb('identb', [128, 128], BF16)
    onesb = sb('onesb', [128, 128], BF16)
    scal = sb('scal', [128, 2, 3, 36])
    scalR = Res()
    stmp = sb('stmp', [128, 16])
    stmpR = Res()
    Sst = sb('Sst', [128, 128])
    SstR = Res()
    Sbf = [sb('Sbf%d' % i, [128, 128], BF16) for i in range(2)]
    SbfR = [Res(), Res()]
    aTs = [sb('aTs%d' % i, [128, 128], BF16) for i in range(2)]
    aTsR = [Res(), Res()]
    pT = [sb('pT%d' % i, [128, 512], BF16) for i in range(3)]
    pTR = [Res() for _ in range(3)]
    gout = [sb('gout%d' % i, [128, 512], BF16) for i in range(2)]
    goutR = [Res(), Res()]
    sqb = [sb('sqb%d' % i, [128, 512], BF16) for i in range(2)]
    sqbR = [Res(), Res()]
    constR = Res()
    rsT = [sb('rsT%d' % i, [128, 512]) for i in range(2)]
    rsTR = [Res(), Res()]
    zsT = [sb('zsT%d' % i, [128, 512]) for i in range(2)]
    zsTR = [Res(), Res()]

    PS = [nc.alloc_psum_tensor('ps%d' % i, [128, 512], F32) for i in range(7)]
    PSR = [Res() for _ in range(7)]
    PSB = nc.alloc_psum_tensor('psb', [128, 512], BF16)
    PSBR = Res()

    def av(off, shape, dt=BF16):
        n = 1
        for s_ in shape:
            n *= s_
        if dt == F32:
            a = arena[:, off:off + 2 * n].bitcast(F32)
        else:
            a = arena[:, off:off + n]
        if len(shape) == 2:
            return a.rearrange('p (a b) -> p a b', b=shape[1])
        return a

    A_ = [av(i * T, [T]) for i in range(4)]
    AR = [Res() for _ in range(4)]
    V_ = [av(4 * T + i * T, [18, 128]) for i in range(3)]
    VR = [Res() for _ in range(3)]
    QF = av(7 * T, [T], F32)
    QFR = Res()
    OACC = av(9 * T, [T], F32)
    OACCR = Res()
    assert 11 * T <= 26624
    Gblk = av(0, [24, 512])
    GblkR = [Res() for _ in range(24)]
    Ublk = av(24 * 512, [16, 512])
    UblkR = [Res() for _ in range(16)]
    ADA = [av(i * 12288, [16, 384], F32) for i in range(2)]
    ADAR = [Res(), Res()]

    cnt = {'w': 0, 't': 0, 'ps': 0, 'pt': 0, 'g': 0, 'sq': 0, 'rs': 0, 'zs': 0}

    def newtmp():
        i = cnt['t'] % NTMP
        cnt['t'] += 1
        return tmp[i], tmpR[i]

    def newps():
        i = cnt['ps'] % 2
        cnt['ps'] += 1
        return PS[i], PSR[i]

    def dma(out, in_, reads, writes, q='sp'):
        S.op(q, lambda e, o=out, i=in_: e.dma_start(out=o, in_=i), reads=reads, writes=writes, dma=True)

    deferred = []

    def flush_stores(keep=0):
        while len(deferred) > keep:
            o, i, r, w = deferred.pop(0)
            dma(o, i, r, w, q='act')

    def store(out, in_, reads, writes):
        deferred.append((out, in_, reads, writes))
        flush_stores(keep=1)

    def bar():
        flush_stores()
        S.barrier()

    def mm(out, lhsT, rhs, start, stop, reads, writes):
        S.op('pe', lambda e, o=out, a=lhsT, b=rhs, s0=start, s1=stop: e.matmul(o, a, b, start=s0, stop=s1),
             reads=reads, writes=writes)

    def act(out, in_, func, reads, writes, bias=None, scale=None):
        kw = {}
        if bias is not None:
            kw['bias'] = bias
        if scale is not None:
            kw['scale'] = scale
        S.op('act', lambda e, o=out, i=in_, f=func, k=kw: e.activation(o, i, f, **k), reads=reads, writes=writes)

    def tt(eng, out, in0, in1, op, reads, writes):
        S.op(eng, lambda e, o=out, a=in0, b=in1, p=op: e.tensor_tensor(o, a, b, p), reads=reads, writes=writes)

    def ts(eng, out, in0, s1, s2, op0, op1, reads, writes):
        if op1 is None:
            S.op(eng, lambda e, o=out, a=in0, x=s1, p0=op0: e.tensor_scalar(o, a, x, None, p0),
                 reads=reads, writes=writes)
        else:
            S.op(eng, lambda e, o=out, a=in0, x=s1, y=s2, p0=op0, p1=op1: e.tensor_scalar(o, a, x, y, p0, p1),
                 reads=reads, writes=writes)

    def stt(out, in0, scalar, in1, op0, op1, reads, writes):
        S.op('dve', lambda e, o=out, a=in0, s_=scalar, b=in1, p0=op0, p1=op1:
             e.scalar_tensor_tensor(o, a, s_, b, p0, p1), reads=reads, writes=writes)

    def cp(eng, out, in_, reads, writes):
        if eng == 'act':
            S.op(eng, lambda e, o=out, i=in_: e.activation(o, i, AF.Identity), reads=reads, writes=writes)
        else:
            S.op(eng, lambda e, o=out, i=in_: e.tensor_copy(o, i), reads=reads, writes=writes)

    def load_w(src, nkc):
        i = cnt['w']
        cnt['w'] += 1
        st, stR = wst[i % 2], wstR[i % 2]
        wb, wbR = wbf[i % 3], wbfR[i % 3]
        dma(st[:, 0:nkc, :], src, [], [stR])
        cp('dve', wb[:, 0:nkc, :], st[:, 0:nkc, :], [stR], [wbR])
        return wb, wbR

    def proj_fm(wb, wbR, nkc, rhs_fn, ps, psR, n):
        for kc in range(nkc):
            r_ap, r_res = rhs_fn(kc)
            mm(ps[:, 0:n], wb[:, kc, :], r_ap, kc == 0, kc == nkc - 1, [wbR, r_res], [psR])

    def h_rhs(t0, n):
        return lambda kc: (hT[:, kc, t0:t0 + n], hR[kc])

    def rstd_from_ps(ps, psR, n, inv_n):
        t1, t1R = newtmp()
        act(t1[:, 0:n], ps[:, 0:n], AF.Ln, [psR, constR], [t1R], bias=epsT[:, 0:1], scale=inv_n)
        i = cnt['rs'] % 2
        cnt['rs'] += 1
        t2, t2R = rsT[i], rsTR[i]
        act(t2[:, 0:n], t1[:, 0:n], AF.Exp, [t1R], [t2R], scale=-0.5)
        return t2, t2R

    def colsumsq(src_ap, srcR, ps, psR, n, first, last):
        i = cnt['sq'] % 2
        cnt['sq'] += 1
        act(sqb[i][:, 0:n], src_ap, AF.Square, [srcR], [sqbR[i]])
        mm(ps[:, 0:n], onesb[:, :], sqb[i][:, 0:n], first, last, [sqbR[i], constR], [psR])

    epsT = sb('epsT', [128, 1])
    S.op('dve', lambda e: e.memset(epsT[:], EPS), writes=[constR])
    S.op('dve', lambda e: e.memset(onesb[:], 1.0), writes=[constR])
    ldR = Res()
    for dst, src in [(ng[:], norm_g), (fg[:], fnorm_g), (adab[:], ada_b), (cs[:], cT), (sinkE[:], sink_in),
                     (qkn[:], qkn_in), (clb[:], clb_in), (rope[:], rope_in), (masks[:], masks_in), (m64[:], m64_in)]:
        dma(dst, src, [], [ldR])
    cp('dve', identb[:], masks[:, 4, :], [ldR], [constR])
    act(csig[:], cs[:], AF.Sigmoid, [ldR], [constR])
    tt('dve', cs[:], cs[:], csig[:], ALU.mult, [constR, ldR], [constR])
    act(sinkE[:], sinkE[:], AF.Exp, [ldR], [constR])
    act(clb[:], clb[:], AF.Exp, [ldR], [constR])
    S.op('dve', lambda e: e.tensor_reduce(lbtmp[:], clb[:], mybir.AxisListType.X, ALU.add), reads=[constR], writes=[constR])
    S.op('dve', lambda e: e.reciprocal(lbtmp[:], lbtmp[:]), reads=[constR], writes=[constR])
    S.op('dve', lambda e: e.memset(lbt[:], 0.0), writes=[constR])
    for l in range(1, L):
        tt('dve', clb[:, :, :, l], clb[:, :, :, l], lbtmp[:], ALU.mult, [constR], [constR])
        tt('dve', lbt[:, :, :, l], lbt[:, :, :, l - 1], clb[:, :, :, l], ALU.add, [constR], [constR])
    ts('dve', oml[:], lbt[:], -1.0, 1.0, ALU.mult, ALU.add, [constR], [constR])
    ts('dve', noml[:], oml[:], -1.0, None, ALU.mult, None, [constR], [constR])

    nslab = 0
    for l in range(LAYERS):
        for s_ in range(16):
            a, aR = ADA[nslab % 2], ADAR[nslab % 2]
            nslab += 1
            dma(a, ada_w[l, s_], [], [aR])
            for jj in range(3):
                j = 3 * s_ + jj
                ps, psR = newps()
                for kc in range(16):
                    mm(ps[:, 0:NB + 1], a[:, kc, jj * 128:(jj + 1) * 128], cs[:, kc, :], kc == 0, kc == 15,
                       [aR, constR], [psR])
                ts('dve', mod[:, l, :, j], ps[:, 0:NB + 1], adab[:, l, j:j + 1], None, ALU.add, None,
                   [psR, ldR], [modR])
    bar()

    def norm_phase(b, l, src, final):
        if not final:
            for ci, col in enumerate((b, NB)):
                ts('dve', gp[:, ci, :], mod[:, l, col, 16:32], 1.0, None, ALU.add, None, [modR], [gpR])
                tt('dve', gp[:, ci, :], gp[:, ci, :], ng[:, l, :], ALU.mult, [gpR, ldR], [gpR])
        for bi, (t0, n) in enumerate(BLK):
            if final and bi == 0:
                continue
            ps, psR = newps()
            for kc in range(16):
                xt, xtR = newtmp()
                dma(xt[:, 0:n], src[b, kc, :, t0:t0 + n], [xsR[kc]], [xtR])
                colsumsq(xt[:, 0:n], xtR, ps, psR, n, kc == 0, kc == 15)
            rs, rsR = rstd_from_ps(ps, psR, n, 1.0 / D)
            ci = 1 if bi == 0 else 0
            col = NB if bi == 0 else b
            dbgn = dbg and b == 0 and l == 0 and bi == 1 and not final
            if dbgn:
                d1 = nc.dram_tensor('dbg_rs', [128, 512], F32, kind='ExternalOutput').ap()
                dma(d1, rs[:, :], [rsR], [outR])
                d3 = nc.dram_tensor('dbg_gp', [128, 32], F32, kind='ExternalOutput').ap()
                dma(d3, gp[:].rearrange('p a b -> p (a b)'), [gpR], [outR])
                d4 = nc.dram_tensor('dbg_ng', [128, 64], F32, kind='ExternalOutput').ap()
                dma(d4, ng[:].rearrange('p a b -> p (a b)'), [ldR], [outR])
            for kc in range(16):
                xt, xtR = newtmp()
                dma(xt[:, 0:n], src[b, kc, :, t0:t0 + n], [xsR[kc]], [xtR])
                if final:
                    stt(xt[:, 0:n], xt[:, 0:n], fg[:, kc:kc + 1], rs[:, 0:n], ALU.mult, ALU.mult,
                        [xtR, rsR, ldR], [xtR])
                    store(outT[b, kc, :, t0 - MC:t0 - MC + n], xt[:, 0:n], [xtR], [outR])
                    if kc == 15:
                        flush_stores()
                else:
                    if dbgn and kc == 0:
                        d5 = nc.dram_tensor('dbg_x0', [128, 512], F32, kind='ExternalOutput').ap()
                        dma(d5, xt[:, :], [xtR], [outR])
                    tt('dve', xt[:, 0:n], xt[:, 0:n], rs[:, 0:n], ALU.mult, [xtR, rsR], [xtR])
                    if dbgn and kc == 0:
                        d6 = nc.dram_tensor('dbg_x1', [128, 512], F32, kind='ExternalOutput').ap()
                        dma(d6, xt[:, :], [xtR], [outR])
                    act(hT[:, kc, t0:t0 + n], xt[:, 0:n], AF.Identity, [xtR, gpR, modR], [hR[kc]],
                        bias=mod[:, l, col, kc:kc + 1], scale=gp[:, ci, kc:kc + 1])

    def rope_apply(src, srcR, dst_ap, dstR, t0, n):
        p0 = t0 - MC
        t1, t1R = newtmp()
        tt('pool', t1[:, 0:n], src[:, 0:n], rope[:, 0, p0:p0 + n], ALU.mult, [srcR, ldR], [t1R])
        t2, t2R = newtmp()
        tt('dve', t2[0:64, 0:n], src[64:128, 0:n], rope[64:128, 1, p0:p0 + n], ALU.mult, [srcR, ldR], [t2R])
        tt('dve', t2[64:128, 0:n], src[0:64, 0:n], rope[0:64, 1, p0:p0 + n], ALU.mult, [srcR, ldR], [t2R])
        tt('pool', dst_ap, t1[:, 0:n], t2[:, 0:n], ALU.add, [t1R, t2R], [dstR])

    def qk_proj(l, j, dstA, dstR, norm_col):
        wb, wbR = load_w(w_in[l, j], 16)
        for bi, (t0, n) in enumerate(BLK):
            ps, psR = newps()
            proj_fm(wb, wbR, 16, h_rhs(t0, n), ps, psR, n)
            q, qR = newtmp()
            if norm_col is None:
                cp('act', q[:, 0:n], ps[:, 0:n], [psR], [qR])
            else:
                cp('act', q[:, 0:n], ps[:, 0:n], [psR], [qR])
                ps2, ps2R = PS[6], PSR[6]
                colsumsq(q[:, 0:n], qR, ps2, ps2R, n, True, True)
                rs, rsR = rstd_from_ps(ps2, ps2R, n, 1.0 / 128)
                stt(q[:, 0:n], q[:, 0:n], qkn[:, l, norm_col:norm_col + 1], rs[:, 0:n], ALU.mult, ALU.mult,
                    [qR, rsR, ldR], [qR])
            if bi == 0:
                cp('pool', dstA[:, t0:t0 + n], q[:, 0:n], [qR], [dstR])
            else:
                rope_apply(q, qR, dstA[:, t0:t0 + n], dstR, t0, n)

    def v_proj(l, j, dstV, dstR):
        wb, wbR = load_w(w_in[l, j], 16)
        for grp in range(5):
            tts = list(range(grp * 4, min(18, grp * 4 + 4)))
            ps, psR = newps()
            for ii, tti in enumerate(tts):
                for kc in range(16):
                    mm(ps[:, ii * 128:(ii + 1) * 128], hT[:, kc, tti * 128:(tti + 1) * 128], wb[:, kc, :],
                       kc == 0, kc == 15, [wbR, hR[kc]], [psR])
            nn = len(tts) * 128
            cp('act', dstV[:, tts[0]:tts[0] + len(tts), :], ps[:, 0:nn].rearrange('p (a b) -> p a b', b=128),
               [psR], [dstR])

    def gate_zs(l, jz, t0, n):
        wb, wbR = load_w(w_in[l, jz], 16)
        ps, psR = newps()
        proj_fm(wb, wbR, 16, h_rhs(t0, n), ps, psR, n)
        i = cnt['zs'] % 2
        cnt['zs'] += 1
        sg, sgR = zsT[i], zsTR[i]
        act(sg[:, 0:n], ps[:, 0:n], AF.Sigmoid, [psR], [sgR])
        tt('dve', sg[:, 0:n], ps[:, 0:n], sg[:, 0:n], ALU.mult, [psR, sgR], [sgR])
        return sg, sgR

    def store_g(src, srcR, n, zs, zsR, zoff, gchunk, t0):
        i = cnt['g'] % 2
        cnt['g'] += 1
        tt('pool', gout[i][:, 0:n], src[:, 0:n], zs[:, zoff:zoff + n], ALU.mult, [srcR, zsR], [goutR[i]])
        store(G[gchunk, :, t0:t0 + n], gout[i][:, 0:n], [goutR[i]], [gR[gchunk]])

    def attend(KT, KR, VT, VTR, QT, QR, q0, qn, klist, sink_ap, zs, zsR, zoff, gchunk):
        oT, oTR = PS[4], PSR[4]
        dn, dnR = PS[5], PSR[5]
        nk = len(klist)
        for idx, (kt, mk) in enumerate(klist):
            sT, sTR = PS[2 + idx % 2], PSR[2 + idx % 2]
            mm(sT[:, 0:qn], KT[:, kt * 128:(kt + 1) * 128], QT[:, q0:q0 + qn], True, True, [KR, QR], [sTR])
            pi = cnt['pt'] % 3
            cnt['pt'] += 1
            act(pT[pi][:, 0:qn], sT[:, 0:qn], AF.Exp, [sTR], [pTR[pi]], scale=128 ** -0.5)
            if mk is not None:
                tt('pool', pT[pi][:, 0:qn], pT[pi][:, 0:qn], masks[:, mk, 0:qn], ALU.mult, [pTR[pi], ldR], [pTR[pi]])
            mm(oT[:, 0:qn], VT[:, kt, :], pT[pi][:, 0:qn], idx == 0, idx == nk - 1, [VTR, pTR[pi]], [oTR])
            mm(dn[:, 0:qn], onesb[:, :], pT[pi][:, 0:qn], idx == 0, idx == nk - 1, [constR, pTR[pi]], [dnR])
        r, rR = newtmp()
        if sink_ap is not None:
            ts('dve', r[:, 0:qn], dn[:, 0:qn], sink_ap, None, ALU.add, None, [dnR, constR], [rR])
            S.op('dve', lambda e, a=r[:, 0:qn]: e.reciprocal(a, a), reads=[rR], writes=[rR])
        else:
            S.op('dve', lambda e, a=r[:, 0:qn], d=dn[:, 0:qn]: e.reciprocal(a, d), reads=[dnR], writes=[rR])
        tt('dve', r[:, 0:qn], oT[:, 0:qn], r[:, 0:qn], ALU.mult, [oTR, rR], [rR])
        store_g(r, rR, qn, zs, zsR, zoff, gchunk, q0)

    def mixer_attn(l, which, with_ctx):
        base = 0 if which == 'a' else 20
        for g in range(2):
            KT, KR = A_[1], AR[1]
            VT, VTR = V_[0], VR[0]
            qk_proj(l, base + 8 + g, KT, KR, None if which == 'a' else 1)
            v_proj(l, base + 10 + g, VT, VTR)
            for i in range(4):
                h = 4 * g + i
                QT, QR = A_[0], AR[0]
                qk_proj(l, base + h, QT, QR, None if which == 'a' else 0)
                sink_ap = sinkE[:, l, h:h + 1] if which == 'a' else None
                gchunk = (0 if which == 'a' else 8) + h
                for bi, (t0, n) in enumerate(BLK):
                    if bi == 0 and not with_ctx:
                        continue
                    zs, zsR = gate_zs(l, base + 12 + h, t0, n)
                    if bi == 0:
                        attend(KT, KR, VT, VTR, QT, QR, 0, 256, [(0, None), (1, None)], sink_ap, zs, zsR, 0, gchunk)
                    elif which == 'b':
                        attend(KT, KR, VT, VTR, QT, QR, t0, n, [(k, None) for k in range(18)], sink_ap,
                               zs, zsR, 0, gchunk)
                    else:
                        for sbk in range(4):
                            nblk = (t0 - MC) // 128 + sbk
                            kl = [(0, None), (1, None)]
                            if nblk > 0:
                                kl.append((2 + nblk - 1, 2))
                            kl.append((2 + nblk, None))
                            if nblk < 15:
                                kl.append((2 + nblk + 1, 3))
                            attend(KT, KR, VT, VTR, QT, QR, t0 + sbk * 128, 128, kl, sink_ap, zs, zsR,
                                   sbk * 128, gchunk)

    def mixer_c(l, with_ctx):
        for h in range(8):
            wb, wbR = load_w(w_in[l, 40 + h], 16)
            for (t0, n) in BLK:
                ps, psR = newps()
                proj_fm(wb, wbR, 16, h_rhs(t0, n), ps, psR, n)
                act(QF[:, t0:t0 + n], ps[:, 0:n], AF.Identity, [psR], [QFR], scale=128 ** -0.5)
            v_proj(l, 64 + h, V_[0], VR[0])
            for d in range(2):
                qt_, qtR = A_[2 * d], AR[2 * d]
                kt_, ktR = A_[2 * d + 1], AR[2 * d + 1]
                wb, wbR = load_w(w_in[l, 48 + 8 * d + h], 16)
                lb_ap = lbt[:, d, h, l:l + 1]
                oml_ap = oml[:, d, h, l:l + 1]
                noml_ap = noml[:, d, h, l:l + 1]
                ref = 31 if d == 0 else 32
                for (t0, n) in BLK:
                    nch = n // 64
                    c0 = t0 // 64
                    ps, psR = newps()
                    proj_fm(wb, wbR, 16, h_rhs(t0, n), ps, psR, n)
                    r, rR = newtmp()
                    act(r[:, 0:n], ps[:, 0:n], AF.Sigmoid, [psR], [rR])
                    lf, lfR = newtmp()
                    act(lf[:, 0:n], r[:, 0:n], AF.Ln, [rR, constR], [lfR], bias=lb_ap, scale=oml_ap)
                    kk, kkR = newtmp()
                    ts('dve', kk[:, 0:n], r[:, 0:n], noml_ap, oml_ap, ALU.mult, ALU.add, [rR, constR], [kkR])
                    cm, cmR = newtmp()
                    S.op('dve', lambda e, o=cm[:, 0:n], a=m64[:, 0:n], b_=lf[:, 0:n]:
                         e.tensor_tensor_scan(o, a, b_, 0.0, ALU.mult, ALU.add), reads=[ldR, lfR], writes=[cmR])
                    cm3 = cm[:, 0:n].rearrange('p (c k) -> p c k', k=64)
                    al = scal[:, d, 0, c0:c0 + nch]
                    be = scal[:, d, 1, c0:c0 + nch]
                    ga = scal[:, d, 2, c0:c0 + nch]
                    if d == 0:
                        C3 = cm3
                        CR = cmR
                        act(al, cm3[:, :, ref], AF.Exp, [cmR], [scalR])
                        act(be, cm3[:, :, 63], AF.Exp, [cmR], [scalR])
                        tt('dve', stmp[:, 0:nch], cm3[:, :, 63], cm3[:, :, ref], ALU.subtract, [cmR], [stmpR])
                        act(ga, stmp[:, 0:nch], AF.Exp, [stmpR], [scalR])
                    else:
                        ec, ecR = newtmp()
                        tt('pool', ec[:, 0:n], cm[:, 0:n], lf[:, 0:n], ALU.subtract, [cmR, lfR], [ecR])
                        C3 = ec[:, 0:n].rearrange('p (c k) -> p c k', k=64)
                        CR = ecR
                        act(be, cm3[:, :, 63], AF.Exp, [cmR], [scalR])
                        act(ga, C3[:, :, ref], AF.Exp, [ecR], [scalR])
                        tt('dve', stmp[:, 0:nch], cm3[:, :, 63], C3[:, :, ref], ALU.subtract, [cmR, ecR], [stmpR])
                        act(al, stmp[:, 0:nch], AF.Exp, [stmpR], [scalR])
                    aa, aaR = newtmp()
                    aa3 = aa[:, 0:n].rearrange('p (c k) -> p c k', k=64)
                    tt('dve', aa3, C3, C3[:, :, ref:ref + 1].broadcast_to([128, nch, 64]), ALU.subtract, [CR], [aaR])
                    wq, wqR = newtmp()
                    act(wq[:, 0:n], aa[:, 0:n], AF.Exp, [aaR], [wqR], scale=(1.0 if d == 0 else -1.0))
                    tt('pool', qt_[:, t0:t0 + n], QF[:, t0:t0 + n], wq[:, 0:n], ALU.mult, [QFR, wqR], [qtR])
                    wk, wkR = newtmp()
                    act(wk[:, 0:n], aa[:, 0:n], AF.Exp, [aaR], [wkR], scale=(-1.0 if d == 0 else 1.0))
                    tt('pool', kt_[:, t0:t0 + n], kk[:, 0:n], wk[:, 0:n], ALU.mult, [kkR, wkR], [ktR])
                ktok, ktokR = V_[1 + d], VR[1 + d]
                for grp in range(5):
                    tts = list(range(grp * 4, min(18, grp * 4 + 4)))
                    for ii, tti in enumerate(tts):
                        S.op('pe', lambda e, o=PSB[:, ii * 128:(ii + 1) * 128], a=kt_[:, tti * 128:(tti + 1) * 128]:
                             e.transpose(o, a, identb[:, :]), reads=[ktR, constR], writes=[PSBR])
                    nn = len(tts) * 128
                    cp('act', ktok[:, tts[0]:tts[0] + len(tts), :], PSB[:, 0:nn].rearrange('p (a b) -> p a b', b=128),
                       [PSBR], [ktokR])
            for d in range(2):
                qt_, qtR = A_[2 * d], AR[2 * d]
                kt_, ktR = A_[2 * d + 1], AR[2 * d + 1]
                ktok, ktokR = V_[1 + d], VR[1 + d]
                vtok, vtokR = V_[0], VR[0]
                order = list(range(18)) if d == 0 else [1, 0] + list(range(17, 1, -1))
                S.op('dve', lambda e: e.memset(Sst[:], 0.0), writes=[SstR])
                for pi_, pp in enumerate(order):
                    aps, apsR = PS[2 + pi_ % 2], PSR[2 + pi_ % 2]
                    tk = slice(pp * 128, (pp + 1) * 128)
                    mm(aps[:, 0:128], kt_[:, tk], qt_[:, tk], True, True, [ktR, qtR], [apsR])
                    ai = pi_ % 2
                    mt, mtR = newtmp()
                    tt('dve', mt[:, 0:128], aps[:, 0:128], masks[:, 5 + 2 * d, :], ALU.min, [apsR, ldR], [mtR])
                    tt('dve', aTs[ai][:, :], mt[:, 0:128], masks[:, 6 + 2 * d, :], ALU.max, [mtR, ldR], [aTsR[ai]])
                    ops_, opsR = PS[4], PSR[4]
                    mm(ops_[:, 0:128], vtok[:, pp, :], aTs[ai][:, :], True, False, [vtokR, aTsR[ai]], [opsR])
                    halves = (0, 1) if d == 0 else (1, 0)
                    for hi, hf in enumerate(halves):
                        c = 2 * pp + hf
                        tc_ = slice(pp * 128 + hf * 64, pp * 128 + hf * 64 + 64)
                        si = hi
                        act(Sbf[si][:, :], Sst[:, :], AF.Identity, [SstR, scalR], [SbfR[si]], scale=scal[:, d, 0, c:c + 1])
                        mm(ops_[:, hf * 64:hf * 64 + 64], Sbf[si][:, :], qt_[:, tc_], False, hi == 1,
                           [SbfR[si], qtR], [opsR])
                        ups, upsR = PS[5 + hi % 2], PSR[5 + hi % 2]
                        mm(ups[:, 0:128], ktok[hf * 64:hf * 64 + 64, pp, :], vtok[hf * 64:hf * 64 + 64, pp, :],
                           True, True, [ktokR, vtokR], [upsR])
                        ts('dve', Sst[:, :], Sst[:, :], scal[:, d, 1, c:c + 1], None, ALU.mult, None,
                           [SstR, scalR], [SstR])
                        stt(Sst[:, :], ups[:, 0:128], scal[:, d, 2, c:c + 1], Sst[:, :], ALU.mult, ALU.add,
                            [upsR, SstR, scalR], [SstR])
                    if d == 0:
                        cp('act', OACC[:, tk], ops_[:, 0:128], [opsR], [OACCR])
                    else:
                        tt('dve', OACC[:, tk], ops_[:, 0:128], OACC[:, tk], ALU.add, [opsR, OACCR], [OACCR])
            for bi, (t0, n) in enumerate(BLK):
                if bi == 0 and not with_ctx:
                    continue
                zs, zsR = gate_zs(l, 72 + h, t0, n)
                ps2, ps2R = PS[6], PSR[6]
                colsumsq(OACC[:, t0:t0 + n], OACCR, ps2, ps2R, n, True, True)
                rs, rsR = rstd_from_ps(ps2, ps2R, n, 1.0 / 128)
                o, oR = newtmp()
                stt(o[:, 0:n], OACC[:, t0:t0 + n], qkn[:, l, 2:3], rs[:, 0:n], ALU.mult, ALU.mult,
                    [OACCR, rsR, ldR], [oR])
                store_g(o, oR, n, zs, zsR, 0, 16 + h, t0)

    def phase3(b, l, src, last):
        for bi, (t0, n) in enumerate(BLK):
            if last and bi == 0:
                continue
            for c in range(24):
                dma(Gblk[:, c, 0:n], G[c, :, t0:t0 + n], [gR[c]], [GblkR[c]])
            col = NB if bi == 0 else b
            for j in range(16):
                ua, uaR = newtmp()
                for x in range(3):
                    wb, wbR = load_w(w_br[l, x, j], 8)
                    yps, ypsR = newps()
                    proj_fm(wb, wbR, 8, lambda kc, x=x: (Gblk[:, 8 * x + kc, 0:n], GblkR[8 * x + kc]), yps, ypsR, n)
                    wb2, wb2R = load_w(w_in[l, 80 + 16 * x + j], 16)
                    bps, bpsR = newps()
                    proj_fm(wb2, wb2R, 16, h_rhs(t0, n), bps, bpsR, n)
                    sg, sgR = newtmp()
                    act(sg[:, 0:n], bps[:, 0:n], AF.Sigmoid, [bpsR], [sgR])
                    if x == 0:
                        tt('dve', ua[:, 0:n], yps[:, 0:n], sg[:, 0:n], ALU.mult, [ypsR, sgR], [uaR])
                    else:
                        tt('dve', sg[:, 0:n], yps[:, 0:n], sg[:, 0:n], ALU.mult, [ypsR, sgR], [sgR])
                        if x == 1:
                            tt('pool', ua[:, 0:n], ua[:, 0:n], sg[:, 0:n], ALU.add, [uaR, sgR], [uaR])
                        else:
                            tt('pool', Ublk[:, j, 0:n], ua[:, 0:n], sg[:, 0:n], ALU.add, [uaR, sgR], [UblkR[j]])
            for j in range(16):
                wb, wbR = load_w(w_out[l, j], 16)
                ops_, opsR = newps()
                proj_fm(wb, wbR, 16, lambda kc: (Ublk[:, kc, 0:n], UblkR[kc]), ops_, opsR, n)
                xt, xtR = newtmp()
                dma(xt[:, 0:n], src[b, j, :, t0:t0 + n], [xsR[j]], [xtR])
                stt(xt[:, 0:n], ops_[:, 0:n], mod[:, l, col, 32 + j:33 + j], xt[:, 0:n], ALU.mult, ALU.add,
                    [opsR, xtR, modR], [xtR])
                store(XS[b, j, :, t0:t0 + n], xt[:, 0:n], [xtR], [xsR[j]])
            flush_stores()

    xsR = [Res() for _ in range(16)]
    gR = [Res() for _ in range(24)]
    outR = Res()
    for b in range(NB):
        for l in range(LAYERS):
            src = xT if l == 0 else XS
            last = (l == L - 1)
            norm_phase(b, l, src, False)
            if dbg and b == 0 and l == 0:
                for kc in range(16):
                    dma(dbg_h[kc], hT[:, kc, :], [hR[kc]], [outR])
            if stop_after == 'norm':
                break
            mixer_attn(l, 'a', not last)
            mixer_attn(l, 'b', not last)
            mixer_c(l, not last)
            bar()
            if dbg and b == 0 and l == 0:
                dma(dbg_g, G, gR, [outR])
            phase3(b, l, src, last)
            if last:
                pass
            bar()
            if dbg and b == 0 and l == 0:
                dma(dbg_x, XS[0], xsR, [outR])
        if LAYERS == L and stop_after is None:
            norm_phase(b, L - 1, XS, True)
            bar()
    if dbg:
        dma(dbg_mod, mod[:].rearrange('p a b c -> p (a b c)'), [modR], [outR])
    bar()
    S.emit(nc)
    build.sbuf_left = nc.sbuf_bytes_remaining
    return nc


def _retile(w, kc):
    K, N = w.shape
    return np.ascontiguousarray(w.reshape(kc, 128, N // 128, 128).transpose(2, 1, 0, 3))


def _vec128(v):
    n = v.shape[-1] // 128
    a = v.reshape(v.shape[:-1] + (n, 128))
    return np.ascontiguousarray(np.moveaxis(a, -1, 0))


def host_prep(inp, NB, cores, LW=L):
    f = np.float32
    shared = {}
    ada_w = np.asarray(inp['ada_w'][:LW], f)
    shared['ada_w'] = np.ascontiguousarray(
        ada_w.reshape(LW, 16, 128, 16, 384).transpose(0, 3, 2, 1, 4))
    shared['ada_b'] = np.ascontiguousarray(np.asarray(inp['ada_b'], f).reshape(L, 48, 128).transpose(2, 0, 1))
    shared['norm_g'] = np.ascontiguousarray(np.asarray(inp['norm_g'], f).reshape(L, 16, 128).transpose(2, 0, 1))
    shared['fnorm_g'] = np.ascontiguousarray(np.asarray(inp['final_norm_g'], f).reshape(16, 128).T)
    shared['w_in'] = np.stack([_retile(np.asarray(inp['w_in'][l], f), 16) for l in range(LW)])
    shared['w_br'] = np.stack([np.stack([_retile(np.asarray(inp[k][l], f), 8) for k in
                                         ('w_branch_a', 'w_branch_b', 'w_branch_c')]) for l in range(LW)])
    shared['w_out'] = np.stack([_retile(np.asarray(inp['w_out'][l], f), 16) for l in range(LW)])
    shared['sink'] = np.ascontiguousarray(np.broadcast_to(np.asarray(inp['a_sink'], f)[None], (128, L, 8)))
    qkn = np.stack([np.asarray(inp['b_q_norm'], f), np.asarray(inp['b_k_norm'], f),
                    np.asarray(inp['c_out_norm'], f)], axis=-1)
    shared['qkn'] = np.ascontiguousarray(qkn.transpose(1, 0, 2))
    clb = np.asarray(inp['c_lower_bound'], f).reshape(L, 2, 8, 128)
    shared['clb'] = np.ascontiguousarray(clb.transpose(3, 1, 2, 0))
    row = np.repeat(np.arange(NL // 64, dtype=f), 64)
    colv = np.tile(np.arange(64, dtype=f), NL // 64)
    inv_freq = (np.float32(10000.0) ** (-np.arange(32, dtype=f) / np.float32(32))).astype(f)
    ang = np.concatenate([row[:, None] * inv_freq, colv[:, None] * inv_freq], axis=-1).astype(f)
    ang = np.concatenate([ang, ang], axis=-1)
    cosT = np.cos(ang).astype(f).T
    sinT = np.sin(ang).astype(f).T.copy()
    sinT[64:] *= -1.0
    shared['rope'] = np.ascontiguousarray(np.stack([cosT, sinT], axis=1))
    kk = np.arange(128)[:, None]
    qq = np.arange(128)[None, :]
    same = (kk // 64) == (qq // 64)
    masks = np.stack([(same & (kk <= qq)), (same & (kk >= qq)), (kk >= qq), (kk <= qq), (kk == qq)], axis=1).astype(f)
    BIG = np.float32(1e30)
    ext = np.stack([BIG * masks[:, 0], -BIG * masks[:, 0], BIG * masks[:, 1], -BIG * masks[:, 1]], axis=1).astype(f)
    shared['masks'] = np.ascontiguousarray(np.concatenate([masks, ext], axis=1))
    m64 = np.ones((128, 512), f)
    m64[:, ::64] = 0.0
    shared['m64'] = m64
    maps = []
    for ci in range(cores):
        bs = list(range(ci * NB, (ci + 1) * NB))
        xs = []
        for b_ in bs:
            full = np.concatenate([np.asarray(inp['ctx'][b_], f), np.asarray(inp['x'][b_], f)], axis=0)
            xs.append(np.ascontiguousarray(full.T.reshape(16, 128, T)))
        m = dict(shared)
        m['xT'] = np.stack(xs)
        cc = np.stack([np.asarray(inp['c'][b_], f) for b_ in bs] + [np.asarray(inp['c_ctx'], f)], axis=-1)
        m['cT'] = np.ascontiguousarray(cc.reshape(16, 128, NB + 1).transpose(1, 0, 2))
        maps.append(m)
    return maps


def kernel(**inputs):
    B = inputs['x'].shape[0]
    cores = NCORES
    NB = B // cores
    nc = build(NB)
    maps = host_prep(inputs, NB, cores)
    res = run_bass_kernel_spmd(nc, maps, core_ids=list(range(cores)))
    out = np.empty((B, NL, D), np.float32)
    for ci in range(cores):
        o = res.results[ci]['outT']
        for i in range(NB):
            out[ci * NB + i] = o[i].reshape(D, NL).T
    return out
```

```python
import numpy as np
import concourse.bass as bass
import concourse.mybir as mybir
from concourse.bass_utils import run_bass_kernel_spmd

F32 = mybir.dt.float32
BF16 = mybir.dt.bfloat16
AF = mybir.ActivationFunctionType
ALU = mybir.AluOpType

L = 4
D = 2048
T = 2304
MC = 256
NL = 2048
BLK = [(0, 256), (256, 512), (768, 512), (1280, 512), (1792, 512)]
EPS = 1e-6
ENGS = ['pe', 'act', 'dve', 'pool', 'sp']
NCORES = 8


class Res:
    __slots__ = ('w', 'r')

    def __init__(self):
        self.w = None
        self.r = {}


class Sched:
    NDMA = 8

    def __init__(self):
        self.ops = {e: [] for e in ENGS}
        self.clock = {e: {} for e in ENGS}
        self.dma_n = 0
        self.dma_cnt = [0] * self.NDMA

    @staticmethod
    def _kv(tok):
        if tok[0] == 'dma':
            return ('dma', tok[1]), tok[2]
        return tok[0], tok[1]

    def op(self, eng, fn, reads=(), writes=(), dma=False):
        idx = len(self.ops[eng])
        deps = []
        for t in reads:
            if t.w is not None:
                deps.append((t.w, True))
        for t in writes:
            if t.w is not None:
                deps.append((t.w, False))
            for k, v in t.r.items():
                deps.append(((('dma', k[1], v) if k[0] == 'dma' else (k, v)), False))
        clk = self.clock[eng]
        if dma:
            slot = self.dma_n % self.NDMA
            self.dma_n += 1
            if self.dma_cnt[slot] > 0:
                deps.append((('dma', slot, self.dma_cnt[slot]), False))
            self.dma_cnt[slot] += 1
            mytok = ('dma', slot, self.dma_cnt[slot])
        else:
            mytok = (eng, idx)
        best = {}
        for tok, raw in deps:
            key, val = self._kv(tok)
            if key == eng and (eng in ('pe', 'sp') or not raw):
                continue
            if clk.get(key, -1) >= val:
                continue
            clk[key] = val
            best[key] = tok
            if tok[0] != 'dma':
                o = self.ops[tok[0]][tok[1]]
                o['sig'] = True
                for k2, v2 in o['clk'].items():
                    if clk.get(k2, -1) < v2:
                        clk[k2] = v2
        snap = dict(clk)
        if not dma:
            snap[eng] = idx
        rec = dict(fn=fn, waits=list(best.values()), sig=False, clk=snap, tok=mytok, dma=dma)
        self.ops[eng].append(rec)
        mk, mv = self._kv(mytok)
        for t in reads:
            if t.r.get(mk, -1) < mv:
                t.r[mk] = mv
        for t in writes:
            t.w = mytok
            t.r = {}
        return mytok

    def barrier(self):
        rs = []
        for e in ENGS:
            if e != 'sp' and self.ops[e]:
                r = Res()
                r.w = (e, len(self.ops[e]) - 1)
                rs.append(r)
        for s in range(self.NDMA):
            if self.dma_cnt[s] > 0:
                r = Res()
                r.w = ('dma', s, self.dma_cnt[s])
                rs.append(r)
        for e in ENGS:
            self.op(e, lambda g: g.nop(), reads=rs)

    def emit(self, nc):
        sems = {e: nc.alloc_semaphore('s_' + e) for e in ENGS if e != 'sp'}
        dsems = [nc.alloc_semaphore('d_%d' % i) for i in range(self.NDMA)]
        cnt = {}
        for e in ENGS:
            c = 0
            arr = []
            for o in self.ops[e]:
                if o['sig'] and not o['dma']:
                    c += 1
                arr.append(c)
            cnt[e] = arr

        def emit_engine(ename, eng):
            for o in self.ops[ename]:
                for tok in o['waits']:
                    if tok[0] == 'dma':
                        eng.wait_ge(dsems[tok[1]], 16 * tok[2])
                    else:
                        eng.wait_ge(sems[tok[0]], cnt[tok[0]][tok[1]])
                ins = o['fn'](eng)
                if o['dma']:
                    ins.then_inc(dsems[o['tok'][1]], 16)
                elif o['sig']:
                    ins.then_inc(sems[ename], 1)

        with nc.Block() as block:
            @block.tensor
            def _(e):
                emit_engine('pe', e)

            @block.scalar
            def _(e):
                emit_engine('act', e)

            @block.vector
            def _(e):
                emit_engine('dve', e)

            @block.gpsimd
            def _(e):
                emit_engine('pool', e)

            @block.sync
            def _(e):
                emit_engine('sp', e)


def build(NB, LAYERS=L, dbg=False, stop_after=None):
    nc = bass.Bass('TRN2', target_bir_lowering=False, dynamic_dma_scratch_size=2048)
    S = Sched()

    def din(name, shape, dt=F32):
        return nc.dram_tensor(name, list(shape), dt, kind='ExternalInput').ap()

    xT = din('xT', [NB, 16, 128, T])
    cT = din('cT', [128, 16, NB + 1])
    ada_w = din('ada_w', [LAYERS, 16, 128, 16, 384])
    ada_b = din('ada_b', [128, L, 48])
    norm_g = din('norm_g', [128, L, 16])
    fnorm_g = din('fnorm_g', [128, 16])
    w_in = din('w_in', [LAYERS, 128, 128, 16, 128])
    w_br = din('w_br', [LAYERS, 3, 16, 128, 8, 128])
    w_out = din('w_out', [LAYERS, 16, 128, 16, 128])
    sink_in = din('sink', [128, L, 8])
    qkn_in = din('qkn', [128, L, 3])
    clb_in = din('clb', [128, 2, 8, L])
    rope_in = din('rope', [128, 2, NL])
    masks_in = din('masks', [128, 9, 128])
    m64_in = din('m64', [128, 512])
    outT = nc.dram_tensor('outT', [NB, 16, 128, NL], F32, kind='ExternalOutput').ap()
    XS = nc.dram_tensor('XS', [NB, 16, 128, T], F32).ap()
    G = nc.dram_tensor('G', [24, 128, T], BF16).ap()
    if dbg:
        dbg_h = nc.dram_tensor('dbg_h', [16, 128, T], BF16, kind='ExternalOutput').ap()
        dbg_g = nc.dram_tensor('dbg_g', [24, 128, T], BF16, kind='ExternalOutput').ap()
        dbg_x = nc.dram_tensor('dbg_x', [16, 128, T], F32, kind='ExternalOutput').ap()
        dbg_mod = nc.dram_tensor('dbg_mod', [128, L * (NB + 1) * 48], F32, kind='ExternalOutput').ap()

    def sb(name, shape, dt=F32):
        return nc.alloc_sbuf_tensor('sb_' + name, list(shape), dt)

    hT = sb('hT', [128, 16, T], BF16)
    hR = [Res() for _ in range(16)]
    arena = sb('arena', [128, 26624], BF16)
    wst = [sb('wst%d' % i, [128, 16, 128]) for i in range(2)]
    wstR = [Res() for _ in range(2)]
    wbf = [sb('wbf%d' % i, [128, 16, 128], BF16) for i in range(3)]
    wbfR = [Res() for _ in range(3)]
    rope = sb('rope', [128, 2, NL])
    ropeR = Res()
    NTMP = 8
    tmp = [sb('tmp%d' % i, [128, 512]) for i in range(NTMP)]
    tmpR = [Res() for _ in range(NTMP)]
    mod = sb('mod', [128, L, NB + 1, 48])
    modR = Res()
    gp = sb('gp', [128, 2, 16])
    gpR = Res()
    ng = sb('ng', [128, L, 16])
    fg = sb('fg', [128, 16])
    adab = sb('adab', [128, L, 48])
    cs = sb('cs', [128, 16, NB + 1])
    csig = sb('csig', [128, 16, NB + 1])
    sinkE = sb('sinkE', [128, L, 8])
    qkn = sb('qkn_s', [128, L, 3])
    clb = sb('clb_s', [128, 2, 8, L])
    lbt = sb('lbt', [128, 2, 8, L])
    oml = sb('oml', [128, 2, 8, L])
    noml = sb('noml', [128, 2, 8, L])
    lbtmp = sb('lbtmp', [128, 2, 8])
    masks = sb('masks_s', [128, 9, 128])
    m64 = sb('m64_s', [128, 512])
    identb = sb('identb', [128, 128], BF16)
    onesb = sb('onesb', [128, 128], BF16)
    scal = sb('scal', [128, 2, 3, 36])
    scalR = Res()
    stmp = sb('stmp', [128, 16])
    stmpR = Res()
    Sst = sb('Sst', [128, 128])
    SstR = Res()
    Sbf = [sb('Sbf%d' % i, [128, 128], BF16) for i in range(2)]
    SbfR = [Res(), Res()]
    aTs = [sb('aTs%d' % i, [128, 128], BF16) for i in range(2)]
    aTsR = [Res(), Res()]
    pT = [sb('pT%d' % i, [128, 512], BF16) for i in range(3)]
    pTR = [Res() for _ in range(3)]
    gout = [sb('gout%d' % i, [128, 512], BF16) for i in range(2)]
    goutR = [Res(), Res()]
    sqb = [sb('sqb%d' % i, [128, 512], BF16) for i in range(2)]
    sqbR = [Res(), Res()]
    constR = Res()
    rsT = [sb('rsT%d' % i, [128, 512]) for i in range(2)]
    rsTR = [Res(), Res()]
    zsT = [sb('zsT%d' % i, [128, 512]) for i in range(2)]
    zsTR = [Res(), Res()]

    PS = [nc.alloc_psum_tensor('ps%d' % i, [128, 512], F32) for i in range(7)]
    PSR = [Res() for _ in range(7)]
    PSB = nc.alloc_psum_tensor('psb', [128, 512], BF16)
    PSBR = Res()

    def av(off, shape, dt=BF16):
        n = 1
        for s_ in shape:
            n *= s_
        if dt == F32:
            a = arena[:, off:off + 2 * n].bitcast(F32)
        else:
            a = arena[:, off:off + n]
        if len(shape) == 2:
            return a.rearrange('p (a b) -> p a b', b=shape[1])
        return a

    A_ = [av(i * T, [T]) for i in range(4)]
    AR = [Res() for _ in range(4)]
    V_ = [av(4 * T + i * T, [18, 128]) for i in range(3)]
    VR = [Res() for _ in range(3)]
    QF = av(7 * T, [T], F32)
    QFR = Res()
    OACC = av(9 * T, [T], F32)
    OACCR = Res()
    assert 11 * T <= 26624
    Gblk = av(0, [24, 512])
    GblkR = [Res() for _ in range(24)]
    Ublk = av(24 * 512, [16, 512])
    UblkR = [Res() for _ in range(16)]
    ADA = [av(i * 12288, [16, 384], F32) for i in range(2)]
    ADAR = [Res(), Res()]

    cnt = {'w': 0, 't': 0, 'ps': 0, 'pt': 0, 'g': 0, 'sq': 0, 'rs': 0, 'zs': 0}

    def newtmp():
        i = cnt['t'] % NTMP
        cnt['t'] += 1
        return tmp[i], tmpR[i]

    def newps():
        i = cnt['ps'] % 2
        cnt['ps'] += 1
        return PS[i], PSR[i]

    def dma(out, in_, reads, writes, q='sp'):
        S.op(q, lambda e, o=out, i=in_: e.dma_start(out=o, in_=i), reads=reads, writes=writes, dma=True)

    deferred = []

    def flush_stores(keep=0):
        while len(deferred) > keep:
            o, i, r, w = deferred.pop(0)
            dma(o, i, r, w, q='act')

    def store(out, in_, reads, writes):
        deferred.append((out, in_, reads, writes))
        flush_stores(keep=1)

    def bar():
        flush_stores()
        S.barrier()

    def mm(out, lhsT, rhs, start, stop, reads, writes):
        S.op('pe', lambda e, o=out, a=lhsT, b=rhs, s0=start, s1=stop: e.matmul(o, a, b, start=s0, stop=s1),
             reads=reads, writes=writes)

    def act(out, in_, func, reads, writes, bias=None, scale=None):
        kw = {}
        if bias is not None:
            kw['bias'] = bias
        if scale is not None:
            kw['scale'] = scale
        S.op('act', lambda e, o=out, i=in_, f=func, k=kw: e.activation(o, i, f, **k), reads=reads, writes=writes)

    def tt(eng, out, in0, in1, op, reads, writes):
        S.op(eng, lambda e, o=out, a=in0, b=in1, p=op: e.tensor_tensor(o, a, b, p), reads=reads, writes=writes)

    def ts(eng, out, in0, s1, s2, op0, op1, reads, writes):
        if op1 is None:
            S.op(eng, lambda e, o=out, a=in0, x=s1, p0=op0: e.tensor_scalar(o, a, x, None, p0),
                 reads=reads, writes=writes)
        else:
            S.op(eng, lambda e, o=out, a=in0, x=s1, y=s2, p0=op0, p1=op1: e.tensor_scalar(o, a, x, y, p0, p1),
                 reads=reads, writes=writes)

    def stt(out, in0, scalar, in1, op0, op1, reads, writes):
        S.op('dve', lambda e, o=out, a=in0, s_=scalar, b=in1, p0=op0, p1=op1:
             e.scalar_tensor_tensor(o, a, s_, b, p0, p1), reads=reads, writes=writes)

    def cp(eng, out, in_, reads, writes):
        if eng == 'act':
            S.op(eng, lambda e, o=out, i=in_: e.activation(o, i, AF.Identity), reads=reads, writes=writes)
        else:
            S.op(eng, lambda e, o=out, i=in_: e.tensor_copy(o, i), reads=reads, writes=writes)

    def load_w(src, nkc):
        i = cnt['w']
        cnt['w'] += 1
        st, stR = wst[i % 2], wstR[i % 2]
        wb, wbR = wbf[i % 3], wbfR[i % 3]
        dma(st[:, 0:nkc, :], src, [], [stR])
        cp('dve', wb[:, 0:nkc, :], st[:, 0:nkc, :], [stR], [wbR])
        return wb, wbR

    def proj_fm(wb, wbR, nkc, rhs_fn, ps, psR, n):
        for kc in range(nkc):
            r_ap, r_res = rhs_fn(kc)
            mm(ps[:, 0:n], wb[:, kc, :], r_ap, kc == 0, kc == nkc - 1, [wbR, r_res], [psR])

    def h_rhs(t0, n):
        return lambda kc: (hT[:, kc, t0:t0 + n], hR[kc])

    def rstd_from_ps(ps, psR, n, inv_n):
        t1, t1R = newtmp()
        act(t1[:, 0:n], ps[:, 0:n], AF.Ln, [psR, constR], [t1R], bias=epsT[:, 0:1], scale=inv_n)
        i = cnt['rs'] % 2
        cnt['rs'] += 1
        t2, t2R = rsT[i], rsTR[i]
        act(t2[:, 0:n], t1[:, 0:n], AF.Exp, [t1R], [t2R], scale=-0.5)
        return t2, t2R

    def colsumsq(src_ap, srcR, ps, psR, n, first, last):
        i = cnt['sq'] % 2
        cnt['sq'] += 1
        act(sqb[i][:, 0:n], src_ap, AF.Square, [srcR], [sqbR[i]])
        mm(ps[:, 0:n], onesb[:, :], sqb[i][:, 0:n], first, last, [sqbR[i], constR], [psR])

    epsT = sb('epsT', [128, 1])
    S.op('dve', lambda e: e.memset(epsT[:], EPS), writes=[constR])
    S.op('dve', lambda e: e.memset(onesb[:], 1.0), writes=[constR])
    ldR = Res()
    for dst, src in [(ng[:], norm_g), (fg[:], fnorm_g), (adab[:], ada_b), (cs[:], cT), (sinkE[:], sink_in),
                     (qkn[:], qkn_in), (clb[:], clb_in), (rope[:], rope_in), (masks[:], masks_in), (m64[:], m64_in)]:
        dma(dst, src, [], [ldR])
    cp('dve', identb[:], masks[:, 4, :], [ldR], [constR])
    act(csig[:], cs[:], AF.Sigmoid, [ldR], [constR])
    tt('dve', cs[:], cs[:], csig[:], ALU.mult, [constR, ldR], [constR])
    act(sinkE[:], sinkE[:], AF.Exp, [ldR], [constR])
    act(clb[:], clb[:], AF.Exp, [ldR], [constR])
    S.op('dve', lambda e: e.tensor_reduce(lbtmp[:], clb[:], mybir.AxisListType.X, ALU.add), reads=[constR], writes=[constR])
    S.op('dve', lambda e: e.reciprocal(lbtmp[:], lbtmp[:]), reads=[constR], writes=[constR])
    S.op('dve', lambda e: e.memset(lbt[:], 0.0), writes=[constR])
    for l in range(1, L):
        tt('dve', clb[:, :, :, l], clb[:, :, :, l], lbtmp[:], ALU.mult, [constR], [constR])
        tt('dve', lbt[:, :, :, l], lbt[:, :, :, l - 1], clb[:, :, :, l], ALU.add, [constR], [constR])
    ts('dve', oml[:], lbt[:], -1.0, 1.0, ALU.mult, ALU.add, [constR], [constR])
    ts('dve', noml[:], oml[:], -1.0, None, ALU.mult, None, [constR], [constR])

    nslab = 0
    for l in range(LAYERS):
        for s_ in range(16):
            a, aR = ADA[nslab % 2], ADAR[nslab % 2]
            nslab += 1
            dma(a, ada_w[l, s_], [], [aR])
            for jj in range(3):
                j = 3 * s_ + jj
                ps, psR = newps()
                for kc in range(16):
                    mm(ps[:, 0:NB + 1], a[:, kc, jj * 128:(jj + 1) * 128], cs[:, kc, :], kc == 0, kc == 15,
                       [aR, constR], [psR])
                ts('dve', mod[:, l, :, j], ps[:, 0:NB + 1], adab[:, l, j:j + 1], None, ALU.add, None,
                   [psR, ldR], [modR])
    bar()

    def norm_phase(b, l, src, final):
        if not final:
            for ci, col in enumerate((b, NB)):
                ts('dve', gp[:, ci, :], mod[:, l, col, 16:32], 1.0, None, ALU.add, None, [modR], [gpR])
                tt('dve', gp[:, ci, :], gp[:, ci, :], ng[:, l, :], ALU.mult, [gpR, ldR], [gpR])
        for bi, (t0, n) in enumerate(BLK):
            if final and bi == 0:
                continue
            ps, psR = newps()
            for kc in range(16):
                xt, xtR = newtmp()
                dma(xt[:, 0:n], src[b, kc, :, t0:t0 + n], [xsR[kc]], [xtR])
                colsumsq(xt[:, 0:n], xtR, ps, psR, n, kc == 0, kc == 15)
            rs, rsR = rstd_from_ps(ps, psR, n, 1.0 / D)
            ci = 1 if bi == 0 else 0
            col = NB if bi == 0 else b
            dbgn = dbg and b == 0 and l == 0 and bi == 1 and not final
            if dbgn:
                d1 = nc.dram_tensor('dbg_rs', [128, 512], F32, kind='ExternalOutput').ap()
                dma(d1, rs[:, :], [rsR], [outR])
                d3 = nc.dram_tensor('dbg_gp', [128, 32], F32, kind='ExternalOutput').ap()
                dma(d3, gp[:].rearrange('p a b -> p (a b)'), [gpR], [outR])
                d4 = nc.dram_tensor('dbg_ng', [128, 64], F32, kind='ExternalOutput').ap()
                dma(d4, ng[:].rearrange('p a b -> p (a b)'), [ldR], [outR])
            for kc in range(16):
                xt, xtR = newtmp()
                dma(xt[:, 0:n], src[b, kc, :, t0:t0 + n], [xsR[kc]], [xtR])
                if final:
                    stt(xt[:, 0:n], xt[:, 0:n], fg[:, kc:kc + 1], rs[:, 0:n], ALU.mult, ALU.mult,
                        [xtR, rsR, ldR], [xtR])
                    store(outT[b, kc, :, t0 - MC:t0 - MC + n], xt[:, 0:n], [xtR], [outR])
                    if kc == 15:
                        flush_stores()
                else:
                    if dbgn and kc == 0:
                        d5 = nc.dram_tensor('dbg_x0', [128, 512], F32, kind='ExternalOutput').ap()
                        dma(d5, xt[:, :], [xtR], [outR])
                    tt('dve', xt[:, 0:n], xt[:, 0:n], rs[:, 0:n], ALU.mult, [xtR, rsR], [xtR])
                    if dbgn and kc == 0:
                        d6 = nc.dram_tensor('dbg_x1', [128, 512], F32, kind='ExternalOutput').ap()
                        dma(d6, xt[:, :], [xtR], [outR])
                    act(hT[:, kc, t0:t0 + n], xt[:, 0:n], AF.Identity, [xtR, gpR, modR], [hR[kc]],
                        bias=mod[:, l, col, kc:kc + 1], scale=gp[:, ci, kc:kc + 1])

    def rope_apply(src, srcR, dst_ap, dstR, t0, n):
        p0 = t0 - MC
        t1, t1R = newtmp()
        tt('pool', t1[:, 0:n], src[:, 0:n], rope[:, 0, p0:p0 + n], ALU.mult, [srcR, ldR], [t1R])
        t2, t2R = newtmp()
        tt('dve', t2[0:64, 0:n], src[64:128, 0:n], rope[64:128, 1, p0:p0 + n], ALU.mult, [srcR, ldR], [t2R])
        tt('dve', t2[64:128, 0:n], src[0:64, 0:n], rope[0:64, 1, p0:p0 + n], ALU.mult, [srcR, ldR], [t2R])
        tt('pool', dst_ap, t1[:, 0:n], t2[:, 0:n], ALU.add, [t1R, t2R], [dstR])

    def qk_proj(l, j, dstA, dstR, norm_col):
        wb, wbR = load_w(w_in[l, j], 16)
        for bi, (t0, n) in enumerate(BLK):
            ps, psR = newps()
            proj_fm(wb, wbR, 16, h_rhs(t0, n), ps, psR, n)
            q, qR = newtmp()
            if norm_col is None:
                cp('act', q[:, 0:n], ps[:, 0:n], [psR], [qR])
            else:
                cp('act', q[:, 0:n], ps[:, 0:n], [psR], [qR])
                ps2, ps2R = PS[6], PSR[6]
                colsumsq(q[:, 0:n], qR, ps2, ps2R, n, True, True)
                rs, rsR = rstd_from_ps(ps2, ps2R, n, 1.0 / 128)
                stt(q[:, 0:n], q[:, 0:n], qkn[:, l, norm_col:norm_col + 1], rs[:, 0:n], ALU.mult, ALU.mult,
                    [qR, rsR, ldR], [qR])
            if bi == 0:
                cp('pool', dstA[:, t0:t0 + n], q[:, 0:n], [qR], [dstR])
            else:
                rope_apply(q, qR, dstA[:, t0:t0 + n], dstR, t0, n)

    def v_proj(l, j, dstV, dstR):
        wb, wbR = load_w(w_in[l, j], 16)
        for grp in range(5):
            tts = list(range(grp * 4, min(18, grp * 4 + 4)))
            ps, psR = newps()
            for ii, tti in enumerate(tts):
                for kc in range(16):
                    mm(ps[:, ii * 128:(ii + 1) * 128], hT[:, kc, tti * 128:(tti + 1) * 128], wb[:, kc, :],
                       kc == 0, kc == 15, [wbR, hR[kc]], [psR])
            nn = len(tts) * 128
            cp('act', dstV[:, tts[0]:tts[0] + len(tts), :], ps[:, 0:nn].rearrange('p (a b) -> p a b', b=128),
               [psR], [dstR])

    def gate_zs(l, jz, t0, n):
        wb, wbR = load_w(w_in[l, jz], 16)
        ps, psR = newps()
        proj_fm(wb, wbR, 16, h_rhs(t0, n), ps, psR, n)
        i = cnt['zs'] % 2
        cnt['zs'] += 1
        sg, sgR = zsT[i], zsTR[i]
        act(sg[:, 0:n], ps[:, 0:n], AF.Sigmoid, [psR], [sgR])
        tt('dve', sg[:, 0:n], ps[:, 0:n], sg[:, 0:n], ALU.mult, [psR, sgR], [sgR])
        return sg, sgR

    def store_g(src, srcR, n, zs, zsR, zoff, gchunk, t0):
        i = cnt['g'] % 2
        cnt['g'] += 1
        tt('pool', gout[i][:, 0:n], src[:, 0:n], zs[:, zoff:zoff + n], ALU.mult, [srcR, zsR], [goutR[i]])
        store(G[gchunk, :, t0:t0 + n], gout[i][:, 0:n], [goutR[i]], [gR[gchunk]])

    def attend(KT, KR, VT, VTR, QT, QR, q0, qn, klist, sink_ap, zs, zsR, zoff, gchunk):
        oT, oTR = PS[4], PSR[4]
        dn, dnR = PS[5], PSR[5]
        nk = len(klist)
        for idx, (kt, mk) in enumerate(klist):
            sT, sTR = PS[2 + idx % 2], PSR[2 + idx % 2]
            mm(sT[:, 0:qn], KT[:, kt * 128:(kt + 1) * 128], QT[:, q0:q0 + qn], True, True, [KR, QR], [sTR])
            pi = cnt['pt'] % 3
            cnt['pt'] += 1
            act(pT[pi][:, 0:qn], sT[:, 0:qn], AF.Exp, [sTR], [pTR[pi]], scale=128 ** -0.5)
            if mk is not None:
                tt('pool', pT[pi][:, 0:qn], pT[pi][:, 0:qn], masks[:, mk, 0:qn], ALU.mult, [pTR[pi], ldR], [pTR[pi]])
            mm(oT[:, 0:qn], VT[:, kt, :], pT[pi][:, 0:qn], idx == 0, idx == nk - 1, [VTR, pTR[pi]], [oTR])
            mm(dn[:, 0:qn], onesb[:, :], pT[pi][:, 0:qn], idx == 0, idx == nk - 1, [constR, pTR[pi]], [dnR])
        r, rR = newtmp()
        if sink_ap is not None:
            ts('dve', r[:, 0:qn], dn[:, 0:qn], sink_ap, None, ALU.add, None, [dnR, constR], [rR])
            S.op('dve', lambda e, a=r[:, 0:qn]: e.reciprocal(a, a), reads=[rR], writes=[rR])
        else:
            S.op('dve', lambda e, a=r[:, 0:qn], d=dn[:, 0:qn]: e.reciprocal(a, d), reads=[dnR], writes=[rR])
        tt('dve', r[:, 0:qn], oT[:, 0:qn], r[:, 0:qn], ALU.mult, [oTR, rR], [rR])
        store_g(r, rR, qn, zs, zsR, zoff, gchunk, q0)

    def mixer_attn(l, which, with_ctx):
        base = 0 if which == 'a' else 20
        for g in range(2):
            KT, KR = A_[1], AR[1]
            VT, VTR = V_[0], VR[0]
            qk_proj(l, base + 8 + g, KT, KR, None if which == 'a' else 1)
            v_proj(l, base + 10 + g, VT, VTR)
            for i in range(4):
                h = 4 * g + i
                QT, QR = A_[0], AR[0]
                qk_proj(l, base + h, QT, QR, None if which == 'a' else 0)
                sink_ap = sinkE[:, l, h:h + 1] if which == 'a' else None
                gchunk = (0 if which == 'a' else 8) + h
                for bi, (t0, n) in enumerate(BLK):
                    if bi == 0 and not with_ctx:
                        continue
                    zs, zsR = gate_zs(l, base + 12 + h, t0, n)
                    if bi == 0:
                        attend(KT, KR, VT, VTR, QT, QR, 0, 256, [(0, None), (1, None)], sink_ap, zs, zsR, 0, gchunk)
                    elif which == 'b':
                        attend(KT, KR, VT, VTR, QT, QR, t0, n, [(k, None) for k in range(18)], sink_ap,
                               zs, zsR, 0, gchunk)
                    else:
                        for sbk in range(4):
                            nblk = (t0 - MC) // 128 + sbk
                            kl = [(0, None), (1, None)]
                            if nblk > 0:
                                kl.append((2 + nblk - 1, 2))
                            kl.append((2 + nblk, None))
                            if nblk < 15:
                                kl.append((2 + nblk + 1, 3))
                            attend(KT, KR, VT, VTR, QT, QR, t0 + sbk * 128, 128, kl, sink_ap, zs, zsR,
                                   sbk * 128, gchunk)

    def mixer_c(l, with_ctx):
        for h in range(8):
            wb, wbR = load_w(w_in[l, 40 + h], 16)
            for (t0, n) in BLK:
                ps, psR = newps()
                proj_fm(wb, wbR, 16, h_rhs(t0, n), ps, psR, n)
                act(QF[:, t0:t0 + n], ps[:, 0:n], AF.Identity, [psR], [QFR], scale=128 ** -0.5)
            v_proj(l, 64 + h, V_[0], VR[0])
            for d in range(2):
                qt_, qtR = A_[2 * d], AR[2 * d]
                kt_, ktR = A_[2 * d + 1], AR[2 * d + 1]
                wb, wbR = load_w(w_in[l, 48 + 8 * d + h], 16)
                lb_ap = lbt[:, d, h, l:l + 1]
                oml_ap = oml[:, d, h, l:l + 1]
                noml_ap = noml[:, d, h, l:l + 1]
                ref = 31 if d == 0 else 32
                for (t0, n) in BLK:
                    nch = n // 64
                    c0 = t0 // 64
                    ps, psR = newps()
                    proj_fm(wb, wbR, 16, h_rhs(t0, n), ps, psR, n)
                    r, rR = newtmp()
                    act(r[:, 0:n], ps[:, 0:n], AF.Sigmoid, [psR], [rR])
                    lf, lfR = newtmp()
                    act(lf[:, 0:n], r[:, 0:n], AF.Ln, [rR, constR], [lfR], bias=lb_ap, scale=oml_ap)
                    kk, kkR = newtmp()
                    ts('dve', kk[:, 0:n], r[:, 0:n], noml_ap, oml_ap, ALU.mult, ALU.add, [rR, constR], [kkR])
                    cm, cmR = newtmp()
                    S.op('dve', lambda e, o=cm[:, 0:n], a=m64[:, 0:n], b_=lf[:, 0:n]:
                         e.tensor_tensor_scan(o, a, b_, 0.0, ALU.mult, ALU.add), reads=[ldR, lfR], writes=[cmR])
                    cm3 = cm[:, 0:n].rearrange('p (c k) -> p c k', k=64)
                    al = scal[:, d, 0, c0:c0 + nch]
                    be = scal[:, d, 1, c0:c0 + nch]
                    ga = scal[:, d, 2, c0:c0 + nch]
                    if d == 0:
                        C3 = cm3
                        CR = cmR
                        act(al, cm3[:, :, ref], AF.Exp, [cmR], [scalR])
                        act(be, cm3[:, :, 63], AF.Exp, [cmR], [scalR])
                        tt('dve', stmp[:, 0:nch], cm3[:, :, 63], cm3[:, :, ref], ALU.subtract, [cmR], [stmpR])
                        act(ga, stmp[:, 0:nch], AF.Exp, [stmpR], [scalR])
                    else:
                        ec, ecR = newtmp()
                        tt('pool', ec[:, 0:n], cm[:, 0:n], lf[:, 0:n], ALU.subtract, [cmR, lfR], [ecR])
                        C3 = ec[:, 0:n].rearrange('p (c k) -> p c k', k=64)
                        CR = ecR
                        act(be, cm3[:, :, 63], AF.Exp, [cmR], [scalR])
                        act(ga, C3[:, :, ref], AF.Exp, [ecR], [scalR])
                        tt('dve', stmp[:, 0:nch], cm3[:, :, 63], C3[:, :, ref], ALU.subtract, [cmR, ecR], [stmpR])
                        act(al, stmp[:, 0:nch], AF.Exp, [stmpR], [scalR])
                    aa, aaR = newtmp()
                    aa3 = aa[:, 0:n].rearrange('p (c k) -> p c k', k=64)
                    tt('dve', aa3, C3, C3[:, :, ref:ref + 1].broadcast_to([128, nch, 64]), ALU.subtract, [CR], [aaR])
                    wq, wqR = newtmp()
                    act(wq[:, 0:n], aa[:, 0:n], AF.Exp, [aaR], [wqR], scale=(1.0 if d == 0 else -1.0))
                    tt('pool', qt_[:, t0:t0 + n], QF[:, t0:t0 + n], wq[:, 0:n], ALU.mult, [QFR, wqR], [qtR])
                    wk, wkR = newtmp()
                    act(wk[:, 0:n], aa[:, 0:n], AF.Exp, [aaR], [wkR], scale=(-1.0 if d == 0 else 1.0))
                    tt('pool', kt_[:, t0:t0 + n], kk[:, 0:n], wk[:, 0:n], ALU.mult, [kkR, wkR], [ktR])
                ktok, ktokR = V_[1 + d], VR[1 + d]
                for grp in range(5):
                    tts = list(range(grp * 4, min(18, grp * 4 + 4)))
                    for ii, tti in enumerate(tts):
                        S.op('pe', lambda e, o=PSB[:, ii * 128:(ii + 1) * 128], a=kt_[:, tti * 128:(tti + 1) * 128]:
                             e.transpose(o, a, identb[:, :]), reads=[ktR, constR], writes=[PSBR])
                    nn = len(tts) * 128
                    cp('act', ktok[:, tts[0]:tts[0] + len(tts), :], PSB[:, 0:nn].rearrange('p (a b) -> p a b', b=128),
                       [PSBR], [ktokR])
            for d in range(2):
                qt_, qtR = A_[2 * d], AR[2 * d]
                kt_, ktR = A_[2 * d + 1], AR[2 * d + 1]
                ktok, ktokR = V_[1 + d], VR[1 + d]
                vtok, vtokR = V_[0], VR[0]
                order = list(range(18)) if d == 0 else [1, 0] + list(range(17, 1, -1))
                S.op('dve', lambda e: e.memset(Sst[:], 0.0), writes=[SstR])
                for pi_, pp in enumerate(order):
                    aps, apsR = PS[2 + pi_ % 2], PSR[2 + pi_ % 2]
                    tk = slice(pp * 128, (pp + 1) * 128)
                    mm(aps[:, 0:128], kt_[:, tk], qt_[:, tk], True, True, [ktR, qtR], [apsR])
                    ai = pi_ % 2
                    mt, mtR = newtmp()
                    tt('dve', mt[:, 0:128], aps[:, 0:128], masks[:, 5 + 2 * d, :], ALU.min, [apsR, ldR], [mtR])
                    tt('dve', aTs[ai][:, :], mt[:, 0:128], masks[:, 6 + 2 * d, :], ALU.max, [mtR, ldR], [aTsR[ai]])
                    ops_, opsR = PS[4], PSR[4]
                    mm(ops_[:, 0:128], vtok[:, pp, :], aTs[ai][:, :], True, False, [vtokR, aTsR[ai]], [opsR])
                    halves = (0, 1) if d == 0 else (1, 0)
                    for hi, hf in enumerate(halves):
                        c = 2 * pp + hf
                        tc_ = slice(pp * 128 + hf * 64, pp * 128 + hf * 64 + 64)
                        si = hi
                        act(Sbf[si][:, :], Sst[:, :], AF.Identity, [SstR, scalR], [SbfR[si]], scale=scal[:, d, 0, c:c + 1])
                        mm(ops_[:, hf * 64:hf * 64 + 64], Sbf[si][:, :], qt_[:, tc_], False, hi == 1,
                           [SbfR[si], qtR], [opsR])
                        ups, upsR = PS[5 + hi % 2], PSR[5 + hi % 2]
                        mm(ups[:, 0:128], ktok[hf * 64:hf * 64 + 64, pp, :], vtok[hf * 64:hf * 64 + 64, pp, :],
                           True, True, [ktokR, vtokR], [upsR])
                        ts('dve', Sst[:, :], Sst[:, :], scal[:, d, 1, c:c + 1], None, ALU.mult, None,
                           [SstR, scalR], [SstR])
                        stt(Sst[:, :], ups[:, 0:128], scal[:, d, 2, c:c + 1], Sst[:, :], ALU.mult, ALU.add,
                            [upsR, SstR, scalR], [SstR])
                    if d == 0:
                        cp('act', OACC[:, tk], ops_[:, 0:128], [opsR], [OACCR])
                    else:
                        tt('dve', OACC[:, tk], ops_[:, 0:128], OACC[:, tk], ALU.add, [opsR, OACCR], [OACCR])
            for bi, (t0, n) in enumerate(BLK):
                if bi == 0 and not with_ctx:
                    continue
                zs, zsR = gate_zs(l, 72 + h, t0, n)
                ps2, ps2R = PS[6], PSR[6]
                colsumsq(OACC[:, t0:t0 + n], OACCR, ps2, ps2R, n, True, True)
                rs, rsR = rstd_from_ps(ps2, ps2R, n, 1.0 / 128)
                o, oR = newtmp()
                stt(o[:, 0:n], OACC[:, t0:t0 + n], qkn[:, l, 2:3], rs[:, 0:n], ALU.mult, ALU.mult,
                    [OACCR, rsR, ldR], [oR])
                store_g(o, oR, n, zs, zsR, 0, 16 + h, t0)

    def phase3(b, l, src, last):
        for bi, (t0, n) in enumerate(BLK):
            if last and bi == 0:
                continue
            for c in range(24):
                dma(Gblk[:, c, 0:n], G[c, :, t0:t0 + n], [gR[c]], [GblkR[c]])
            col = NB if bi == 0 else b
            for j in range(16):
                ua, uaR = newtmp()
                for x in range(3):
                    wb, wbR = load_w(w_br[l, x, j], 8)
                    yps, ypsR = newps()
                    proj_fm(wb, wbR, 8, lambda kc, x=x: (Gblk[:, 8 * x + kc, 0:n], GblkR[8 * x + kc]), yps, ypsR, n)
                    wb2, wb2R = load_w(w_in[l, 80 + 16 * x + j], 16)
                    bps, bpsR = newps()
                    proj_fm(wb2, wb2R, 16, h_rhs(t0, n), bps, bpsR, n)
                    sg, sgR = newtmp()
                    act(sg[:, 0:n], bps[:, 0:n], AF.Sigmoid, [bpsR], [sgR])
                    if x == 0:
                        tt('dve', ua[:, 0:n], yps[:, 0:n], sg[:, 0:n], ALU.mult, [ypsR, sgR], [uaR])
                    else:
                        tt('dve', sg[:, 0:n], yps[:, 0:n], sg[:, 0:n], ALU.mult, [ypsR, sgR], [sgR])
                        if x == 1:
                            tt('pool', ua[:, 0:n], ua[:, 0:n], sg[:, 0:n], ALU.add, [uaR, sgR], [uaR])
                        else:
                            tt('pool', Ublk[:, j, 0:n], ua[:, 0:n], sg[:, 0:n], ALU.add, [uaR, sgR], [UblkR[j]])
            for j in range(16):
                wb, wbR = load_w(w_out[l, j], 16)
                ops_, opsR = newps()
                proj_fm(wb, wbR, 16, lambda kc: (Ublk[:, kc, 0:n], UblkR[kc]), ops_, opsR, n)
                xt, xtR = newtmp()
                dma(xt[:, 0:n], src[b, j, :, t0:t0 + n], [xsR[j]], [xtR])
                stt(xt[:, 0:n], ops_[:, 0:n], mod[:, l, col, 32 + j:33 + j], xt[:, 0:n], ALU.mult, ALU.add,
                    [opsR, xtR, modR], [xtR])
                store(XS[b, j, :, t0:t0 + n], xt[:, 0:n], [xtR], [xsR[j]])
            flush_stores()

    xsR = [Res() for _ in range(16)]
    gR = [Res() for _ in range(24)]
    outR = Res()
    for b in range(NB):
        for l in range(LAYERS):
            src = xT if l == 0 else XS
            last = (l == L - 1)
            norm_phase(b, l, src, False)
            if dbg and b == 0 and l == 0:
                for kc in range(16):
                    dma(dbg_h[kc], hT[:, kc, :], [hR[kc]], [outR])
            if stop_after == 'norm':
                break
            mixer_attn(l, 'a', not last)
            mixer_attn(l, 'b', not last)
            mixer_c(l, not last)
            bar()
            if dbg and b == 0 and l == 0:
                dma(dbg_g, G, gR, [outR])
            phase3(b, l, src, last)
            if last:
                pass
            bar()
            if dbg and b == 0 and l == 0:
                dma(dbg_x, XS[0], xsR, [outR])
        if LAYERS == L and stop_after is None:
            norm_phase(b, L - 1, XS, True)
            bar()
    if dbg:
        dma(dbg_mod, mod[:].rearrange('p a b c -> p (a b c)'), [modR], [outR])
    bar()
    S.emit(nc)
    build.sbuf_left = nc.sbuf_bytes_remaining
    return nc


def _retile(w, kc):
    K, N = w.shape
    return np.ascontiguousarray(w.reshape(kc, 128, N // 128, 128).transpose(2, 1, 0, 3))


def _vec128(v):
    n = v.shape[-1] // 128
    a = v.reshape(v.shape[:-1] + (n, 128))
    return np.ascontiguousarray(np.moveaxis(a, -1, 0))


def host_prep(inp, NB, cores, LW=L):
    f = np.float32
    shared = {}
    ada_w = np.asarray(inp['ada_w'][:LW], f)
    shared['ada_w'] = np.ascontiguousarray(
        ada_w.reshape(LW, 16, 128, 16, 384).transpose(0, 3, 2, 1, 4))
    shared['ada_b'] = np.ascontiguousarray(np.asarray(inp['ada_b'], f).reshape(L, 48, 128).transpose(2, 0, 1))
    shared['norm_g'] = np.ascontiguousarray(np.asarray(inp['norm_g'], f).reshape(L, 16, 128).transpose(2, 0, 1))
    shared['fnorm_g'] = np.ascontiguousarray(np.asarray(inp['final_norm_g'], f).reshape(16, 128).T)
    shared['w_in'] = np.stack([_retile(np.asarray(inp['w_in'][l], f), 16) for l in range(LW)])
    shared['w_br'] = np.stack([np.stack([_retile(np.asarray(inp[k][l], f), 8) for k in
                                         ('w_branch_a', 'w_branch_b', 'w_branch_c')]) for l in range(LW)])
    shared['w_out'] = np.stack([_retile(np.asarray(inp['w_out'][l], f), 16) for l in range(LW)])
    shared['sink'] = np.ascontiguousarray(np.broadcast_to(np.asarray(inp['a_sink'], f)[None], (128, L, 8)))
    qkn = np.stack([np.asarray(inp['b_q_norm'], f), np.asarray(inp['b_k_norm'], f),
                    np.asarray(inp['c_out_norm'], f)], axis=-1)
    shared['qkn'] = np.ascontiguousarray(qkn.transpose(1, 0, 2))
    clb = np.asarray(inp['c_lower_bound'], f).reshape(L, 2, 8, 128)
    shared['clb'] = np.ascontiguousarray(clb.transpose(3, 1, 2, 0))
    row = np.repeat(np.arange(NL // 64, dtype=f), 64)
    colv = np.tile(np.arange(64, dtype=f), NL // 64)
    inv_freq = (np.float32(10000.0) ** (-np.arange(32, dtype=f) / np.float32(32))).astype(f)
    ang = np.concatenate([row[:, None] * inv_freq, colv[:, None] * inv_freq], axis=-1).astype(f)
    ang = np.concatenate([ang, ang], axis=-1)
    cosT = np.cos(ang).astype(f).T
    sinT = np.sin(ang).astype(f).T.copy()
    sinT[64:] *= -1.0
    shared['rope'] = np.ascontiguousarray(np.stack([cosT, sinT], axis=1))
    kk = np.arange(128)[:, None]
    qq = np.arange(128)[None, :]
    same = (kk // 64) == (qq // 64)
    masks = np.stack([(same & (kk <= qq)), (same & (kk >= qq)), (kk >= qq), (kk <= qq), (kk == qq)], axis=1).astype(f)
    BIG = np.float32(1e30)
    ext = np.stack([BIG * masks[:, 0], -BIG * masks[:, 0], BIG * masks[:, 1], -BIG * masks[:, 1]], axis=1).astype(f)
    shared['masks'] = np.ascontiguousarray(np.concatenate([masks, ext], axis=1))
    m64 = np.ones((128, 512), f)
    m64[:, ::64] = 0.0
    shared['m64'] = m64
    maps = []
    for ci in range(cores):
        bs = list(range(ci * NB, (ci + 1) * NB))
        xs = []
        for b_ in bs:
            full = np.concatenate([np.asarray(inp['ctx'][b_], f), np.asarray(inp['x'][b_], f)], axis=0)
            xs.append(np.ascontiguousarray(full.T.reshape(16, 128, T)))
        m = dict(shared)
        m['xT'] = np.stack(xs)
        cc = np.stack([np.asarray(inp['c'][b_], f) for b_ in bs] + [np.asarray(inp['c_ctx'], f)], axis=-1)
        m['cT'] = np.ascontiguousarray(cc.reshape(16, 128, NB + 1).transpose(1, 0, 2))
        maps.append(m)
    return maps


def kernel(**inputs):
    B = inputs['x'].shape[0]
    cores = NCORES
    NB = B // cores
    nc = build(NB)
    maps = host_prep(inputs, NB, cores)
    res = run_bass_kernel_spmd(nc, maps, core_ids=list(range(cores)))
    out = np.empty((B, NL, D), np.float32)
    for ci in range(cores):
        o = res.results[ci]['outT']
        for i in range(NB):
            out[ci * NB + i] = o[i].reshape(D, NL).T
    return out
```

```python
import numpy as np
import concourse.bass as bass
import concourse.mybir as mybir
from concourse.bass_utils import run_bass_kernel_spmd

F32 = mybir.dt.float32
BF16 = mybir.dt.bfloat16
AF = mybir.ActivationFunctionType
ALU = mybir.AluOpType

L = 4
D = 2048
T = 2304
MC = 256
NL = 2048
BLK = [(0, 256), (256, 512), (768, 512), (1280, 512), (1792, 512)]
EPS = 1e-6
ENGS = ['pe', 'act', 'dve', 'pool', 'sp']
NCORES = 8


class Res:
    __slots__ = ('w', 'r')

    def __init__(self):
        self.w = None
        self.r = {}


class Sched:
    NDMA = 8

    def __init__(self):
        self.ops = {e: [] for e in ENGS}
        self.clock = {e: {} for e in ENGS}
        self.dma_n = 0
        self.dma_cnt = [0] * self.NDMA

    @staticmethod
    def _kv(tok):
        if tok[0] == 'dma':
            return ('dma', tok[1]), tok[2]
        return tok[0], tok[1]

    def op(self, eng, fn, reads=(), writes=(), dma=False):
        idx = len(self.ops[eng])
        deps = []
        for t in reads:
            if t.w is not None:
                deps.append((t.w, True))
        for t in writes:
            if t.w is not None:
                deps.append((t.w, False))
            for k, v in t.r.items():
                deps.append(((('dma', k[1], v) if k[0] == 'dma' else (k, v)), False))
        clk = self.clock[eng]
        if dma:
            slot = self.dma_n % self.NDMA
            self.dma_n += 1
            if self.dma_cnt[slot] > 0:
                deps.append((('dma', slot, self.dma_cnt[slot]), False))
            self.dma_cnt[slot] += 1
            mytok = ('dma', slot, self.dma_cnt[slot])
        else:
            mytok = (eng, idx)
        best = {}
        for tok, raw in deps:
            key, val = self._kv(tok)
            if key == eng and (eng in ('pe', 'sp') or not raw):
                continue
            if clk.get(key, -1) >= val:
                continue
            clk[key] = val
            best[key] = tok
            if tok[0] != 'dma':
                o = self.ops[tok[0]][tok[1]]
                o['sig'] = True
                for k2, v2 in o['clk'].items():
                    if clk.get(k2, -1) < v2:
                        clk[k2] = v2
        snap = dict(clk)
        if not dma:
            snap[eng] = idx
        rec = dict(fn=fn, waits=list(best.values()), sig=False, clk=snap, tok=mytok, dma=dma)
        self.ops[eng].append(rec)
        mk, mv = self._kv(mytok)
        for t in reads:
            if t.r.get(mk, -1) < mv:
                t.r[mk] = mv
        for t in writes:
            t.w = mytok
            t.r = {}
        return mytok

    def barrier(self):
        rs = []
        for e in ENGS:
            if e != 'sp' and self.ops[e]:
                r = Res()
                r.w = (e, len(self.ops[e]) - 1)
                rs.append(r)
        for s in range(self.NDMA):
            if self.dma_cnt[s] > 0:
                r = Res()
                r.w = ('dma', s, self.dma_cnt[s])
                rs.append(r)
        for e in ENGS:
            self.op(e, lambda g: g.nop(), reads=rs)

    def emit(self, nc):
        sems = {e: nc.alloc_semaphore('s_' + e) for e in ENGS if e != 'sp'}
        dsems = [nc.alloc_semaphore('d_%d' % i) for i in range(self.NDMA)]
        cnt = {}
        for e in ENGS:
            c = 0
            arr = []
            for o in self.ops[e]:
                if o['sig'] and not o['dma']:
                    c += 1
                arr.append(c)
            cnt[e] = arr

        def emit_engine(ename, eng):
            for o in self.ops[ename]:
                for tok in o['waits']:
                    if tok[0] == 'dma':
                        eng.wait_ge(dsems[tok[1]], 16 * tok[2])
                    else:
                        eng.wait_ge(sems[tok[0]], cnt[tok[0]][tok[1]])
                ins = o['fn'](eng)
                if o['dma']:
                    ins.then_inc(dsems[o['tok'][1]], 16)
                elif o['sig']:
                    ins.then_inc(sems[ename], 1)

        with nc.Block() as block:
            @block.tensor
            def _(e):
                emit_engine('pe', e)

            @block.scalar
            def _(e):
                emit_engine('act', e)

            @block.vector
            def _(e):
                emit_engine('dve', e)

            @block.gpsimd
            def _(e):
                emit_engine('pool', e)

            @block.sync
            def _(e):
                emit_engine('sp', e)


def build(NB, LAYERS=L, dbg=False, stop_after=None):
    nc = bass.Bass('TRN2', target_bir_lowering=False, dynamic_dma_scratch_size=2048)
    S = Sched()

    def din(name, shape, dt=F32):
        return nc.dram_tensor(name, list(shape), dt, kind='ExternalInput').ap()

    xT = din('xT', [NB, 16, 128, T])
    cT = din('cT', [128, 16, NB + 1])
    ada_w = din('ada_w', [LAYERS, 16, 128, 16, 384])
    ada_b = din('ada_b', [128, L, 48])
    norm_g = din('norm_g', [128, L, 16])
    fnorm_g = din('fnorm_g', [128, 16])
    w_in = din('w_in', [LAYERS, 128, 128, 16, 128])
    w_br = din('w_br', [LAYERS, 3, 16, 128, 8, 128])
    w_out = din('w_out', [LAYERS, 16, 128, 16, 128])
    sink_in = din('sink', [128, L, 8])
    qkn_in = din('qkn', [128, L, 3])
    clb_in = din('clb', [128, 2, 8, L])
    rope_in = din('rope', [128, 2, NL])
    masks_in = din('masks', [128, 9, 128])
    m64_in = din('m64', [128, 512])
    outT = nc.dram_tensor('outT', [NB, 16, 128, NL], F32, kind='ExternalOutput').ap()
    XS = nc.dram_tensor('XS', [NB, 16, 128, T], F32).ap()
    G = nc.dram_tensor('G', [24, 128, T], BF16).ap()
    if dbg:
        dbg_h = nc.dram_tensor('dbg_h', [16, 128, T], BF16, kind='ExternalOutput').ap()
        dbg_g = nc.dram_tensor('dbg_g', [24, 128, T], BF16, kind='ExternalOutput').ap()
        dbg_x = nc.dram_tensor('dbg_x', [16, 128, T], F32, kind='ExternalOutput').ap()
        dbg_mod = nc.dram_tensor('dbg_mod', [128, L * (NB + 1) * 48], F32, kind='ExternalOutput').ap()

    def sb(name, shape, dt=F32):
        return nc.alloc_sbuf_tensor('sb_' + name, list(shape), dt)

    hT = sb('hT', [128, 16, T], BF16)
    hR = [Res() for _ in range(16)]
    arena = sb('arena', [128, 26624], BF16)
    wst = [sb('wst%d' % i, [128, 16, 128]) for i in range(3)]
    wstR = [Res() for _ in range(3)]
    wbf = [sb('wbf%d' % i, [128, 16, 128], BF16) for i in range(3)]
    wbfR = [Res() for _ in range(3)]
    rope = sb('rope', [128, 2, NL])
    ropeR = Res()
    NTMP = 8
    tmp = [sb('tmp%d' % i, [128, 512]) for i in range(NTMP)]
    tmpR = [Res() for _ in range(NTMP)]
    mod = sb('mod', [128, L, NB + 1, 48])
    modR = Res()
    gp = sb('gp', [128, 2, 16])
    gpR = Res()
    ng = sb('ng', [128, L, 16])
    fg = sb('fg', [128, 16])
    adab = sb('adab', [128, L, 48])
    cs = sb('cs', [128, 16, NB + 1])
    csig = sb('csig', [128, 16, NB + 1])
    sinkE = sb('sinkE', [128, L, 8])
    qkn = sb('qkn_s', [128, L, 3])
    clb = sb('clb_s', [128, 2, 8, L])
    lbt = sb('lbt', [128, 2, 8, L])
    oml = sb('oml', [128, 2, 8, L])
    noml = sb('noml', [128, 2, 8, L])
    lbtmp = sb('lbtmp', [128, 2, 8])
    masks = sb('masks_s', [128, 9, 128])
    m64 = sb('m64_s', [128, 512])
    identb = sb('identb', [128, 128], BF16)
    onesb = sb('onesb', [128, 128], BF16)
    scal = sb('scal', [128, 2, 3, 36])
    scalR = Res()
    stmp = sb('stmp', [128, 16])
    stmpR = Res()
    Sst = sb('Sst', [128, 128])
    SstR = Res()
    Sst2 = [Sst, sb('Sst1', [128, 128])]
    Sst2R = [SstR, Res()]
    Sbf = [sb('Sbf%d' % i, [128, 128], BF16) for i in range(4)]
    Sbf2 = [[Sbf[0], Sbf[1]], [Sbf[2], Sbf[3]]]
    Sbf2R = [[Res(), Res()], [Res(), Res()]]
    OPR = [Res() for _ in range(18)]
    aTs = [sb('aTs%d' % i, [128, 128], BF16) for i in range(2)]
    aTsR = [Res(), Res()]
    pT = [sb('pT%d' % i, [128, 512], BF16) for i in range(3)]
    pTR = [Res() for _ in range(3)]
    gout = [sb('gout%d' % i, [128, 512], BF16) for i in range(2)]
    goutR = [Res(), Res()]
    sqb = [sb('sqb%d' % i, [128, 512], BF16) for i in range(2)]
    sqbR = [Res(), Res()]
    constR = Res()
    rsT = [sb('rsT%d' % i, [128, 512]) for i in range(2)]
    rsTR = [Res(), Res()]
    zsT = [sb('zsT%d' % i, [128, 512]) for i in range(2)]
    zsTR = [Res(), Res()]

    PS = [nc.alloc_psum_tensor('ps%d' % i, [128, 512], F32) for i in range(7)]
    PSR = [Res() for _ in range(7)]
    PSB = nc.alloc_psum_tensor('psb', [128, 512], BF16)
    PSBR = Res()

    def av(off, shape, dt=BF16):
        n = 1
        for s_ in shape:
            n *= s_
        if dt == F32:
            a = arena[:, off:off + 2 * n].bitcast(F32)
        else:
            a = arena[:, off:off + n]
        if len(shape) == 2:
            return a.rearrange('p (a b) -> p a b', b=shape[1])
        return a

    A_ = [av(i * T, [T]) for i in range(4)]
    AR = [Res() for _ in range(4)]
    V_ = [av(4 * T + i * T, [18, 128]) for i in range(3)]
    VR = [Res() for _ in range(3)]
    QF = av(7 * T, [T], F32)
    QFR = Res()
    OACC = av(9 * T, [T], F32)
    OACCR = Res()
    assert 11 * T <= 26624
    Gblk = av(0, [24, 512])
    GblkR = [Res() for _ in range(24)]
    Ublk = av(24 * 512, [16, 512])
    UblkR = [Res() for _ in range(16)]
    ADA = [av(i * 12288, [16, 384], F32) for i in range(2)]
    ADAR = [Res(), Res()]

    cnt = {'w': 0, 't': 0, 'ps': 0, 'pt': 0, 'g': 0, 'sq': 0, 'rs': 0, 'zs': 0}

    def newtmp():
        i = cnt['t'] % NTMP
        cnt['t'] += 1
        return tmp[i], tmpR[i]

    def newps():
        i = cnt['ps'] % 2
        cnt['ps'] += 1
        return PS[i], PSR[i]

    def dma(out, in_, reads, writes, q='sp'):
        S.op(q, lambda e, o=out, i=in_: e.dma_start(out=o, in_=i), reads=reads, writes=writes, dma=True)

    deferred = []

    def flush_stores(keep=0):
        while len(deferred) > keep:
            o, i, r, w = deferred.pop(0)
            dma(o, i, r, w, q='act')

    def store(out, in_, reads, writes):
        deferred.append((out, in_, reads, writes))
        flush_stores(keep=1)

    def bar():
        flush_stores()
        S.barrier()

    def mm(out, lhsT, rhs, start, stop, reads, writes):
        S.op('pe', lambda e, o=out, a=lhsT, b=rhs, s0=start, s1=stop: e.matmul(o, a, b, start=s0, stop=s1),
             reads=reads, writes=writes)

    def act(out, in_, func, reads, writes, bias=None, scale=None):
        kw = {}
        if bias is not None:
            kw['bias'] = bias
        if scale is not None:
            kw['scale'] = scale
        S.op('act', lambda e, o=out, i=in_, f=func, k=kw: e.activation(o, i, f, **k), reads=reads, writes=writes)

    def tt(eng, out, in0, in1, op, reads, writes):
        S.op(eng, lambda e, o=out, a=in0, b=in1, p=op: e.tensor_tensor(o, a, b, p), reads=reads, writes=writes)

    def ts(eng, out, in0, s1, s2, op0, op1, reads, writes):
        if op1 is None:
            S.op(eng, lambda e, o=out, a=in0, x=s1, p0=op0: e.tensor_scalar(o, a, x, None, p0),
                 reads=reads, writes=writes)
        else:
            S.op(eng, lambda e, o=out, a=in0, x=s1, y=s2, p0=op0, p1=op1: e.tensor_scalar(o, a, x, y, p0, p1),
                 reads=reads, writes=writes)

    def stt(out, in0, scalar, in1, op0, op1, reads, writes):
        S.op('dve', lambda e, o=out, a=in0, s_=scalar, b=in1, p0=op0, p1=op1:
             e.scalar_tensor_tensor(o, a, s_, b, p0, p1), reads=reads, writes=writes)

    def cp(eng, out, in_, reads, writes):
        if eng == 'act':
            S.op(eng, lambda e, o=out, i=in_: e.activation(o, i, AF.Identity), reads=reads, writes=writes)
        else:
            S.op(eng, lambda e, o=out, i=in_: e.tensor_copy(o, i), reads=reads, writes=writes)

    def load_w(src, nkc):
        i = cnt['w']
        cnt['w'] += 1
        st, stR = wst[i % 3], wstR[i % 3]
        wb, wbR = wbf[i % 3], wbfR[i % 3]
        dma(st[:, 0:nkc, :], src, [], [stR])
        cp('dve', wb[:, 0:nkc, :], st[:, 0:nkc, :], [stR], [wbR])
        return wb, wbR

    def proj_fm(wb, wbR, nkc, rhs_fn, ps, psR, n):
        for kc in range(nkc):
            r_ap, r_res = rhs_fn(kc)
            mm(ps[:, 0:n], wb[:, kc, :], r_ap, kc == 0, kc == nkc - 1, [wbR, r_res], [psR])

    def h_rhs(t0, n):
        return lambda kc: (hT[:, kc, t0:t0 + n], hR[kc])

    def rstd_from_ps(ps, psR, n, inv_n):
        t1, t1R = newtmp()
        act(t1[:, 0:n], ps[:, 0:n], AF.Ln, [psR, constR], [t1R], bias=epsT[:, 0:1], scale=inv_n)
        i = cnt['rs'] % 2
        cnt['rs'] += 1
        t2, t2R = rsT[i], rsTR[i]
        act(t2[:, 0:n], t1[:, 0:n], AF.Exp, [t1R], [t2R], scale=-0.5)
        return t2, t2R

    def colsumsq(src_ap, srcR, ps, psR, n, first, last):
        i = cnt['sq'] % 2
        cnt['sq'] += 1
        act(sqb[i][:, 0:n], src_ap, AF.Square, [srcR], [sqbR[i]])
        mm(ps[:, 0:n], onesb[:, :], sqb[i][:, 0:n], first, last, [sqbR[i], constR], [psR])

    epsT = sb('epsT', [128, 1])
    S.op('dve', lambda e: e.memset(epsT[:], EPS), writes=[constR])
    S.op('dve', lambda e: e.memset(onesb[:], 1.0), writes=[constR])
    ldR = Res()
    for dst, src in [(ng[:], norm_g), (fg[:], fnorm_g), (adab[:], ada_b), (cs[:], cT), (sinkE[:], sink_in),
                     (qkn[:], qkn_in), (clb[:], clb_in), (rope[:], rope_in), (masks[:], masks_in), (m64[:], m64_in)]:
        dma(dst, src, [], [ldR])
    cp('dve', identb[:], masks[:, 4, :], [ldR], [constR])
    act(csig[:], cs[:], AF.Sigmoid, [ldR], [constR])
    tt('dve', cs[:], cs[:], csig[:], ALU.mult, [constR, ldR], [constR])
    act(sinkE[:], sinkE[:], AF.Exp, [ldR], [constR])
    act(clb[:], clb[:], AF.Exp, [ldR], [constR])
    S.op('dve', lambda e: e.tensor_reduce(lbtmp[:], clb[:], mybir.AxisListType.X, ALU.add), reads=[constR], writes=[constR])
    S.op('dve', lambda e: e.reciprocal(lbtmp[:], lbtmp[:]), reads=[constR], writes=[constR])
    S.op('dve', lambda e: e.memset(lbt[:], 0.0), writes=[constR])
    for l in range(1, L):
        tt('dve', clb[:, :, :, l], clb[:, :, :, l], lbtmp[:], ALU.mult, [constR], [constR])
        tt('dve', lbt[:, :, :, l], lbt[:, :, :, l - 1], clb[:, :, :, l], ALU.add, [constR], [constR])
    ts('dve', oml[:], lbt[:], -1.0, 1.0, ALU.mult, ALU.add, [constR], [constR])
    ts('dve', noml[:], oml[:], -1.0, None, ALU.mult, None, [constR], [constR])

    nslab = 0
    for l in range(LAYERS):
        for s_ in range(16):
            a, aR = ADA[nslab % 2], ADAR[nslab % 2]
            nslab += 1
            dma(a, ada_w[l, s_], [], [aR])
            for jj in range(3):
                j = 3 * s_ + jj
                ps, psR = newps()
                for kc in range(16):
                    mm(ps[:, 0:NB + 1], a[:, kc, jj * 128:(jj + 1) * 128], cs[:, kc, :], kc == 0, kc == 15,
                       [aR, constR], [psR])
                ts('dve', mod[:, l, :, j], ps[:, 0:NB + 1], adab[:, l, j:j + 1], None, ALU.add, None,
                   [psR, ldR], [modR])
    bar()

    def norm_phase(b, l, src, final):
        if not final:
            for ci, col in enumerate((b, NB)):
                ts('dve', gp[:, ci, :], mod[:, l, col, 16:32], 1.0, None, ALU.add, None, [modR], [gpR])
                tt('dve', gp[:, ci, :], gp[:, ci, :], ng[:, l, :], ALU.mult, [gpR, ldR], [gpR])
        for bi, (t0, n) in enumerate(BLK):
            if final and bi == 0:
                continue
            ps, psR = newps()
            for kc in range(16):
                xt, xtR = newtmp()
                dma(xt[:, 0:n], src[b, kc, :, t0:t0 + n], [xsR[kc]], [xtR])
                colsumsq(xt[:, 0:n], xtR, ps, psR, n, kc == 0, kc == 15)
            rs, rsR = rstd_from_ps(ps, psR, n, 1.0 / D)
            ci = 1 if bi == 0 else 0
            col = NB if bi == 0 else b
            dbgn = dbg and b == 0 and l == 0 and bi == 1 and not final
            if dbgn:
                d1 = nc.dram_tensor('dbg_rs', [128, 512], F32, kind='ExternalOutput').ap()
                dma(d1, rs[:, :], [rsR], [outR])
                d3 = nc.dram_tensor('dbg_gp', [128, 32], F32, kind='ExternalOutput').ap()
                dma(d3, gp[:].rearrange('p a b -> p (a b)'), [gpR], [outR])
                d4 = nc.dram_tensor('dbg_ng', [128, 64], F32, kind='ExternalOutput').ap()
                dma(d4, ng[:].rearrange('p a b -> p (a b)'), [ldR], [outR])
            for kc in range(16):
                xt, xtR = newtmp()
                dma(xt[:, 0:n], src[b, kc, :, t0:t0 + n], [xsR[kc]], [xtR])
                if final:
                    stt(xt[:, 0:n], xt[:, 0:n], fg[:, kc:kc + 1], rs[:, 0:n], ALU.mult, ALU.mult,
                        [xtR, rsR, ldR], [xtR])
                    store(outT[b, kc, :, t0 - MC:t0 - MC + n], xt[:, 0:n], [xtR], [outR])
                    if kc == 15:
                        flush_stores()
                else:
                    if dbgn and kc == 0:
                        d5 = nc.dram_tensor('dbg_x0', [128, 512], F32, kind='ExternalOutput').ap()
                        dma(d5, xt[:, :], [xtR], [outR])
                    tt('dve', xt[:, 0:n], xt[:, 0:n], rs[:, 0:n], ALU.mult, [xtR, rsR], [xtR])
                    if dbgn and kc == 0:
                        d6 = nc.dram_tensor('dbg_x1', [128, 512], F32, kind='ExternalOutput').ap()
                        dma(d6, xt[:, :], [xtR], [outR])
                    act(hT[:, kc, t0:t0 + n], xt[:, 0:n], AF.Identity, [xtR, gpR, modR], [hR[kc]],
                        bias=mod[:, l, col, kc:kc + 1], scale=gp[:, ci, kc:kc + 1])

    def rope_apply(src, srcR, dst_ap, dstR, t0, n):
        p0 = t0 - MC
        t1, t1R = newtmp()
        tt('pool', t1[:, 0:n], src[:, 0:n], rope[:, 0, p0:p0 + n], ALU.mult, [srcR, ldR], [t1R])
        t2, t2R = newtmp()
        tt('dve', t2[0:64, 0:n], src[64:128, 0:n], rope[64:128, 1, p0:p0 + n], ALU.mult, [srcR, ldR], [t2R])
        tt('dve', t2[64:128, 0:n], src[0:64, 0:n], rope[0:64, 1, p0:p0 + n], ALU.mult, [srcR, ldR], [t2R])
        tt('pool', dst_ap, t1[:, 0:n], t2[:, 0:n], ALU.add, [t1R, t2R], [dstR])

    def qk_proj(l, j, dstA, dstR, norm_col):
        wb, wbR = load_w(w_in[l, j], 16)
        for bi, (t0, n) in enumerate(BLK):
            ps, psR = newps()
            proj_fm(wb, wbR, 16, h_rhs(t0, n), ps, psR, n)
            q, qR = newtmp()
            if norm_col is None:
                cp('act', q[:, 0:n], ps[:, 0:n], [psR], [qR])
            else:
                cp('act', q[:, 0:n], ps[:, 0:n], [psR], [qR])
                ps2, ps2R = PS[6], PSR[6]
                colsumsq(q[:, 0:n], qR, ps2, ps2R, n, True, True)
                rs, rsR = rstd_from_ps(ps2, ps2R, n, 1.0 / 128)
                stt(q[:, 0:n], q[:, 0:n], qkn[:, l, norm_col:norm_col + 1], rs[:, 0:n], ALU.mult, ALU.mult,
                    [qR, rsR, ldR], [qR])
            if bi == 0:
                cp('pool', dstA[:, t0:t0 + n], q[:, 0:n], [qR], [dstR])
            else:
                rope_apply(q, qR, dstA[:, t0:t0 + n], dstR, t0, n)

    def v_proj(l, j, dstV, dstR):
        wb, wbR = load_w(w_in[l, j], 16)
        for grp in range(5):
            tts = list(range(grp * 4, min(18, grp * 4 + 4)))
            ps, psR = newps()
            for ii, tti in enumerate(tts):
                for kc in range(16):
                    mm(ps[:, ii * 128:(ii + 1) * 128], hT[:, kc, tti * 128:(tti + 1) * 128], wb[:, kc, :],
                       kc == 0, kc == 15, [wbR, hR[kc]], [psR])
            nn = len(tts) * 128
            cp('act', dstV[:, tts[0]:tts[0] + len(tts), :], ps[:, 0:nn].rearrange('p (a b) -> p a b', b=128),
               [psR], [dstR])

    def gate_zs(l, jz, t0, n):
        wb, wbR = load_w(w_in[l, jz], 16)
        ps, psR = newps()
        proj_fm(wb, wbR, 16, h_rhs(t0, n), ps, psR, n)
        i = cnt['zs'] % 2
        cnt['zs'] += 1
        sg, sgR = zsT[i], zsTR[i]
        act(sg[:, 0:n], ps[:, 0:n], AF.Sigmoid, [psR], [sgR])
        tt('dve', sg[:, 0:n], ps[:, 0:n], sg[:, 0:n], ALU.mult, [psR, sgR], [sgR])
        return sg, sgR

    def store_g(src, srcR, n, zs, zsR, zoff, gchunk, t0):
        i = cnt['g'] % 2
        cnt['g'] += 1
        tt('pool', gout[i][:, 0:n], src[:, 0:n], zs[:, zoff:zoff + n], ALU.mult, [srcR, zsR], [goutR[i]])
        store(G[gchunk, :, t0:t0 + n], gout[i][:, 0:n], [goutR[i]], [gR[gchunk]])

    def attend(KT, KR, VT, VTR, QT, QR, q0, qn, klist, sink_ap, zs, zsR, zoff, gchunk):
        oT, oTR = PS[4], PSR[4]
        dn, dnR = PS[5], PSR[5]
        nk = len(klist)

        def issue_s(idx):
            kt = klist[idx][0]
            sT, sTR = PS[2 + idx % 2], PSR[2 + idx % 2]
            mm(sT[:, 0:qn], KT[:, kt * 128:(kt + 1) * 128], QT[:, q0:q0 + qn], True, True, [KR, QR], [sTR])

        issue_s(0)
        for idx, (kt, mk) in enumerate(klist):
            sT, sTR = PS[2 + idx % 2], PSR[2 + idx % 2]
            pi = cnt['pt'] % 3
            cnt['pt'] += 1
            act(pT[pi][:, 0:qn], sT[:, 0:qn], AF.Exp, [sTR], [pTR[pi]], scale=128 ** -0.5)
            if idx + 1 < nk:
                issue_s(idx + 1)
            if mk is not None:
                tt('dve', pT[pi][:, 0:qn], pT[pi][:, 0:qn], masks[:, mk, 0:qn], ALU.mult, [pTR[pi], ldR], [pTR[pi]])
            mm(oT[:, 0:qn], VT[:, kt, :], pT[pi][:, 0:qn], idx == 0, idx == nk - 1, [VTR, pTR[pi]], [oTR])
            mm(dn[:, 0:qn], onesb[:, :], pT[pi][:, 0:qn], idx == 0, idx == nk - 1, [constR, pTR[pi]], [dnR])
        r, rR = newtmp()
        if sink_ap is not None:
            ts('dve', r[:, 0:qn], dn[:, 0:qn], sink_ap, None, ALU.add, None, [dnR, constR], [rR])
            S.op('dve', lambda e, a=r[:, 0:qn]: e.reciprocal(a, a), reads=[rR], writes=[rR])
        else:
            S.op('dve', lambda e, a=r[:, 0:qn], d=dn[:, 0:qn]: e.reciprocal(a, d), reads=[dnR], writes=[rR])
        tt('dve', r[:, 0:qn], oT[:, 0:qn], r[:, 0:qn], ALU.mult, [oTR, rR], [rR])
        store_g(r, rR, qn, zs, zsR, zoff, gchunk, q0)

    def mixer_attn(l, which, with_ctx):
        base = 0 if which == 'a' else 20
        for g in range(2):
            KT, KR = A_[1], AR[1]
            VT, VTR = V_[0], VR[0]
            qk_proj(l, base + 8 + g, KT, KR, None if which == 'a' else 1)
            v_proj(l, base + 10 + g, VT, VTR)
            for i in range(4):
                h = 4 * g + i
                QT, QR = A_[0], AR[0]
                qk_proj(l, base + h, QT, QR, None if which == 'a' else 0)
                sink_ap = sinkE[:, l, h:h + 1] if which == 'a' else None
                gchunk = (0 if which == 'a' else 8) + h
                for bi, (t0, n) in enumerate(BLK):
                    if bi == 0 and not with_ctx:
                        continue
                    zs, zsR = gate_zs(l, base + 12 + h, t0, n)
                    if bi == 0:
                        attend(KT, KR, VT, VTR, QT, QR, 0, 256, [(0, None), (1, None)], sink_ap, zs, zsR, 0, gchunk)
                    elif which == 'b':
                        attend(KT, KR, VT, VTR, QT, QR, t0, n, [(k, None) for k in range(18)], sink_ap,
                               zs, zsR, 0, gchunk)
                    else:
                        for sbk in range(4):
                            nblk = (t0 - MC) // 128 + sbk
                            kl = [(0, None), (1, None)]
                            if nblk > 0:
                                kl.append((2 + nblk - 1, 2))
                            kl.append((2 + nblk, None))
                            if nblk < 15:
                                kl.append((2 + nblk + 1, 3))
                            attend(KT, KR, VT, VTR, QT, QR, t0 + sbk * 128, 128, kl, sink_ap, zs, zsR,
                                   sbk * 128, gchunk)

    def mixer_c(l, with_ctx):
        for h in range(8):
            wb, wbR = load_w(w_in[l, 40 + h], 16)
            for (t0, n) in BLK:
                ps, psR = newps()
                proj_fm(wb, wbR, 16, h_rhs(t0, n), ps, psR, n)
                act(QF[:, t0:t0 + n], ps[:, 0:n], AF.Identity, [psR], [QFR], scale=128 ** -0.5)
            v_proj(l, 64 + h, V_[0], VR[0])
            for d in range(2):
                qt_, qtR = A_[2 * d], AR[2 * d]
                kt_, ktR = A_[2 * d + 1], AR[2 * d + 1]
                wb, wbR = load_w(w_in[l, 48 + 8 * d + h], 16)
                lb_ap = lbt[:, d, h, l:l + 1]
                oml_ap = oml[:, d, h, l:l + 1]
                noml_ap = noml[:, d, h, l:l + 1]
                ref = 31 if d == 0 else 32
                for (t0, n) in BLK:
                    nch = n // 64
                    c0 = t0 // 64
                    ps, psR = newps()
                    proj_fm(wb, wbR, 16, h_rhs(t0, n), ps, psR, n)
                    r, rR = newtmp()
                    act(r[:, 0:n], ps[:, 0:n], AF.Sigmoid, [psR], [rR])
                    lf, lfR = newtmp()
                    act(lf[:, 0:n], r[:, 0:n], AF.Ln, [rR, constR], [lfR], bias=lb_ap, scale=oml_ap)
                    kk, kkR = newtmp()
                    ts('dve', kk[:, 0:n], r[:, 0:n], noml_ap, oml_ap, ALU.mult, ALU.add, [rR, constR], [kkR])
                    cm, cmR = newtmp()
                    S.op('dve', lambda e, o=cm[:, 0:n], a=m64[:, 0:n], b_=lf[:, 0:n]:
                         e.tensor_tensor_scan(o, a, b_, 0.0, ALU.mult, ALU.add), reads=[ldR, lfR], writes=[cmR])
                    cm3 = cm[:, 0:n].rearrange('p (c k) -> p c k', k=64)
                    al = scal[:, d, 0, c0:c0 + nch]
                    be = scal[:, d, 1, c0:c0 + nch]
                    ga = scal[:, d, 2, c0:c0 + nch]
                    if d == 0:
                        C3 = cm3
                        CR = cmR
                        act(al, cm3[:, :, ref], AF.Exp, [cmR], [scalR])
                        act(be, cm3[:, :, 63], AF.Exp, [cmR], [scalR])
                        tt('dve', stmp[:, 0:nch], cm3[:, :, 63], cm3[:, :, ref], ALU.subtract, [cmR], [stmpR])
                        act(ga, stmp[:, 0:nch], AF.Exp, [stmpR], [scalR])
                    else:
                        ec, ecR = newtmp()
                        tt('pool', ec[:, 0:n], cm[:, 0:n], lf[:, 0:n], ALU.subtract, [cmR, lfR], [ecR])
                        C3 = ec[:, 0:n].rearrange('p (c k) -> p c k', k=64)
                        CR = ecR
                        act(be, cm3[:, :, 63], AF.Exp, [cmR], [scalR])
                        act(ga, C3[:, :, ref], AF.Exp, [ecR], [scalR])
                        tt('dve', stmp[:, 0:nch], cm3[:, :, 63], C3[:, :, ref], ALU.subtract, [cmR, ecR], [stmpR])
                        act(al, stmp[:, 0:nch], AF.Exp, [stmpR], [scalR])
                    aa, aaR = newtmp()
                    aa3 = aa[:, 0:n].rearrange('p (c k) -> p c k', k=64)
                    tt('dve', aa3, C3, C3[:, :, ref:ref + 1].broadcast_to([128, nch, 64]), ALU.subtract, [CR], [aaR])
                    wq, wqR = newtmp()
                    act(wq[:, 0:n], aa[:, 0:n], AF.Exp, [aaR], [wqR], scale=(1.0 if d == 0 else -1.0))
                    tt('pool', qt_[:, t0:t0 + n], QF[:, t0:t0 + n], wq[:, 0:n], ALU.mult, [QFR, wqR], [qtR])
                    wk, wkR = newtmp()
                    act(wk[:, 0:n], aa[:, 0:n], AF.Exp, [aaR], [wkR], scale=(-1.0 if d == 0 else 1.0))
                    tt('pool', kt_[:, t0:t0 + n], kk[:, 0:n], wk[:, 0:n], ALU.mult, [kkR, wkR], [ktR])
                ktok, ktokR = V_[1 + d], VR[1 + d]
                for grp in range(5):
                    tts = list(range(grp * 4, min(18, grp * 4 + 4)))
                    for ii, tti in enumerate(tts):
                        S.op('pe', lambda e, o=PSB[:, ii * 128:(ii + 1) * 128], a=kt_[:, tti * 128:(tti + 1) * 128]:
                             e.transpose(o, a, identb[:, :]), reads=[ktR, constR], writes=[PSBR])
                    nn = len(tts) * 128
                    cp('act', ktok[:, tts[0]:tts[0] + len(tts), :], PSB[:, 0:nn].rearrange('p (a b) -> p a b', b=128),
                       [PSBR], [ktokR])
            owritten = set()

            def chunk_gen(d):
                qt_, qtR = A_[2 * d], AR[2 * d]
                kt_, ktR = A_[2 * d + 1], AR[2 * d + 1]
                ktok, ktokR = V_[1 + d], VR[1 + d]
                vtok, vtokR = V_[0], VR[0]
                order = list(range(18)) if d == 0 else [1, 0] + list(range(17, 1, -1))
                St, StR = Sst2[d], Sst2R[d]
                aps, apsR = (PS[2], PSR[2]) if d == 0 else (PS[3], PSR[3])
                ops_, opsR = (PS[4], PSR[4]) if d == 0 else (PS[6], PSR[6])
                ubank = PS[5] if d == 0 else PS[0]
                ubR = (PSR[5] if d == 0 else PSR[0])
                S.op('dve', lambda e, t=St: e.memset(t[:], 0.0), writes=[StR])
                for pi_, pp in enumerate(order):
                    tk = slice(pp * 128, (pp + 1) * 128)
                    mm(aps[:, 0:128], kt_[:, tk], qt_[:, tk], True, True, [ktR, qtR], [apsR])
                    mt, mtR = newtmp()
                    tt('dve', mt[:, 0:128], aps[:, 0:128], masks[:, 5 + 2 * d, :], ALU.min, [apsR, ldR], [mtR])
                    tt('dve', aTs[d][:, :], mt[:, 0:128], masks[:, 6 + 2 * d, :], ALU.max, [mtR, ldR], [aTsR[d]])
                    mm(ops_[:, 0:128], vtok[:, pp, :], aTs[d][:, :], True, False, [vtokR, aTsR[d]], [opsR])
                    halves = (0, 1) if d == 0 else (1, 0)
                    for hi, hf in enumerate(halves):
                        c = 2 * pp + hf
                        tc_ = slice(pp * 128 + hf * 64, pp * 128 + hf * 64 + 64)
                        sbt, sbtR = Sbf2[d][hi], Sbf2R[d][hi]
                        act(sbt[:, :], St[:, :], AF.Identity, [StR, scalR], [sbtR], scale=scal[:, d, 0, c:c + 1])
                        mm(ops_[:, hf * 64:hf * 64 + 64], sbt[:, :], qt_[:, tc_], False, hi == 1, [sbtR, qtR], [opsR])
                        mm(ubank[:, 0:128], ktok[hf * 64:hf * 64 + 64, pp, :], vtok[hf * 64:hf * 64 + 64, pp, :],
                           True, True, [ktokR, vtokR], [ubR])
                        ts('dve', St[:, :], St[:, :], scal[:, d, 1, c:c + 1], None, ALU.mult, None, [StR, scalR], [StR])
                        stt(St[:, :], ubank[:, 0:128], scal[:, d, 2, c:c + 1], St[:, :], ALU.mult, ALU.add,
                            [ubR, StR, scalR], [StR])
                    if pp not in owritten:
                        owritten.add(pp)
                        cp('act', OACC[:, tk], ops_[:, 0:128], [opsR], [OPR[pp]])
                    else:
                        tt('dve', OACC[:, tk], ops_[:, 0:128], OACC[:, tk], ALU.add, [opsR, OPR[pp]], [OPR[pp]])
                    yield

            gens = [chunk_gen(0), chunk_gen(1)]
            for _ in range(18):
                for g_ in gens:
                    next(g_)
            for bi, (t0, n) in enumerate(BLK):
                if bi == 0 and not with_ctx:
                    continue
                zs, zsR = gate_zs(l, 72 + h, t0, n)
                ps2, ps2R = PS[6], PSR[6]
                oprs = OPR[t0 // 128:(t0 + n) // 128]
                i_ = cnt['sq'] % 2
                cnt['sq'] += 1
                act(sqb[i_][:, 0:n], OACC[:, t0:t0 + n], AF.Square, oprs, [sqbR[i_]])
                mm(ps2[:, 0:n], onesb[:, :], sqb[i_][:, 0:n], True, True, [sqbR[i_], constR], [ps2R])
                rs, rsR = rstd_from_ps(ps2, ps2R, n, 1.0 / 128)
                o, oR = newtmp()
                stt(o[:, 0:n], OACC[:, t0:t0 + n], qkn[:, l, 2:3], rs[:, 0:n], ALU.mult, ALU.mult,
                    oprs + [rsR, ldR], [oR])
                store_g(o, oR, n, zs, zsR, 0, 16 + h, t0)

    def phase3(b, l, src, last):
        for bi, (t0, n) in enumerate(BLK):
            if last and bi == 0:
                continue
            for c in range(24):
                dma(Gblk[:, c, 0:n], G[c, :, t0:t0 + n], [gR[c]], [GblkR[c]])
            col = NB if bi == 0 else b
            for j in range(16):
                ua, uaR = newtmp()
                for x in range(3):
                    wb, wbR = load_w(w_br[l, x, j], 8)
                    yps, ypsR = newps()
                    proj_fm(wb, wbR, 8, lambda kc, x=x: (Gblk[:, 8 * x + kc, 0:n], GblkR[8 * x + kc]), yps, ypsR, n)
                    wb2, wb2R = load_w(w_in[l, 80 + 16 * x + j], 16)
                    bps, bpsR = newps()
                    proj_fm(wb2, wb2R, 16, h_rhs(t0, n), bps, bpsR, n)
                    sg, sgR = newtmp()
                    act(sg[:, 0:n], bps[:, 0:n], AF.Sigmoid, [bpsR], [sgR])
                    if x == 0:
                        tt('dve', ua[:, 0:n], yps[:, 0:n], sg[:, 0:n], ALU.mult, [ypsR, sgR], [uaR])
                    else:
                        tt('dve', sg[:, 0:n], yps[:, 0:n], sg[:, 0:n], ALU.mult, [ypsR, sgR], [sgR])
                        if x == 1:
                            tt('pool', ua[:, 0:n], ua[:, 0:n], sg[:, 0:n], ALU.add, [uaR, sgR], [uaR])
                        else:
                            tt('pool', Ublk[:, j, 0:n], ua[:, 0:n], sg[:, 0:n], ALU.add, [uaR, sgR], [UblkR[j]])
            for j in range(16):
                wb, wbR = load_w(w_out[l, j], 16)
                ops_, opsR = newps()
                proj_fm(wb, wbR, 16, lambda kc: (Ublk[:, kc, 0:n], UblkR[kc]), ops_, opsR, n)
                xt, xtR = newtmp()
                dma(xt[:, 0:n], src[b, j, :, t0:t0 + n], [xsR[j]], [xtR])
                stt(xt[:, 0:n], ops_[:, 0:n], mod[:, l, col, 32 + j:33 + j], xt[:, 0:n], ALU.mult, ALU.add,
                    [opsR, xtR, modR], [xtR])
                store(XS[b, j, :, t0:t0 + n], xt[:, 0:n], [xtR], [xsR[j]])
            flush_stores()

    xsR = [Res() for _ in range(16)]
    gR = [Res() for _ in range(24)]
    outR = Res()
    for b in range(NB):
        for l in range(LAYERS):
            src = xT if l == 0 else XS
            last = (l == L - 1)
            norm_phase(b, l, src, False)
            if dbg and b == 0 and l == 0:
                for kc in range(16):
                    dma(dbg_h[kc], hT[:, kc, :], [hR[kc]], [outR])
            if stop_after == 'norm':
                break
            mixer_attn(l, 'a', not last)
            mixer_attn(l, 'b', not last)
            mixer_c(l, not last)
            bar()
            if dbg and b == 0 and l == 0:
                dma(dbg_g, G, gR, [outR])
            phase3(b, l, src, last)
            if last:
                pass
            bar()
            if dbg and b == 0 and l == 0:
                dma(dbg_x, XS[0], xsR, [outR])
        if LAYERS == L and stop_after is None:
            norm_phase(b, L - 1, XS, True)
            bar()
    if dbg:
        dma(dbg_mod, mod[:].rearrange('p a b c -> p (a b c)'), [modR], [outR])
    bar()
    S.emit(nc)
    build.sbuf_left = nc.sbuf_bytes_remaining
    return nc


def _retile(w, kc):
    K, N = w.shape
    return np.ascontiguousarray(w.reshape(kc, 128, N // 128, 128).transpose(2, 1, 0, 3))


def _vec128(v):
    n = v.shape[-1] // 128
    a = v.reshape(v.shape[:-1] + (n, 128))
    return np.ascontiguousarray(np.moveaxis(a, -1, 0))


def host_prep(inp, NB, cores, LW=L):
    f = np.float32
    shared = {}
    ada_w = np.asarray(inp['ada_w'][:LW], f)
    shared['ada_w'] = np.ascontiguousarray(
        ada_w.reshape(LW, 16, 128, 16, 384).transpose(0, 3, 2, 1, 4))
    shared['ada_b'] = np.ascontiguousarray(np.asarray(inp['ada_b'], f).reshape(L, 48, 128).transpose(2, 0, 1))
    shared['norm_g'] = np.ascontiguousarray(np.asarray(inp['norm_g'], f).reshape(L, 16, 128).transpose(2, 0, 1))
    shared['fnorm_g'] = np.ascontiguousarray(np.asarray(inp['final_norm_g'], f).reshape(16, 128).T)
    shared['w_in'] = np.stack([_retile(np.asarray(inp['w_in'][l], f), 16) for l in range(LW)])
    shared['w_br'] = np.stack([np.stack([_retile(np.asarray(inp[k][l], f), 8) for k in
                                         ('w_branch_a', 'w_branch_b', 'w_branch_c')]) for l in range(LW)])
    shared['w_out'] = np.stack([_retile(np.asarray(inp['w_out'][l], f), 16) for l in range(LW)])
    shared['sink'] = np.ascontiguousarray(np.broadcast_to(np.asarray(inp['a_sink'], f)[None], (128, L, 8)))
    qkn = np.stack([np.asarray(inp['b_q_norm'], f), np.asarray(inp['b_k_norm'], f),
                    np.asarray(inp['c_out_norm'], f)], axis=-1)
    shared['qkn'] = np.ascontiguousarray(qkn.transpose(1, 0, 2))
    clb = np.asarray(inp['c_lower_bound'], f).reshape(L, 2, 8, 128)
    shared['clb'] = np.ascontiguousarray(clb.transpose(3, 1, 2, 0))
    row = np.repeat(np.arange(NL // 64, dtype=f), 64)
    colv = np.tile(np.arange(64, dtype=f), NL // 64)
    inv_freq = (np.float32(10000.0) ** (-np.arange(32, dtype=f) / np.float32(32))).astype(f)
    ang = np.concatenate([row[:, None] * inv_freq, colv[:, None] * inv_freq], axis=-1).astype(f)
    ang = np.concatenate([ang, ang], axis=-1)
    cosT = np.cos(ang).astype(f).T
    sinT = np.sin(ang).astype(f).T.copy()
    sinT[64:] *= -1.0
    shared['rope'] = np.ascontiguousarray(np.stack([cosT, sinT], axis=1))
    kk = np.arange(128)[:, None]
    qq = np.arange(128)[None, :]
    same = (kk // 64) == (qq // 64)
    masks = np.stack([(same & (kk <= qq)), (same & (kk >= qq)), (kk >= qq), (kk <= qq), (kk == qq)], axis=1).astype(f)
    BIG = np.float32(1e30)
    ext = np.stack([BIG * masks[:, 0], -BIG * masks[:, 0], BIG * masks[:, 1], -BIG * masks[:, 1]], axis=1).astype(f)
    shared['masks'] = np.ascontiguousarray(np.concatenate([masks, ext], axis=1))
    m64 = np.ones((128, 512), f)
    m64[:, ::64] = 0.0
    shared['m64'] = m64
    maps = []
    for ci in range(cores):
        bs = list(range(ci * NB, (ci + 1) * NB))
        xs = []
        for b_ in bs:
            full = np.concatenate([np.asarray(inp['ctx'][b_], f), np.asarray(inp['x'][b_], f)], axis=0)
            xs.append(np.ascontiguousarray(full.T.reshape(16, 128, T)))
        m = dict(shared)
        m['xT'] = np.stack(xs)
        cc = np.stack([np.asarray(inp['c'][b_], f) for b_ in bs] + [np.asarray(inp['c_ctx'], f)], axis=-1)
        m['cT'] = np.ascontiguousarray(cc.reshape(16, 128, NB + 1).transpose(1, 0, 2))
        maps.append(m)
    return maps


def kernel(**inputs):
    B = inputs['x'].shape[0]
    cores = NCORES
    NB = B // cores
    nc = build(NB)
    maps = host_prep(inputs, NB, cores)
    res = run_bass_kernel_spmd(nc, maps, core_ids=list(range(cores)))
    out = np.empty((B, NL, D), np.float32)
    for ci in range(cores):
        o = res.results[ci]['outT']
        for i in range(NB):
            out[ci * NB + i] = o[i].reshape(D, NL).T
    return out
```

```python
import numpy as np
import concourse.bass as bass
import concourse.mybir as mybir
from concourse.bass_utils import run_bass_kernel_spmd

F32 = mybir.dt.float32
BF16 = mybir.dt.bfloat16
AF = mybir.ActivationFunctionType
ALU = mybir.AluOpType

L = 4
D = 2048
T = 2304
MC = 256
NL = 2048
BLK = [(0, 256), (256, 512), (768, 512), (1280, 512), (1792, 512)]
EPS = 1e-6
ENGS = ['pe', 'act', 'dve', 'pool', 'sp']
NCORES = 8


class Res:
    __slots__ = ('w', 'r')

    def __init__(self):
        self.w = None
        self.r = {}


class Sched:
    NDMA = 8

    def __init__(self):
        self.ops = {e: [] for e in ENGS}
        self.clock = {e: {} for e in ENGS}
        self.dma_n = 0
        self.dma_cnt = [0] * self.NDMA

    @staticmethod
    def _kv(tok):
        if tok[0] == 'dma':
            return ('dma', tok[1]), tok[2]
        return tok[0], tok[1]

    def op(self, eng, fn, reads=(), writes=(), dma=False):
        idx = len(self.ops[eng])
        deps = []
        for t in reads:
            if t.w is not None:
                deps.append((t.w, True))
        for t in writes:
            if t.w is not None:
                deps.append((t.w, False))
            for k, v in t.r.items():
                deps.append(((('dma', k[1], v) if k[0] == 'dma' else (k, v)), False))
        clk = self.clock[eng]
        if dma:
            slot = self.dma_n % self.NDMA
            self.dma_n += 1
            if self.dma_cnt[slot] > 0:
                deps.append((('dma', slot, self.dma_cnt[slot]), False))
            self.dma_cnt[slot] += 1
            mytok = ('dma', slot, self.dma_cnt[slot])
        else:
            mytok = (eng, idx)
        best = {}
        for tok, raw in deps:
            key, val = self._kv(tok)
            if key == eng and eng in ('pe', 'sp'):
                continue
            if clk.get(key, -1) >= val:
                continue
            clk[key] = val
            best[key] = tok
            if tok[0] != 'dma':
                o = self.ops[tok[0]][tok[1]]
                o['sig'] = True
                for k2, v2 in o['clk'].items():
                    if clk.get(k2, -1) < v2:
                        clk[k2] = v2
        snap = dict(clk)
        if not dma:
            snap[eng] = idx
        rec = dict(fn=fn, waits=list(best.values()), sig=False, clk=snap, tok=mytok, dma=dma)
        self.ops[eng].append(rec)
        mk, mv = self._kv(mytok)
        for t in reads:
            if t.r.get(mk, -1) < mv:
                t.r[mk] = mv
        for t in writes:
            t.w = mytok
            t.r = {}
        return mytok

    def barrier(self):
        rs = []
        for e in ENGS:
            if e != 'sp' and self.ops[e]:
                r = Res()
                r.w = (e, len(self.ops[e]) - 1)
                rs.append(r)
        for s in range(self.NDMA):
            if self.dma_cnt[s] > 0:
                r = Res()
                r.w = ('dma', s, self.dma_cnt[s])
                rs.append(r)
        for e in ENGS:
            self.op(e, lambda g: g.nop(), reads=rs)

    def emit(self, nc):
        sems = {e: nc.alloc_semaphore('s_' + e) for e in ENGS if e != 'sp'}
        dsems = [nc.alloc_semaphore('d_%d' % i) for i in range(self.NDMA)]
        cnt = {}
        for e in ENGS:
            c = 0
            arr = []
            for o in self.ops[e]:
                if o['sig'] and not o['dma']:
                    c += 1
                arr.append(c)
            cnt[e] = arr

        def emit_engine(ename, eng):
            for o in self.ops[ename]:
                for tok in o['waits']:
                    if tok[0] == 'dma':
                        eng.wait_ge(dsems[tok[1]], 16 * tok[2])
                    else:
                        eng.wait_ge(sems[tok[0]], cnt[tok[0]][tok[1]])
                ins = o['fn'](eng)
                if o['dma']:
                    ins.then_inc(dsems[o['tok'][1]], 16)
                elif o['sig']:
                    ins.then_inc(sems[ename], 1)

        with nc.Block() as block:
            @block.tensor
            def _(e):
                emit_engine('pe', e)

            @block.scalar
            def _(e):
                emit_engine('act', e)

            @block.vector
            def _(e):
                emit_engine('dve', e)

            @block.gpsimd
            def _(e):
                emit_engine('pool', e)

            @block.sync
            def _(e):
                emit_engine('sp', e)


def build(NB, LAYERS=L, dbg=False, stop_after=None):
    nc = bass.Bass('TRN2', target_bir_lowering=False, dynamic_dma_scratch_size=2048)
    S = Sched()

    def din(name, shape, dt=F32):
        return nc.dram_tensor(name, list(shape), dt, kind='ExternalInput').ap()

    xT = din('xT', [NB, 16, 128, T])
    cT = din('cT', [128, 16, NB + 1])
    ada_w = din('ada_w', [LAYERS, 16, 128, 16, 384])
    ada_b = din('ada_b', [128, L, 48])
    norm_g = din('norm_g', [128, L, 16])
    fnorm_g = din('fnorm_g', [128, 16])
    w_in = din('w_in', [LAYERS, 128, 128, 16, 128])
    w_br = din('w_br', [LAYERS, 3, 16, 128, 8, 128])
    w_out = din('w_out', [LAYERS, 16, 128, 16, 128])
    sink_in = din('sink', [128, L, 8])
    qkn_in = din('qkn', [128, L, 3])
    clb_in = din('clb', [128, 2, 8, L])
    rope_in = din('rope', [128, 2, NL])
    masks_in = din('masks', [128, 9, 128])
    m64_in = din('m64', [128, 512])
    outT = nc.dram_tensor('outT', [NB, 16, 128, NL], F32, kind='ExternalOutput').ap()
    XS = nc.dram_tensor('XS', [NB, 16, 128, T], F32).ap()
    G = nc.dram_tensor('G', [24, 128, T], BF16).ap()
    if dbg:
        dbg_h = nc.dram_tensor('dbg_h', [16, 128, T], BF16, kind='ExternalOutput').ap()
        dbg_g = nc.dram_tensor('dbg_g', [24, 128, T], BF16, kind='ExternalOutput').ap()
        dbg_x = nc.dram_tensor('dbg_x', [16, 128, T], F32, kind='ExternalOutput').ap()
        dbg_mod = nc.dram_tensor('dbg_mod', [128, L * (NB + 1) * 48], F32, kind='ExternalOutput').ap()

    def sb(name, shape, dt=F32):
        return nc.alloc_sbuf_tensor('sb_' + name, list(shape), dt)

    hT = sb('hT', [128, 16, T], BF16)
    hR = [Res() for _ in range(16)]
    arena = sb('arena', [128, 26624], BF16)
    wst = [sb('wst%d' % i, [128, 16, 128]) for i in range(3)]
    wstR = [Res() for _ in range(3)]
    wbf = [sb('wbf%d' % i, [128, 16, 128], BF16) for i in range(3)]
    wbfR = [Res() for _ in range(3)]
    rope = sb('rope', [128, 2, NL])
    ropeR = Res()
    NTMP = 8
    tmp = [sb('tmp%d' % i, [128, 512]) for i in range(NTMP)]
    tmpR = [Res() for _ in range(NTMP)]
    mod = sb('mod', [128, L, NB + 1, 48])
    modR = Res()
    gp = sb('gp', [128, 2, 16])
    gpR = Res()
    ng = sb('ng', [128, L, 16])
    fg = sb('fg', [128, 16])
    adab = sb('adab', [128, L, 48])
    cs = sb('cs', [128, 16, NB + 1])
    csig = sb('csig', [128, 16, NB + 1])
    sinkE = sb('sinkE', [128, L, 8])
    qkn = sb('qkn_s', [128, L, 3])
    clb = sb('clb_s', [128, 2, 8, L])
    lbt = sb('lbt', [128, 2, 8, L])
    oml = sb('oml', [128, 2, 8, L])
    noml = sb('noml', [128, 2, 8, L])
    lbtmp = sb('lbtmp', [128, 2, 8])
    masks = sb('masks_s', [128, 9, 128])
    m64 = sb('m64_s', [128, 512])
    identb = sb('identb', [128, 128], BF16)
    onesb = sb('onesb', [128, 128], BF16)
    scal = sb('scal', [128, 2, 3, 36])
    scalR = Res()
    stmp = sb('stmp', [128, 16])
    stmpR = Res()
    Sst = sb('Sst', [128, 128])
    SstR = Res()
    Sst2 = [Sst, sb('Sst1', [128, 128])]
    Sst2R = [SstR, Res()]
    Sbf = [sb('Sbf%d' % i, [128, 128], BF16) for i in range(4)]
    Sbf2 = [[Sbf[0], Sbf[1]], [Sbf[2], Sbf[3]]]
    Sbf2R = [[Res(), Res()], [Res(), Res()]]
    OPR = [Res() for _ in range(18)]
    aTs = [sb('aTs%d' % i, [128, 128], BF16) for i in range(2)]
    aTsR = [Res(), Res()]
    pT = [sb('pT%d' % i, [128, 512], BF16) for i in range(3)]
    pTR = [Res() for _ in range(3)]
    gout = [sb('gout%d' % i, [128, 512], BF16) for i in range(2)]
    goutR = [Res(), Res()]
    sqb = [sb('sqb%d' % i, [128, 512], BF16) for i in range(2)]
    sqbR = [Res(), Res()]
    constR = Res()
    rsT = [sb('rsT%d' % i, [128, 512]) for i in range(2)]
    rsTR = [Res(), Res()]
    zsT = [sb('zsT%d' % i, [128, 512]) for i in range(2)]
    zsTR = [Res(), Res()]

    PS = [nc.alloc_psum_tensor('ps%d' % i, [128, 512], F32) for i in range(7)]
    PSR = [Res() for _ in range(7)]
    PSB = nc.alloc_psum_tensor('psb', [128, 512], BF16)
    PSBR = Res()

    def av(off, shape, dt=BF16):
        n = 1
        for s_ in shape:
            n *= s_
        if dt == F32:
            a = arena[:, off:off + 2 * n].bitcast(F32)
        else:
            a = arena[:, off:off + n]
        if len(shape) == 2:
            return a.rearrange('p (a b) -> p a b', b=shape[1])
        return a

    A_ = [av(i * T, [T]) for i in range(4)]
    AR = [Res() for _ in range(4)]
    V_ = [av(4 * T + i * T, [18, 128]) for i in range(3)]
    VR = [Res() for _ in range(3)]
    QF = av(7 * T, [T], F32)
    QFR = Res()
    OACC = av(9 * T, [T], F32)
    OACCR = Res()
    assert 11 * T <= 26624
    Gblk = av(0, [24, 512])
    GblkR = [Res() for _ in range(24)]
    Ublk = av(24 * 512, [16, 512])
    UblkR = [Res() for _ in range(16)]
    ADA = [av(i * 12288, [16, 384], F32) for i in range(2)]
    ADAR = [Res(), Res()]

    cnt = {'w': 0, 't': 0, 'ps': 0, 'pt': 0, 'g': 0, 'sq': 0, 'rs': 0, 'zs': 0}

    def newtmp():
        i = cnt['t'] % NTMP
        cnt['t'] += 1
        return tmp[i], tmpR[i]

    def newps():
        i = cnt['ps'] % 2
        cnt['ps'] += 1
        return PS[i], PSR[i]

    def dma(out, in_, reads, writes, q='sp'):
        S.op(q, lambda e, o=out, i=in_: e.dma_start(out=o, in_=i), reads=reads, writes=writes, dma=True)

    deferred = []

    def flush_stores(keep=0):
        while len(deferred) > keep:
            o, i, r, w = deferred.pop(0)
            dma(o, i, r, w, q='act')

    def store(out, in_, reads, writes):
        deferred.append((out, in_, reads, writes))
        flush_stores(keep=1)

    def bar():
        flush_stores()
        S.barrier()

    def mm(out, lhsT, rhs, start, stop, reads, writes):
        S.op('pe', lambda e, o=out, a=lhsT, b=rhs, s0=start, s1=stop: e.matmul(o, a, b, start=s0, stop=s1),
             reads=reads, writes=writes)

    def act(out, in_, func, reads, writes, bias=None, scale=None):
        kw = {}
        if bias is not None:
            kw['bias'] = bias
        if scale is not None:
            kw['scale'] = scale
        S.op('act', lambda e, o=out, i=in_, f=func, k=kw: e.activation(o, i, f, **k), reads=reads, writes=writes)

    def tt(eng, out, in0, in1, op, reads, writes):
        S.op(eng, lambda e, o=out, a=in0, b=in1, p=op: e.tensor_tensor(o, a, b, p), reads=reads, writes=writes)

    def ts(eng, out, in0, s1, s2, op0, op1, reads, writes):
        if op1 is None:
            S.op(eng, lambda e, o=out, a=in0, x=s1, p0=op0: e.tensor_scalar(o, a, x, None, p0),
                 reads=reads, writes=writes)
        else:
            S.op(eng, lambda e, o=out, a=in0, x=s1, y=s2, p0=op0, p1=op1: e.tensor_scalar(o, a, x, y, p0, p1),
                 reads=reads, writes=writes)

    def stt(out, in0, scalar, in1, op0, op1, reads, writes):
        S.op('dve', lambda e, o=out, a=in0, s_=scalar, b=in1, p0=op0, p1=op1:
             e.scalar_tensor_tensor(o, a, s_, b, p0, p1), reads=reads, writes=writes)

    def cp(eng, out, in_, reads, writes):
        if eng == 'act':
            S.op(eng, lambda e, o=out, i=in_: e.activation(o, i, AF.Identity), reads=reads, writes=writes)
        else:
            S.op(eng, lambda e, o=out, i=in_: e.tensor_copy(o, i), reads=reads, writes=writes)

    def load_w(src, nkc):
        i = cnt['w']
        cnt['w'] += 1
        st, stR = wst[i % 3], wstR[i % 3]
        wb, wbR = wbf[i % 3], wbfR[i % 3]
        dma(st[:, 0:nkc, :], src, [], [stR])
        cp('dve', wb[:, 0:nkc, :], st[:, 0:nkc, :], [stR], [wbR])
        return wb, wbR

    def proj_fm(wb, wbR, nkc, rhs_fn, ps, psR, n):
        for kc in range(nkc):
            r_ap, r_res = rhs_fn(kc)
            mm(ps[:, 0:n], wb[:, kc, :], r_ap, kc == 0, kc == nkc - 1, [wbR, r_res], [psR])

    def h_rhs(t0, n):
        return lambda kc: (hT[:, kc, t0:t0 + n], hR[kc])

    def rstd_from_ps(ps, psR, n, inv_n):
        t1, t1R = newtmp()
        act(t1[:, 0:n], ps[:, 0:n], AF.Ln, [psR, constR], [t1R], bias=epsT[:, 0:1], scale=inv_n)
        i = cnt['rs'] % 2
        cnt['rs'] += 1
        t2, t2R = rsT[i], rsTR[i]
        act(t2[:, 0:n], t1[:, 0:n], AF.Exp, [t1R], [t2R], scale=-0.5)
        return t2, t2R

    def colsumsq(src_ap, srcR, ps, psR, n, first, last):
        i = cnt['sq'] % 2
        cnt['sq'] += 1
        act(sqb[i][:, 0:n], src_ap, AF.Square, [srcR], [sqbR[i]])
        mm(ps[:, 0:n], onesb[:, :], sqb[i][:, 0:n], first, last, [sqbR[i], constR], [psR])

    epsT = sb('epsT', [128, 1])
    S.op('dve', lambda e: e.memset(epsT[:], EPS), writes=[constR])
    S.op('dve', lambda e: e.memset(onesb[:], 1.0), writes=[constR])
    ldR = Res()
    for dst, src in [(ng[:], norm_g), (fg[:], fnorm_g), (adab[:], ada_b), (cs[:], cT), (sinkE[:], sink_in),
                     (qkn[:], qkn_in), (clb[:], clb_in), (rope[:], rope_in), (masks[:], masks_in), (m64[:], m64_in)]:
        dma(dst, src, [], [ldR])
    cp('dve', identb[:], masks[:, 4, :], [ldR], [constR])
    act(csig[:], cs[:], AF.Sigmoid, [ldR], [constR])
    tt('dve', cs[:], cs[:], csig[:], ALU.mult, [constR, ldR], [constR])
    act(sinkE[:], sinkE[:], AF.Exp, [ldR], [constR])
    act(clb[:], clb[:], AF.Exp, [ldR], [constR])
    S.op('dve', lambda e: e.tensor_reduce(lbtmp[:], clb[:], mybir.AxisListType.X, ALU.add), reads=[constR], writes=[constR])
    S.op('dve', lambda e: e.reciprocal(lbtmp[:], lbtmp[:]), reads=[constR], writes=[constR])
    S.op('dve', lambda e: e.memset(lbt[:], 0.0), writes=[constR])
    for l in range(1, L):
        tt('dve', clb[:, :, :, l], clb[:, :, :, l], lbtmp[:], ALU.mult, [constR], [constR])
        tt('dve', lbt[:, :, :, l], lbt[:, :, :, l - 1], clb[:, :, :, l], ALU.add, [constR], [constR])
    ts('dve', oml[:], lbt[:], -1.0, 1.0, ALU.mult, ALU.add, [constR], [constR])
    ts('dve', noml[:], oml[:], -1.0, None, ALU.mult, None, [constR], [constR])

    nslab = 0
    for l in range(LAYERS):
        for s_ in range(16):
            a, aR = ADA[nslab % 2], ADAR[nslab % 2]
            nslab += 1
            dma(a, ada_w[l, s_], [], [aR])
            for jj in range(3):
                j = 3 * s_ + jj
                ps, psR = newps()
                for kc in range(16):
                    mm(ps[:, 0:NB + 1], a[:, kc, jj * 128:(jj + 1) * 128], cs[:, kc, :], kc == 0, kc == 15,
                       [aR, constR], [psR])
                ts('dve', mod[:, l, :, j], ps[:, 0:NB + 1], adab[:, l, j:j + 1], None, ALU.add, None,
                   [psR, ldR], [modR])
    bar()

    def norm_phase(b, l, src, final):
        if not final:
            for ci, col in enumerate((b, NB)):
                ts('dve', gp[:, ci, :], mod[:, l, col, 16:32], 1.0, None, ALU.add, None, [modR], [gpR])
                tt('dve', gp[:, ci, :], gp[:, ci, :], ng[:, l, :], ALU.mult, [gpR, ldR], [gpR])
        for bi, (t0, n) in enumerate(BLK):
            if final and bi == 0:
                continue
            ps, psR = newps()
            for kc in range(16):
                xt, xtR = newtmp()
                dma(xt[:, 0:n], src[b, kc, :, t0:t0 + n], [xsR[kc]], [xtR])
                colsumsq(xt[:, 0:n], xtR, ps, psR, n, kc == 0, kc == 15)
            rs, rsR = rstd_from_ps(ps, psR, n, 1.0 / D)
            ci = 1 if bi == 0 else 0
            col = NB if bi == 0 else b
            dbgn = dbg and b == 0 and l == 0 and bi == 1 and not final
            if dbgn:
                d1 = nc.dram_tensor('dbg_rs', [128, 512], F32, kind='ExternalOutput').ap()
                dma(d1, rs[:, :], [rsR], [outR])
                d3 = nc.dram_tensor('dbg_gp', [128, 32], F32, kind='ExternalOutput').ap()
                dma(d3, gp[:].rearrange('p a b -> p (a b)'), [gpR], [outR])
                d4 = nc.dram_tensor('dbg_ng', [128, 64], F32, kind='ExternalOutput').ap()
                dma(d4, ng[:].rearrange('p a b -> p (a b)'), [ldR], [outR])
            for kc in range(16):
                xt, xtR = newtmp()
                dma(xt[:, 0:n], src[b, kc, :, t0:t0 + n], [xsR[kc]], [xtR])
                if final:
                    stt(xt[:, 0:n], xt[:, 0:n], fg[:, kc:kc + 1], rs[:, 0:n], ALU.mult, ALU.mult,
                        [xtR, rsR, ldR], [xtR])
                    store(outT[b, kc, :, t0 - MC:t0 - MC + n], xt[:, 0:n], [xtR], [outR])
                    if kc == 15:
                        flush_stores()
                else:
                    if dbgn and kc == 0:
                        d5 = nc.dram_tensor('dbg_x0', [128, 512], F32, kind='ExternalOutput').ap()
                        dma(d5, xt[:, :], [xtR], [outR])
                    tt('dve', xt[:, 0:n], xt[:, 0:n], rs[:, 0:n], ALU.mult, [xtR, rsR], [xtR])
                    if dbgn and kc == 0:
                        d6 = nc.dram_tensor('dbg_x1', [128, 512], F32, kind='ExternalOutput').ap()
                        dma(d6, xt[:, :], [xtR], [outR])
                    act(hT[:, kc, t0:t0 + n], xt[:, 0:n], AF.Identity, [xtR, gpR, modR], [hR[kc]],
                        bias=mod[:, l, col, kc:kc + 1], scale=gp[:, ci, kc:kc + 1])

    def rope_apply(src, srcR, dst_ap, dstR, t0, n):
        p0 = t0 - MC
        t1, t1R = newtmp()
        tt('pool', t1[:, 0:n], src[:, 0:n], rope[:, 0, p0:p0 + n], ALU.mult, [srcR, ldR], [t1R])
        t2, t2R = newtmp()
        tt('dve', t2[0:64, 0:n], src[64:128, 0:n], rope[64:128, 1, p0:p0 + n], ALU.mult, [srcR, ldR], [t2R])
        tt('dve', t2[64:128, 0:n], src[0:64, 0:n], rope[0:64, 1, p0:p0 + n], ALU.mult, [srcR, ldR], [t2R])
        tt('pool', dst_ap, t1[:, 0:n], t2[:, 0:n], ALU.add, [t1R, t2R], [dstR])

    def qk_proj(l, j, dstA, dstR, norm_col):
        wb, wbR = load_w(w_in[l, j], 16)
        for bi, (t0, n) in enumerate(BLK):
            ps, psR = newps()
            proj_fm(wb, wbR, 16, h_rhs(t0, n), ps, psR, n)
            q, qR = newtmp()
            if norm_col is None:
                cp('act', q[:, 0:n], ps[:, 0:n], [psR], [qR])
            else:
                cp('act', q[:, 0:n], ps[:, 0:n], [psR], [qR])
                ps2, ps2R = PS[6], PSR[6]
                colsumsq(q[:, 0:n], qR, ps2, ps2R, n, True, True)
                rs, rsR = rstd_from_ps(ps2, ps2R, n, 1.0 / 128)
                stt(q[:, 0:n], q[:, 0:n], qkn[:, l, norm_col:norm_col + 1], rs[:, 0:n], ALU.mult, ALU.mult,
                    [qR, rsR, ldR], [qR])
            if bi == 0:
                cp('pool', dstA[:, t0:t0 + n], q[:, 0:n], [qR], [dstR])
            else:
                rope_apply(q, qR, dstA[:, t0:t0 + n], dstR, t0, n)

    def v_proj(l, j, dstV, dstR):
        wb, wbR = load_w(w_in[l, j], 16)
        for grp in range(5):
            tts = list(range(grp * 4, min(18, grp * 4 + 4)))
            ps, psR = newps()
            for ii, tti in enumerate(tts):
                for kc in range(16):
                    mm(ps[:, ii * 128:(ii + 1) * 128], hT[:, kc, tti * 128:(tti + 1) * 128], wb[:, kc, :],
                       kc == 0, kc == 15, [wbR, hR[kc]], [psR])
            nn = len(tts) * 128
            cp('act', dstV[:, tts[0]:tts[0] + len(tts), :], ps[:, 0:nn].rearrange('p (a b) -> p a b', b=128),
               [psR], [dstR])

    def gate_zs(l, jz, t0, n):
        wb, wbR = load_w(w_in[l, jz], 16)
        ps, psR = newps()
        proj_fm(wb, wbR, 16, h_rhs(t0, n), ps, psR, n)
        i = cnt['zs'] % 2
        cnt['zs'] += 1
        sg, sgR = zsT[i], zsTR[i]
        act(sg[:, 0:n], ps[:, 0:n], AF.Sigmoid, [psR], [sgR])
        tt('dve', sg[:, 0:n], ps[:, 0:n], sg[:, 0:n], ALU.mult, [psR, sgR], [sgR])
        return sg, sgR

    def store_g(src, srcR, n, zs, zsR, zoff, gchunk, t0):
        i = cnt['g'] % 2
        cnt['g'] += 1
        tt('pool', gout[i][:, 0:n], src[:, 0:n], zs[:, zoff:zoff + n], ALU.mult, [srcR, zsR], [goutR[i]])
        store(G[gchunk, :, t0:t0 + n], gout[i][:, 0:n], [goutR[i]], [gR[gchunk]])

    def attend(KT, KR, VT, VTR, QT, QR, q0, qn, klist, sink_ap, zs, zsR, zoff, gchunk):
        oT, oTR = PS[4], PSR[4]
        dn, dnR = PS[5], PSR[5]
        nk = len(klist)

        def issue_s(idx):
            kt = klist[idx][0]
            sT, sTR = PS[2 + idx % 2], PSR[2 + idx % 2]
            mm(sT[:, 0:qn], KT[:, kt * 128:(kt + 1) * 128], QT[:, q0:q0 + qn], True, True, [KR, QR], [sTR])

        issue_s(0)
        for idx, (kt, mk) in enumerate(klist):
            sT, sTR = PS[2 + idx % 2], PSR[2 + idx % 2]
            pi = cnt['pt'] % 3
            cnt['pt'] += 1
            act(pT[pi][:, 0:qn], sT[:, 0:qn], AF.Exp, [sTR], [pTR[pi]], scale=128 ** -0.5)
            if idx + 1 < nk:
                issue_s(idx + 1)
            if mk is not None:
                tt('dve', pT[pi][:, 0:qn], pT[pi][:, 0:qn], masks[:, mk, 0:qn], ALU.mult, [pTR[pi], ldR], [pTR[pi]])
            mm(oT[:, 0:qn], VT[:, kt, :], pT[pi][:, 0:qn], idx == 0, idx == nk - 1, [VTR, pTR[pi]], [oTR])
            mm(dn[:, 0:qn], onesb[:, :], pT[pi][:, 0:qn], idx == 0, idx == nk - 1, [constR, pTR[pi]], [dnR])
        r, rR = newtmp()
        if sink_ap is not None:
            ts('dve', r[:, 0:qn], dn[:, 0:qn], sink_ap, None, ALU.add, None, [dnR, constR], [rR])
            S.op('dve', lambda e, a=r[:, 0:qn]: e.reciprocal(a, a), reads=[rR], writes=[rR])
        else:
            S.op('dve', lambda e, a=r[:, 0:qn], d=dn[:, 0:qn]: e.reciprocal(a, d), reads=[dnR], writes=[rR])
        tt('dve', r[:, 0:qn], oT[:, 0:qn], r[:, 0:qn], ALU.mult, [oTR, rR], [rR])
        store_g(r, rR, qn, zs, zsR, zoff, gchunk, q0)

    def mixer_attn(l, which, with_ctx):
        base = 0 if which == 'a' else 20
        for g in range(2):
            KT, KR = A_[1], AR[1]
            VT, VTR = V_[0], VR[0]
            qk_proj(l, base + 8 + g, KT, KR, None if which == 'a' else 1)
            v_proj(l, base + 10 + g, VT, VTR)
            for i in range(4):
                h = 4 * g + i
                QT, QR = A_[0], AR[0]
                qk_proj(l, base + h, QT, QR, None if which == 'a' else 0)
                sink_ap = sinkE[:, l, h:h + 1] if which == 'a' else None
                gchunk = (0 if which == 'a' else 8) + h
                for bi, (t0, n) in enumerate(BLK):
                    if bi == 0 and not with_ctx:
                        continue
                    zs, zsR = gate_zs(l, base + 12 + h, t0, n)
                    if bi == 0:
                        attend(KT, KR, VT, VTR, QT, QR, 0, 256, [(0, None), (1, None)], sink_ap, zs, zsR, 0, gchunk)
                    elif which == 'b':
                        attend(KT, KR, VT, VTR, QT, QR, t0, n, [(k, None) for k in range(18)], sink_ap,
                               zs, zsR, 0, gchunk)
                    else:
                        for sbk in range(4):
                            nblk = (t0 - MC) // 128 + sbk
                            kl = [(0, None), (1, None)]
                            if nblk > 0:
                                kl.append((2 + nblk - 1, 2))
                            kl.append((2 + nblk, None))
                            if nblk < 15:
                                kl.append((2 + nblk + 1, 3))
                            attend(KT, KR, VT, VTR, QT, QR, t0 + sbk * 128, 128, kl, sink_ap, zs, zsR,
                                   sbk * 128, gchunk)

    def mixer_c(l, with_ctx):
        for h in range(8):
            wb, wbR = load_w(w_in[l, 40 + h], 16)
            for (t0, n) in BLK:
                ps, psR = newps()
                proj_fm(wb, wbR, 16, h_rhs(t0, n), ps, psR, n)
                act(QF[:, t0:t0 + n], ps[:, 0:n], AF.Identity, [psR], [QFR], scale=128 ** -0.5)
            v_proj(l, 64 + h, V_[0], VR[0])
            for d in range(2):
                qt_, qtR = A_[2 * d], AR[2 * d]
                kt_, ktR = A_[2 * d + 1], AR[2 * d + 1]
                wb, wbR = load_w(w_in[l, 48 + 8 * d + h], 16)
                lb_ap = lbt[:, d, h, l:l + 1]
                oml_ap = oml[:, d, h, l:l + 1]
                noml_ap = noml[:, d, h, l:l + 1]
                ref = 31 if d == 0 else 32
                for (t0, n) in BLK:
                    nch = n // 64
                    c0 = t0 // 64
                    ps, psR = newps()
                    proj_fm(wb, wbR, 16, h_rhs(t0, n), ps, psR, n)
                    r, rR = newtmp()
                    act(r[:, 0:n], ps[:, 0:n], AF.Sigmoid, [psR], [rR])
                    lf, lfR = newtmp()
                    act(lf[:, 0:n], r[:, 0:n], AF.Ln, [rR, constR], [lfR], bias=lb_ap, scale=oml_ap)
                    kk, kkR = newtmp()
                    ts('dve', kk[:, 0:n], r[:, 0:n], noml_ap, oml_ap, ALU.mult, ALU.add, [rR, constR], [kkR])
                    cm, cmR = newtmp()
                    S.op('dve', lambda e, o=cm[:, 0:n], a=m64[:, 0:n], b_=lf[:, 0:n]:
                         e.tensor_tensor_scan(o, a, b_, 0.0, ALU.mult, ALU.add), reads=[ldR, lfR], writes=[cmR])
                    cm3 = cm[:, 0:n].rearrange('p (c k) -> p c k', k=64)
                    al = scal[:, d, 0, c0:c0 + nch]
                    be = scal[:, d, 1, c0:c0 + nch]
                    ga = scal[:, d, 2, c0:c0 + nch]
                    if d == 0:
                        C3 = cm3
                        CR = cmR
                        act(al, cm3[:, :, ref], AF.Exp, [cmR], [scalR])
                        act(be, cm3[:, :, 63], AF.Exp, [cmR], [scalR])
                        tt('dve', stmp[:, 0:nch], cm3[:, :, 63], cm3[:, :, ref], ALU.subtract, [cmR], [stmpR])
                        act(ga, stmp[:, 0:nch], AF.Exp, [stmpR], [scalR])
                    else:
                        ec, ecR = newtmp()
                        tt('pool', ec[:, 0:n], cm[:, 0:n], lf[:, 0:n], ALU.subtract, [cmR, lfR], [ecR])
                        C3 = ec[:, 0:n].rearrange('p (c k) -> p c k', k=64)
                        CR = ecR
                        act(be, cm3[:, :, 63], AF.Exp, [cmR], [scalR])
                        act(ga, C3[:, :, ref], AF.Exp, [ecR], [scalR])
                        tt('dve', stmp[:, 0:nch], cm3[:, :, 63], C3[:, :, ref], ALU.subtract, [cmR, ecR], [stmpR])
                        act(al, stmp[:, 0:nch], AF.Exp, [stmpR], [scalR])
                    aa, aaR = newtmp()
                    aa3 = aa[:, 0:n].rearrange('p (c k) -> p c k', k=64)
                    tt('dve', aa3, C3, C3[:, :, ref:ref + 1].broadcast_to([128, nch, 64]), ALU.subtract, [CR], [aaR])
                    wq, wqR = newtmp()
                    act(wq[:, 0:n], aa[:, 0:n], AF.Exp, [aaR], [wqR], scale=(1.0 if d == 0 else -1.0))
                    tt('pool', qt_[:, t0:t0 + n], QF[:, t0:t0 + n], wq[:, 0:n], ALU.mult, [QFR, wqR], [qtR])
                    wk, wkR = newtmp()
                    act(wk[:, 0:n], aa[:, 0:n], AF.Exp, [aaR], [wkR], scale=(-1.0 if d == 0 else 1.0))
                    tt('pool', kt_[:, t0:t0 + n], kk[:, 0:n], wk[:, 0:n], ALU.mult, [kkR, wkR], [ktR])
                ktok, ktokR = V_[1 + d], VR[1 + d]
                for grp in range(5):
                    tts = list(range(grp * 4, min(18, grp * 4 + 4)))
                    for ii, tti in enumerate(tts):
                        S.op('pe', lambda e, o=PSB[:, ii * 128:(ii + 1) * 128], a=kt_[:, tti * 128:(tti + 1) * 128]:
                             e.transpose(o, a, identb[:, :]), reads=[ktR, constR], writes=[PSBR])
                    nn = len(tts) * 128
                    cp('act', ktok[:, tts[0]:tts[0] + len(tts), :], PSB[:, 0:nn].rearrange('p (a b) -> p a b', b=128),
                       [PSBR], [ktokR])
            owritten = set()

            def chunk_gen(d):
                qt_, qtR = A_[2 * d], AR[2 * d]
                kt_, ktR = A_[2 * d + 1], AR[2 * d + 1]
                ktok, ktokR = V_[1 + d], VR[1 + d]
                vtok, vtokR = V_[0], VR[0]
                order = list(range(18)) if d == 0 else [1, 0] + list(range(17, 1, -1))
                St, StR = Sst2[d], Sst2R[d]
                aps, apsR = (PS[2], PSR[2]) if d == 0 else (PS[3], PSR[3])
                ops_, opsR = (PS[4], PSR[4]) if d == 0 else (PS[6], PSR[6])
                ubank = PS[5] if d == 0 else PS[0]
                ubR = (PSR[5] if d == 0 else PSR[0])
                S.op('dve', lambda e, t=St: e.memset(t[:], 0.0), writes=[StR])
                for pi_, pp in enumerate(order):
                    tk = slice(pp * 128, (pp + 1) * 128)
                    mm(aps[:, 0:128], kt_[:, tk], qt_[:, tk], True, True, [ktR, qtR], [apsR])
                    mt, mtR = newtmp()
                    tt('dve', mt[:, 0:128], aps[:, 0:128], masks[:, 5 + 2 * d, :], ALU.min, [apsR, ldR], [mtR])
                    tt('dve', aTs[d][:, :], mt[:, 0:128], masks[:, 6 + 2 * d, :], ALU.max, [mtR, ldR], [aTsR[d]])
                    mm(ops_[:, 0:128], vtok[:, pp, :], aTs[d][:, :], True, False, [vtokR, aTsR[d]], [opsR])
                    halves = (0, 1) if d == 0 else (1, 0)
                    for hi, hf in enumerate(halves):
                        c = 2 * pp + hf
                        tc_ = slice(pp * 128 + hf * 64, pp * 128 + hf * 64 + 64)
                        sbt, sbtR = Sbf2[d][hi], Sbf2R[d][hi]
                        act(sbt[:, :], St[:, :], AF.Identity, [StR, scalR], [sbtR], scale=scal[:, d, 0, c:c + 1])
                        mm(ops_[:, hf * 64:hf * 64 + 64], sbt[:, :], qt_[:, tc_], False, hi == 1, [sbtR, qtR], [opsR])
                        mm(ubank[:, 0:128], ktok[hf * 64:hf * 64 + 64, pp, :], vtok[hf * 64:hf * 64 + 64, pp, :],
                           True, True, [ktokR, vtokR], [ubR])
                        ts('dve', St[:, :], St[:, :], scal[:, d, 1, c:c + 1], None, ALU.mult, None, [StR, scalR], [StR])
                        stt(St[:, :], ubank[:, 0:128], scal[:, d, 2, c:c + 1], St[:, :], ALU.mult, ALU.add,
                            [ubR, StR, scalR], [StR])
                    if pp not in owritten:
                        owritten.add(pp)
                        cp('act', OACC[:, tk], ops_[:, 0:128], [opsR], [OPR[pp]])
                    else:
                        tt('dve', OACC[:, tk], ops_[:, 0:128], OACC[:, tk], ALU.add, [opsR, OPR[pp]], [OPR[pp]])
                    yield

            gens = [chunk_gen(0), chunk_gen(1)]
            for _ in range(18):
                for g_ in gens:
                    next(g_)
            for bi, (t0, n) in enumerate(BLK):
                if bi == 0 and not with_ctx:
                    continue
                zs, zsR = gate_zs(l, 72 + h, t0, n)
                ps2, ps2R = PS[6], PSR[6]
                oprs = OPR[t0 // 128:(t0 + n) // 128]
                i_ = cnt['sq'] % 2
                cnt['sq'] += 1
                act(sqb[i_][:, 0:n], OACC[:, t0:t0 + n], AF.Square, oprs, [sqbR[i_]])
                mm(ps2[:, 0:n], onesb[:, :], sqb[i_][:, 0:n], True, True, [sqbR[i_], constR], [ps2R])
                rs, rsR = rstd_from_ps(ps2, ps2R, n, 1.0 / 128)
                o, oR = newtmp()
                stt(o[:, 0:n], OACC[:, t0:t0 + n], qkn[:, l, 2:3], rs[:, 0:n], ALU.mult, ALU.mult,
                    oprs + [rsR, ldR], [oR])
                store_g(o, oR, n, zs, zsR, 0, 16 + h, t0)

    def phase3(b, l, src, last):
        for bi, (t0, n) in enumerate(BLK):
            if last and bi == 0:
                continue
            for c in range(24):
                dma(Gblk[:, c, 0:n], G[c, :, t0:t0 + n], [gR[c]], [GblkR[c]])
            col = NB if bi == 0 else b
            for j in range(16):
                ua, uaR = newtmp()
                for x in range(3):
                    wb, wbR = load_w(w_br[l, x, j], 8)
                    yps, ypsR = newps()
                    proj_fm(wb, wbR, 8, lambda kc, x=x: (Gblk[:, 8 * x + kc, 0:n], GblkR[8 * x + kc]), yps, ypsR, n)
                    wb2, wb2R = load_w(w_in[l, 80 + 16 * x + j], 16)
                    bps, bpsR = newps()
                    proj_fm(wb2, wb2R, 16, h_rhs(t0, n), bps, bpsR, n)
                    sg, sgR = newtmp()
                    act(sg[:, 0:n], bps[:, 0:n], AF.Sigmoid, [bpsR], [sgR])
                    if x == 0:
                        tt('dve', ua[:, 0:n], yps[:, 0:n], sg[:, 0:n], ALU.mult, [ypsR, sgR], [uaR])
                    else:
                        tt('dve', sg[:, 0:n], yps[:, 0:n], sg[:, 0:n], ALU.mult, [ypsR, sgR], [sgR])
                        if x == 1:
                            tt('pool', ua[:, 0:n], ua[:, 0:n], sg[:, 0:n], ALU.add, [uaR, sgR], [uaR])
                        else:
                            tt('pool', Ublk[:, j, 0:n], ua[:, 0:n], sg[:, 0:n], ALU.add, [uaR, sgR], [UblkR[j]])
            for j in range(16):
                wb, wbR = load_w(w_out[l, j], 16)
                ops_, opsR = newps()
                proj_fm(wb, wbR, 16, lambda kc: (Ublk[:, kc, 0:n], UblkR[kc]), ops_, opsR, n)
                xt, xtR = newtmp()
                dma(xt[:, 0:n], src[b, j, :, t0:t0 + n], [xsR[j]], [xtR])
                stt(xt[:, 0:n], ops_[:, 0:n], mod[:, l, col, 32 + j:33 + j], xt[:, 0:n], ALU.mult, ALU.add,
                    [opsR, xtR, modR], [xtR])
                store(XS[b, j, :, t0:t0 + n], xt[:, 0:n], [xtR], [xsR[j]])
            flush_stores()

    xsR = [Res() for _ in range(16)]
    gR = [Res() for _ in range(24)]
    outR = Res()
    for b in range(NB):
        for l in range(LAYERS):
            src = xT if l == 0 else XS
            last = (l == L - 1)
            norm_phase(b, l, src, False)
            if dbg and b == 0 and l == 0:
                for kc in range(16):
                    dma(dbg_h[kc], hT[:, kc, :], [hR[kc]], [outR])
            if stop_after == 'norm':
                break
            mixer_attn(l, 'a', not last)
            mixer_attn(l, 'b', not last)
            mixer_c(l, not last)
            bar()
            if dbg and b == 0 and l == 0:
                dma(dbg_g, G, gR, [outR])
            phase3(b, l, src, last)
            if last:
                pass
            bar()
            if dbg and b == 0 and l == 0:
                dma(dbg_x, XS[0], xsR, [outR])
        if LAYERS == L and stop_after is None:
            norm_phase(b, L - 1, XS, True)
            bar()
    if dbg:
        dma(dbg_mod, mod[:].rearrange('p a b c -> p (a b c)'), [modR], [outR])
    bar()
    S.emit(nc)
    build.sbuf_left = nc.sbuf_bytes_remaining
    return nc


def _retile(w, kc):
    K, N = w.shape
    return np.ascontiguousarray(w.reshape(kc, 128, N // 128, 128).transpose(2, 1, 0, 3))


def _vec128(v):
    n = v.shape[-1] // 128
    a = v.reshape(v.shape[:-1] + (n, 128))
    return np.ascontiguousarray(np.moveaxis(a, -1, 0))


def host_prep(inp, NB, cores, LW=L):
    f = np.float32
    shared = {}
    ada_w = np.asarray(inp['ada_w'][:LW], f)
    shared['ada_w'] = np.ascontiguousarray(
        ada_w.reshape(LW, 16, 128, 16, 384).transpose(0, 3, 2, 1, 4))
    shared['ada_b'] = np.ascontiguousarray(np.asarray(inp['ada_b'], f).reshape(L, 48, 128).transpose(2, 0, 1))
    shared['norm_g'] = np.ascontiguousarray(np.asarray(inp['norm_g'], f).reshape(L, 16, 128).transpose(2, 0, 1))
    shared['fnorm_g'] = np.ascontiguousarray(np.asarray(inp['final_norm_g'], f).reshape(16, 128).T)
    shared['w_in'] = np.stack([_retile(np.asarray(inp['w_in'][l], f), 16) for l in range(LW)])
    shared['w_br'] = np.stack([np.stack([_retile(np.asarray(inp[k][l], f), 8) for k in
                                         ('w_branch_a', 'w_branch_b', 'w_branch_c')]) for l in range(LW)])
    shared['w_out'] = np.stack([_retile(np.asarray(inp['w_out'][l], f), 16) for l in range(LW)])
    shared['sink'] = np.ascontiguousarray(np.broadcast_to(np.asarray(inp['a_sink'], f)[None], (128, L, 8)))
    qkn = np.stack([np.asarray(inp['b_q_norm'], f), np.asarray(inp['b_k_norm'], f),
                    np.asarray(inp['c_out_norm'], f)], axis=-1)
    shared['qkn'] = np.ascontiguousarray(qkn.transpose(1, 0, 2))
    clb = np.asarray(inp['c_lower_bound'], f).reshape(L, 2, 8, 128)
    shared['clb'] = np.ascontiguousarray(clb.transpose(3, 1, 2, 0))
    row = np.repeat(np.arange(NL // 64, dtype=f), 64)
    colv = np.tile(np.arange(64, dtype=f), NL // 64)
    inv_freq = (np.float32(10000.0) ** (-np.arange(32, dtype=f) / np.float32(32))).astype(f)
    ang = np.concatenate([row[:, None] * inv_freq, colv[:, None] * inv_freq], axis=-1).astype(f)
    ang = np.concatenate([ang, ang], axis=-1)
    cosT = np.cos(ang).astype(f).T
    sinT = np.sin(ang).astype(f).T.copy()
    sinT[64:] *= -1.0
    shared['rope'] = np.ascontiguousarray(np.stack([cosT, sinT], axis=1))
    kk = np.arange(128)[:, None]
    qq = np.arange(128)[None, :]
    same = (kk // 64) == (qq // 64)
    masks = np.stack([(same & (kk <= qq)), (same & (kk >= qq)), (kk >= qq), (kk <= qq), (kk == qq)], axis=1).astype(f)
    BIG = np.float32(1e30)
    ext = np.stack([BIG * masks[:, 0], -BIG * masks[:, 0], BIG * masks[:, 1], -BIG * masks[:, 1]], axis=1).astype(f)
    shared['masks'] = np.ascontiguousarray(np.concatenate([masks, ext], axis=1))
    m64 = np.ones((128, 512), f)
    m64[:, ::64] = 0.0
    shared['m64'] = m64
    maps = []
    for ci in range(cores):
        bs = list(range(ci * NB, (ci + 1) * NB))
        xs = []
        for b_ in bs:
            full = np.concatenate([np.asarray(inp['ctx'][b_], f), np.asarray(inp['x'][b_], f)], axis=0)
            xs.append(np.ascontiguousarray(full.T.reshape(16, 128, T)))
        m = dict(shared)
        m['xT'] = np.stack(xs)
        cc = np.stack([np.asarray(inp['c'][b_], f) for b_ in bs] + [np.asarray(inp['c_ctx'], f)], axis=-1)
        m['cT'] = np.ascontiguousarray(cc.reshape(16, 128, NB + 1).transpose(1, 0, 2))
        maps.append(m)
    return maps


def kernel(**inputs):
    B = inputs['x'].shape[0]
    cores = NCORES
    NB = B // cores
    nc = build(NB)
    maps = host_prep(inputs, NB, cores)
    res = run_bass_kernel_spmd(nc, maps, core_ids=list(range(cores)))
    out = np.empty((B, NL, D), np.float32)
    for ci in range(cores):
        o = res.results[ci]['outT']
        for i in range(NB):
            out[ci * NB + i] = o[i].reshape(D, NL).T
    return out
```
